# Optimizing a Trainium2 kernel written in Bass

```python
import numpy as np
import jax
import jax.numpy as jnp
from jax import lax

D_MODEL = 2048
BATCH = 8
SEQ = 2048
DEPTH = 2

HEAD_DIM = 128
ROPE_THETA = 10000.0
EPS = 1e-6
NEG = -1e30
BLOCK = 128

CONV_CH = D_MODEL // 2
CONV_WIDTH = 31
NSA_Q_HEADS = (D_MODEL // 2) // HEAD_DIM
NSA_KV_GROUPS = 2
NSA_CMP_LEN = 32
NSA_CMP_STRIDE = 16
NSA_CMP_HIDDEN = 256
NSA_SEL_LEN = 64
NSA_TOPK = 8
NSA_WINDOW = 256
NSA_SEL_QBLOCK = 64
NSA_FORCE = 1e6
SWA_Q_HEADS = (D_MODEL // 2) // HEAD_DIM
SWA_KV_HEADS = 2
SWA_WINDOW = 128
GMLP_CH = D_MODEL // 2
GMLP_GROUPS = GMLP_CH // HEAD_DIM
GMLP_CHUNK = 128
D_FF = 11 * D_MODEL // 4
FFN_CONV_WIDTH = 3

N_EVEN = (DEPTH + 1) // 2
N_ODD = DEPTH // 2
NSA_KV_W = NSA_KV_GROUPS * HEAD_DIM
SWA_KV_W = SWA_KV_HEADS * HEAD_DIM
EVEN_SPLITS = (2 * CONV_CH, NSA_Q_HEADS * HEAD_DIM) + (NSA_KV_W,) * 6 + (3 * NSA_Q_HEADS,)
ODD_SPLITS = (SWA_Q_HEADS * HEAD_DIM, SWA_KV_W, SWA_KV_W, 2 * GMLP_CH)
EVEN_IN = sum(EVEN_SPLITS)
ODD_IN = sum(ODD_SPLITS)
EVEN_MIX = CONV_CH + NSA_Q_HEADS * HEAD_DIM
ODD_MIX = SWA_Q_HEADS * HEAD_DIM + GMLP_CH

kernel_name = 'hybrid_conv_nsa_swa_gmlp_trunk'


def split_cols(z, sizes):
    return jnp.split(z, [int(s) for s in np.cumsum(sizes)[:-1]], axis=-1)


def rms_norm(x, g):
    xf = x.astype(jnp.float32)
    y = xf * lax.rsqrt(jnp.mean(xf * xf, axis=-1, keepdims=True) + EPS)
    return (y * g.astype(jnp.float32)).astype(x.dtype)


def layer_norm(x, g, b):
    xf = x.astype(jnp.float32)
    mu = jnp.mean(xf, axis=-1, keepdims=True)
    var = jnp.mean(jnp.square(xf - mu), axis=-1, keepdims=True)
    y = (xf - mu) * lax.rsqrt(var + EPS) * g.astype(jnp.float32) + b.astype(jnp.float32)
    return y.astype(x.dtype)


def rope_angles(pos):
    inv = ROPE_THETA ** (-jnp.arange(0, HEAD_DIM, 2, dtype=jnp.float32) / HEAD_DIM)
    ang = pos.astype(jnp.float32)[..., None] * inv
    return jnp.cos(ang), jnp.sin(ang)


def apply_rope(x, cos, sin):
    shp = (cos.shape[0],) + (1,) * (x.ndim - 3) + cos.shape[1:]
    cs, sn = cos.reshape(shp), sin.reshape(shp)
    x1, x2 = jnp.split(x.astype(jnp.float32), 2, axis=-1)
    return jnp.concatenate([x1 * cs - x2 * sn, x2 * cs + x1 * sn], axis=-1).astype(x.dtype)


def causal_dwconv(x, w, b):
    k = w.shape[0]
    y = lax.conv_general_dilated(x, w[:, None, :], window_strides=(1,), padding=[(k - 1, 0)],
                                 dimension_numbers=('NWC', 'WIO', 'NWC'),
                                 feature_group_count=x.shape[-1])
    return y + b


def banded_attention(q, k, v, window, sink=None):
    B, G, R, T, dh = q.shape
    nb, nw = T // BLOCK, window // BLOCK

    def band(a):
        ap = jnp.pad(a, ((0, 0), (0, 0), (nw * BLOCK, 0), (0, 0))).reshape(B, G, nb + nw, BLOCK, dh)
        return jnp.concatenate([ap[:, :, i:i + nb] for i in range(nw + 1)], axis=3)

    kb, vb = band(k), band(v)
    qb = q.reshape(B, G, R, nb, BLOCK, dh)
    s = jnp.einsum('bgrnqd,bgnkd->bgrnqk', qb, kb).astype(jnp.float32) * (dh ** -0.5)
    qi = np.arange(BLOCK)[:, None]
    ki = np.arange((nw + 1) * BLOCK)[None, :]
    rel = qi - ki + nw * BLOCK
    kpos = np.arange(nb)[:, None, None] * BLOCK + ki[None] - nw * BLOCK
    mask = (rel >= 0) & (rel < window) & (kpos >= 0)
    s = jnp.where(mask, s, NEG)
    if sink is None:
        p = jax.nn.softmax(s, axis=-1)
    else:
        sk = sink.astype(jnp.float32)[None, :, :, None, None, None]
        m = jnp.maximum(jnp.max(s, axis=-1, keepdims=True), sk)
        e = jnp.exp(s - m)
        p = e / (jnp.sum(e, axis=-1, keepdims=True) + jnp.exp(sk - m))
    o = jnp.einsum('bgrnqk,bgnkd->bgrnqd', p.astype(v.dtype), vb)
    return o.reshape(B, G, R, T, dh)


def conformer_conv(a, w_dw, b_dw, ln_g, ln_b):
    a1, a2 = jnp.split(a, 2, axis=-1)
    y = causal_dwconv(a1 * jax.nn.sigmoid(a2), w_dw, b_dw)
    return jax.nn.silu(layer_norm(y, ln_g, ln_b))


def nsa_attention(q, k_cmp, v_cmp, k_sel, v_sel, k_win, v_win, gate_logits, cos, sin, positions,
                  q_norm, k_norm, cmp_k_pos, cmp_k_w1, cmp_k_w2, cmp_v_pos, cmp_v_w1, cmp_v_w2):
    B, T, _ = q.shape
    G, R, dh = NSA_KV_GROUPS, NSA_Q_HEADS // NSA_KV_GROUPS, HEAD_DIM
    scale = dh ** -0.5
    q = q.reshape(B, T, G, R, dh).transpose(0, 2, 3, 1, 4)
    q = apply_rope(rms_norm(q, q_norm), cos, sin)

    def heads(a):
        return a.reshape(B, T, G, dh).transpose(0, 2, 1, 3)

    k_cmp, v_cmp, k_sel, v_sel, k_win, v_win = [heads(a) for a in (k_cmp, v_cmp, k_sel, v_sel, k_win, v_win)]
    t_idx = np.arange(T)

    nc = (T - NSA_CMP_LEN) // NSA_CMP_STRIDE + 1
    starts = np.arange(nc) * NSA_CMP_STRIDE
    ends = starts + NSA_CMP_LEN - 1
    bidx = starts[:, None] + np.arange(NSA_CMP_LEN)[None, :]

    def compress(a, pos_emb, w1, w2):
        ab = (a[:, :, bidx] + pos_emb).reshape(B, G, nc, NSA_CMP_LEN * dh)
        return jax.nn.silu(ab @ w1) @ w2

    kc = compress(k_cmp, cmp_k_pos, cmp_k_w1, cmp_k_w2)
    vc = compress(v_cmp, cmp_v_pos, cmp_v_w1, cmp_v_w2)
    cc, sc = rope_angles(positions[:, ends])
    kc = apply_rope(rms_norm(kc, k_norm[0]), cc, sc)
    s_c = jnp.einsum('bgrtd,bgnd->bgrtn', q, kc).astype(jnp.float32) * scale
    valid_c = ends[None, :] <= t_idx[:, None]
    p_c = jax.nn.softmax(jnp.where(valid_c, s_c, NEG), axis=-1) * valid_c
    o_cmp = jnp.einsum('bgrtn,bgnd->bgrtd', p_c.astype(vc.dtype), vc)

    ns = T // NSA_SEL_LEN
    sel_start = np.arange(ns) * NSA_SEL_LEN
    overlap = ((starts[:, None] <= sel_start[None, :] + NSA_SEL_LEN - 1) &
               (ends[:, None] >= sel_start[None, :])).astype(np.float32)
    imp = jnp.einsum('bgrtn,ns->bgts', p_c, jnp.asarray(overlap))
    cur = t_idx // NSA_SEL_LEN
    jb = np.arange(ns)[None, :]
    forced = (jb == 0) | (jb == cur[:, None]) | (jb == cur[:, None] - 1)
    valid_s = sel_start[None, :] <= t_idx[:, None]
    score = jnp.where(forced, NSA_FORCE, jnp.where(valid_s, imp, -1.0))
    n_top = min(NSA_TOPK, ns)
    _, sel = lax.top_k(score, n_top)

    k_sel = apply_rope(rms_norm(k_sel, k_norm[1]), cos, sin)
    ks_blk = k_sel.reshape(B, G, ns, NSA_SEL_LEN, dh)
    vs_blk = v_sel.reshape(B, G, ns, NSA_SEL_LEN, dh)
    qbs = NSA_SEL_QBLOCK
    nq = T // qbs
    q_ch = q.reshape(B, G, R, nq, qbs, dh).transpose(3, 0, 1, 2, 4, 5)
    sel_ch = sel.reshape(B, G, nq, qbs, n_top).transpose(2, 0, 1, 3, 4)
    t_ch = jnp.arange(T, dtype=jnp.int32).reshape(nq, qbs)
    bi = jnp.arange(B)[:, None, None, None]
    gi = jnp.arange(G)[None, :, None, None]

    def sel_block(args):
        qc, ic, tc = args
        kg = ks_blk[bi, gi, ic]
        vg = vs_blk[bi, gi, ic]
        s = jnp.einsum('bgrqd,bgqnld->bgrqnl', qc, kg).astype(jnp.float32) * scale
        kpos = ic[..., None] * NSA_SEL_LEN + jnp.arange(NSA_SEL_LEN, dtype=jnp.int32)
        m = kpos <= tc[:, None, None]
        s = jnp.where(m[:, :, None], s, NEG).reshape(B, G, R, qbs, n_top * NSA_SEL_LEN)
        p = jax.nn.softmax(s, axis=-1).reshape(B, G, R, qbs, n_top, NSA_SEL_LEN)
        return jnp.einsum('bgrqnl,bgqnld->bgrqd', p.astype(vg.dtype), vg)

    o_sel = lax.map(sel_block, (q_ch, sel_ch, t_ch))
    o_sel = o_sel.transpose(1, 2, 3, 0, 4, 5).reshape(B, G, R, T, dh)

    k_win = apply_rope(rms_norm(k_win, k_norm[2]), cos, sin)
    o_win = banded_attention(q, k_win, v_win, NSA_WINDOW)

    g = jax.nn.sigmoid(gate_logits.astype(jnp.float32)).reshape(B, T, G, R, 3)
    g = g.transpose(0, 2, 3, 1, 4).astype(q.dtype)
    o = g[..., 0:1] * o_cmp + g[..., 1:2] * o_sel + g[..., 2:3] * o_win
    return o.transpose(0, 3, 1, 2, 4).reshape(B, T, NSA_Q_HEADS * dh)


def chunked_gmlp(uv, ln_g, ln_b, w_s, b_s):
    B, T, _ = uv.shape
    u, v = jnp.split(jax.nn.gelu(uv), 2, axis=-1)
    v = layer_norm(v, ln_g, ln_b)
    nch = T // GMLP_CHUNK
    v = v.reshape(B, nch, GMLP_CHUNK, GMLP_GROUPS, HEAD_DIM)
    tril = np.tril(np.ones((GMLP_CHUNK, GMLP_CHUNK), dtype=bool))
    w = jnp.where(tril, w_s, jnp.zeros_like(w_s))
    sp = jnp.einsum('gts,bnsgc->bntgc', w, v) + b_s.T[:, :, None]
    return u * sp.reshape(B, T, GMLP_CH)


def even_mixer(h, cos, sin, positions, w_in, w_out, conv_w, conv_b, conv_ln_g, conv_ln_b, q_norm, k_norm,
               cmp_k_pos, cmp_k_w1, cmp_k_w2, cmp_v_pos, cmp_v_w1, cmp_v_w2):
    z = h @ w_in
    a, q, kc, vc, ks, vs, kw, vw, gl = split_cols(z, EVEN_SPLITS)
    y_a = conformer_conv(a, conv_w, conv_b, conv_ln_g, conv_ln_b)
    y_b = nsa_attention(q, kc, vc, ks, vs, kw, vw, gl, cos, sin, positions, q_norm, k_norm,
                        cmp_k_pos, cmp_k_w1, cmp_k_w2, cmp_v_pos, cmp_v_w1, cmp_v_w2)
    return jnp.concatenate([y_a, y_b], axis=-1) @ w_out


def odd_mixer(h, cos, sin, w_in, w_out, q_norm, k_norm, sinks, ln_g, ln_b, w_s, b_s):
    B, T, _ = h.shape
    G, R = SWA_KV_HEADS, SWA_Q_HEADS // SWA_KV_HEADS
    z = h @ w_in
    q, k, v, uv = split_cols(z, ODD_SPLITS)
    q = q.reshape(B, T, G, R, HEAD_DIM).transpose(0, 2, 3, 1, 4)
    k = k.reshape(B, T, G, HEAD_DIM).transpose(0, 2, 1, 3)
    v = v.reshape(B, T, G, HEAD_DIM).transpose(0, 2, 1, 3)
    q = apply_rope(rms_norm(q, q_norm), cos, sin)
    k = apply_rope(rms_norm(k, k_norm), cos, sin)
    o = banded_attention(q, k, v, SWA_WINDOW, sinks.reshape(G, R))
    y_c = o.transpose(0, 3, 1, 2, 4).reshape(B, T, SWA_Q_HEADS * HEAD_DIM)
    y_d = chunked_gmlp(uv, ln_g, ln_b, w_s, b_s)
    return jnp.concatenate([y_c, y_d], axis=-1) @ w_out


def conv_ffn(h, w_up, conv_w, conv_b, w_down):
    u = causal_dwconv(h @ w_up, conv_w, conv_b)
    a, b = jnp.split(u, 2, axis=-1)
    return (jax.nn.silu(a) * b) @ w_down


def setup_inputs(seed: int = 0) -> dict:
    key = jax.random.key(seed)
    ks = iter(jax.random.split(key, 48))

    def nrm(shape, scale):
        return scale * jax.random.normal(next(ks), shape, jnp.float32)

    def gain(shape):
        return 1.0 + nrm(shape, 0.02)

    D, dh = D_MODEL, HEAD_DIM
    L, H = NSA_CMP_LEN, NSA_CMP_HIDDEN
    inp = {}
    inp['x'] = nrm((BATCH, SEQ, D), 1.0)
    inp['c'] = nrm((BATCH, D), 1.0)
    off = jax.random.randint(next(ks), (BATCH, 1), 0, 4096, dtype=jnp.int32)
    inp['positions'] = (off + jnp.arange(SEQ, dtype=jnp.int32)[None, :]).astype(jnp.int32)
    inp['ada_w'] = nrm((DEPTH, D, 6 * D), 0.5 * D ** -0.5)
    inp['ada_b'] = nrm((DEPTH, 6 * D), 0.01)
    inp['norm_g'] = gain((DEPTH, 2, D))
    inp['ffn_w_up'] = nrm((DEPTH, D, 2 * D_FF), D ** -0.5)
    inp['ffn_conv_w'] = nrm((DEPTH, FFN_CONV_WIDTH, 2 * D_FF), FFN_CONV_WIDTH ** -0.5)
    inp['ffn_conv_b'] = nrm((DEPTH, 2 * D_FF), 0.01)
    inp['ffn_w_down'] = nrm((DEPTH, D_FF, D), D_FF ** -0.5)
    inp['ev_w_in'] = nrm((N_EVEN, D, EVEN_IN), D ** -0.5)
    inp['ev_w_out'] = nrm((N_EVEN, EVEN_MIX, D), EVEN_MIX ** -0.5)
    inp['ev_conv_w'] = nrm((N_EVEN, CONV_WIDTH, CONV_CH), CONV_WIDTH ** -0.5)
    inp['ev_conv_b'] = nrm((N_EVEN, CONV_CH), 0.01)
    inp['ev_conv_ln_g'] = gain((N_EVEN, CONV_CH))
    inp['ev_conv_ln_b'] = nrm((N_EVEN, CONV_CH), 0.01)
    inp['ev_q_norm'] = gain((N_EVEN, dh))
    inp['ev_k_norm'] = gain((N_EVEN, 3, dh))
    inp['ev_cmp_k_pos'] = nrm((N_EVEN, L, dh), 0.02)
    inp['ev_cmp_k_w1'] = nrm((N_EVEN, L * dh, H), (L * dh) ** -0.5)
    inp['ev_cmp_k_w2'] = nrm((N_EVEN, H, dh), H ** -0.5)
    inp['ev_cmp_v_pos'] = nrm((N_EVEN, L, dh), 0.02)
    inp['ev_cmp_v_w1'] = nrm((N_EVEN, L * dh, H), (L * dh) ** -0.5)
    inp['ev_cmp_v_w2'] = nrm((N_EVEN, H, dh), H ** -0.5)
    inp['od_w_in'] = nrm((N_ODD, D, ODD_IN), D ** -0.5)
    inp['od_w_out'] = nrm((N_ODD, ODD_MIX, D), ODD_MIX ** -0.5)
    inp['od_q_norm'] = gain((N_ODD, dh))
    inp['od_k_norm'] = gain((N_ODD, dh))
    inp['od_sinks'] = nrm((N_ODD, SWA_Q_HEADS), 0.5)
    inp['od_gmlp_ln_g'] = gain((N_ODD, GMLP_CH))
    inp['od_gmlp_ln_b'] = nrm((N_ODD, GMLP_CH), 0.01)
    inp['od_gmlp_w_s'] = nrm((N_ODD, GMLP_GROUPS, GMLP_CHUNK, GMLP_CHUNK), GMLP_CHUNK ** -0.5)
    inp['od_gmlp_b_s'] = gain((N_ODD, GMLP_GROUPS, GMLP_CHUNK))
    return inp


def reference(x, c, positions, ada_w, ada_b, norm_g, ffn_w_up, ffn_conv_w, ffn_conv_b, ffn_w_down,
              ev_w_in, ev_w_out, ev_conv_w, ev_conv_b, ev_conv_ln_g, ev_conv_ln_b, ev_q_norm, ev_k_norm,
              ev_cmp_k_pos, ev_cmp_k_w1, ev_cmp_k_w2, ev_cmp_v_pos, ev_cmp_v_w1, ev_cmp_v_w2,
              od_w_in, od_w_out, od_q_norm, od_k_norm, od_sinks, od_gmlp_ln_g, od_gmlp_ln_b,
              od_gmlp_w_s, od_gmlp_b_s):
    cos, sin = rope_angles(positions)
    c_act = jax.nn.silu(c)
    for i in range(DEPTH):
        mod = (c_act @ ada_w[i] + ada_b[i])[:, None, :]
        sh1, sc1, g1, sh2, sc2, g2 = jnp.split(mod, 6, axis=-1)
        h = rms_norm(x, norm_g[i, 0]) * (1.0 + sc1) + sh1
        if i % 2 == 0:
            j = i // 2
            y = even_mixer(h, cos, sin, positions, ev_w_in[j], ev_w_out[j], ev_conv_w[j], ev_conv_b[j],
                           ev_conv_ln_g[j], ev_conv_ln_b[j], ev_q_norm[j], ev_k_norm[j],
                           ev_cmp_k_pos[j], ev_cmp_k_w1[j], ev_cmp_k_w2[j],
                           ev_cmp_v_pos[j], ev_cmp_v_w1[j], ev_cmp_v_w2[j])
        else:
            j = i // 2
            y = odd_mixer(h, cos, sin, od_w_in[j], od_w_out[j], od_q_norm[j], od_k_norm[j], od_sinks[j],
                          od_gmlp_ln_g[j], od_gmlp_ln_b[j], od_gmlp_w_s[j], od_gmlp_b_s[j])
        x = x + g1 * y
        h = rms_norm(x, norm_g[i, 1]) * (1.0 + sc2) + sh2
        x = x + g2 * conv_ffn(h, ffn_w_up[i], ffn_conv_w[i], ffn_conv_b[i], ffn_w_down[i])
    return x
```

```python
import numpy as np
import ml_dtypes
from contextlib import ExitStack
import concourse.bass as bass
import concourse.mybir as mybir
from concourse.bass_utils import run_bass_kernel_spmd

F32 = mybir.dt.float32
BF16 = mybir.dt.bfloat16
I32 = mybir.dt.int32
AF = mybir.ActivationFunctionType
ALU = mybir.AluOpType
AX = mybir.AxisListType

D = 2048
T = 2048
NCH = 16
DFF = 5632
EVEN_IN = 4632
ODD_IN = 3584
EPS = 1e-6
ENGS = ("pe", "act", "dve", "pool", "sp")
NDMA = 90


class Slot:
    __slots__ = ("name", "w", "r", "is_dram")

    def __init__(self, name="", is_dram=False):
        self.name = name
        self.w = None
        self.r = {}
        self.is_dram = is_dram


class Prog:
    def __init__(self, nc):
        self.nc = nc
        self.streams = {e: [] for e in ENGS}
        self.cnt = {e: 0 for e in ENGS}
        for i in range(NDMA):
            self.cnt["dma%d" % i] = 0
        self.waited = {e: {} for e in ENGS}
        self.nins = 0
        self.slot_sem = {}
        self.free_dma = ["dma%d" % i for i in range(NDMA)]

    def dma_sem_for(self, slots):
        cand = [s for s in slots if not s.is_dram]
        assert len(cand) >= 1, "DMA needs an SBUF-side slot"
        s = cand[0]
        k = self.slot_sem.get(id(s))
        if k is None:
            assert self.free_dma, "out of DMA semaphores in this phase"
            k = self.free_dma.pop(0)
            self.slot_sem[id(s)] = k
        for o in cand[1:]:
            self.slot_sem.setdefault(id(o), k)
        return k

    def _need(self, eng, ev, waits):
        if ev is None:
            return
        k, v = ev
        if self.waited[eng].get(k, 0) >= v:
            return
        waits[k] = max(waits.get(k, 0), v)

    def op(self, eng, meth, *args, reads=(), writes=(), dma=None, **kwargs):
        waits = {}
        for s in reads:
            self._need(eng, s.w, waits)
        for s in writes:
            self._need(eng, s.w, waits)
            for ev in s.r.items():
                self._need(eng, ev, waits)
        if eng == "pe":
            waits.pop("pe", None)
        for k, v in waits.items():
            self.waited[eng][k] = max(self.waited[eng].get(k, 0), v)
        if dma is None:
            self.cnt[eng] += 1
            ev = (eng, self.cnt[eng])
            inc = (eng, 1)
        else:
            k = self.dma_sem_for(list(writes) + list(reads))
            self.cnt[k] += 16
            ev = (k, self.cnt[k])
            inc = (k, 16)
        self.streams[eng].append((list(waits.items()), (meth, args, kwargs), inc))
        self.nins += 1
        for s in reads:
            if s.r.get(ev[0], 0) < ev[1]:
                s.r[ev[0]] = ev[1]
        for s in writes:
            s.w = ev
            s.r = {}
        return ev

    def wait_all(self, eng, events):
        waits = {}
        for ev in events:
            self._need(eng, ev, waits)
        for k, v in waits.items():
            self.waited[eng][k] = max(self.waited[eng].get(k, 0), v)
        self.streams[eng].append((list(waits.items()), None, None))

    def barrier(self):
        evs = [(k, v) for k, v in self.cnt.items() if v > 0]
        for e in ENGS:
            self.wait_all(e, evs)
        self.slot_sem = {}
        self.free_dma = ["dma%d" % i for i in range(NDMA)]

    def emit(self, stack):
        nc = self.nc
        sems = {k: stack.enter_context(nc.semaphore("s_" + k)) for k in self.cnt}
        block = stack.enter_context(nc.Block())
        streams = self.streams

        def run(name):
            def body(eng):
                for waits, fn, inc in streams[name]:
                    for k, v in waits:
                        eng.wait_ge(sems[k], v)
                    if fn is not None:
                        ins = getattr(eng, fn[0])(*fn[1], **fn[2])
                        ins.then_inc(sems[inc[0]], inc[1])
            return body

        block.tensor(run("pe"))
        block.scalar(run("act"))
        block.vector(run("dve"))
        block.gpsimd(run("pool"))
        block.sync(run("sp"))


class Arena:
    def __init__(self, t, words):
        self.t = t
        self.words = words
        self.base = 0
        self.off = 0

    def reset(self):
        self.off = self.base

    def alloc(self, free_shape, dt, name=""):
        n = int(np.prod(free_shape))
        esz = 4 if dt in (F32, I32) else 2
        w = (n * esz + 3) // 4
        w = (w + 7) // 8 * 8
        assert self.off + w <= self.words, "arena overflow %s need %d have %d" % (name, w, self.words - self.off)
        v = self.t[:, self.off:self.off + w]
        self.off += w
        if esz == 2:
            v = v.bitcast(dt)[:, 0:n]
        else:
            v = v[:, 0:n]
            if dt != F32:
                v = v.bitcast(dt)
        if len(free_shape) == 2:
            v = v.rearrange("p (a b) -> p a b", a=free_shape[0])
        elif len(free_shape) == 3:
            v = v.rearrange("p (a b c) -> p a b c", a=free_shape[0], b=free_shape[1])
        return v, Slot(name)


DMA_W, DMA_LD, DMA_ST, DMA_MISC, DMA_LD2, DMA_W2 = 0, 1, 2, 3, 4, 5


class Builder:
    def __init__(self, nc, st, dbg=None):
        self.nc = nc
        self.st = st
        self.P = Prog(nc)
        self.dbg = dbg
        self.arena_t = st.enter_context(nc.sbuf_tensor("arena", [128, 196 * 256], F32))
        self.A = Arena(self.arena_t, 196 * 256)
        self.ps = []
        self.pss = []
        for i in range(8):
            self.ps.append(st.enter_context(nc.psum_tensor("ps%d" % i, [128, 512], F32)))
            self.pss.append(Slot("ps%d" % i))
        self.dslots = {}

    def op(self, *a, **k):
        return self.P.op(*a, **k)

    def dslot(self, key):
        if key not in self.dslots:
            self.dslots[key] = Slot(str(key), is_dram=True)
        return self.dslots[key]

    def phase(self):
        self.P.barrier()
        self.A.reset()

    def load_consts(self, cd):
        A = self.A
        self.identb, s1 = A.alloc([128], BF16, "identb")
        self.identf, s2 = A.alloc([128], F32, "identf")
        self.onesf, s3 = A.alloc([128], F32, "onesf")
        self.onesb, s4 = A.alloc([128], BF16, "onesb")
        self.epsc, s5 = A.alloc([8], F32, "epsc")
        self.cslot = Slot("consts")
        for dst, src in ((self.identb, cd["identb"]), (self.identf, cd["identf"]),
                         (self.onesf, cd["onesf"]), (self.onesb, cd["onesb"]), (self.epsc, cd["epsc"])):
            self.op("sp", "dma_start", out=dst, in_=src[:, :], writes=[self.cslot], dma=DMA_MISC)
        self.modcol, self.modcol_s = A.alloc([2 * 4, 16], F32, "modcol")
        A.base = A.off

    def phase_mod(self, layers, c_d, ada_w, ada_b, norm_g, gbc_d):
        self.phase()
        for _ in self.mod_gen(layers, c_d, ada_w, ada_b, norm_g, gbc_d, (0, 1, 2, 3)):
            pass

    def mod_gen(self, layers, c_d, ada_w, ada_b, norm_g, gbc_d, banks):
        A, op, ps, pss = self.A, self.op, self.ps, self.pss
        cs = self.cslot
        bR, bB, bC, bN = banks
        cT, cT_s = A.alloc([128], F32, "cT")
        cact, cact_s = A.alloc([16], BF16, "cact")
        ngT, ngT_s = A.alloc([128], F32, "ngT")
        wb = [A.alloc([16, 512], BF16, "adaw%d" % i) for i in range(2)]
        br = [A.alloc([512], F32, "brow%d" % i) for i in range(2)]
        mr = [A.alloc([512], F32, "mrow%d" % i) for i in range(2)]
        gst = [A.alloc([512], F32, "gst%d" % i) for i in range(2)]
        tmpc, tmpc_s = A.alloc([16], F32, "tmpc")
        tmpg, tmpg_s = A.alloc([16], F32, "tmpg")
        colst, colst_s = A.alloc([16], F32, "colst")
        op("sp", "dma_start", out=cT[0:16, :], in_=c_d.rearrange("o (k p) -> (o k) p", p=128), writes=[cT_s], dma=DMA_MISC)
        op("pe", "matmul", ps[bC][:, 0:16], cT[0:16, :], self.identf[0:16, 0:16], start=True, stop=True,
           reads=[cT_s, cs], writes=[pss[bC]])
        op("act", "activation", cact, ps[bC][:, 0:16], AF.Silu, reads=[pss[bC]], writes=[cact_s])
        steps = [(i, n) for i in layers for n in range(24)]

        def issue_loads(idx):
            i_, n_ = steps[idx]
            awv_ = ada_w[i_].rearrange("(k p) n -> p k n", p=128)
            w_, w_s_ = wb[idx % 2]
            b_, b_s_ = br[idx % 2]
            op("pool", "dma_start", out=w_, in_=awv_[:, :, n_ * 512:(n_ + 1) * 512], writes=[w_s_], dma=DMA_W)
            op("sp", "dma_start", out=b_[0:1, :], in_=ada_b[i_:i_ + 1, n_ * 512:(n_ + 1) * 512], writes=[b_s_], dma=DMA_MISC)

        issue_loads(0)
        it = 0
        for i in layers:
            for n in range(24):
                w, w_s = wb[it % 2]
                b, b_s = br[it % 2]
                m, m_s = mr[it % 2]
                g, g_s = gst[it % 2]
                it += 1
                if it < len(steps):
                    issue_loads(it)
                for k in range(16):
                    op("pe", "matmul", ps[bR][0:1, :], cact[:, k:k + 1], w[:, k, :], start=(k == 0), stop=(k == 15),
                       reads=[cact_s, w_s], writes=[pss[bR]])
                op("dve", "tensor_tensor", m[0:1, :], ps[bR][0:1, :], b[0:1, :], ALU.add, reads=[pss[bR], b_s], writes=[m_s])
                seg, q = n // 4, n % 4
                if seg in (2, 5):
                    gi = 0 if seg == 2 else 1
                    op("pe", "matmul", ps[bB][:, :], self.onesf[0:1, 0:128], m[0:1, :], start=True, stop=True,
                       reads=[m_s, cs], writes=[pss[bB]])
                    op("act", "copy", g, ps[bB][:, :], reads=[pss[bB]], writes=[g_s])
                    op("sp", "dma_start", out=gbc_d[i, gi, :, q * 512:(q + 1) * 512], in_=g, reads=[g_s],
                       writes=[self.dslot(("gbc", i, gi))], dma=DMA_ST)
                else:
                    si = {0: 0, 1: 1, 3: 2, 4: 3}[seg]
                    for jj in range(4):
                        cix = q * 4 + jj
                        op("pe", "matmul", ps[bC][:, 32 + cix:32 + cix + 1], m[0:1, jj * 128:(jj + 1) * 128], self.onesf[0:1, 0:1],
                           start=True, stop=True, reads=[m_s, cs], writes=[pss[bC]])
                    op("dve", "tensor_copy", colst[:, q * 4:(q + 1) * 4], ps[bC][:, 32 + q * 4:32 + q * 4 + 4], reads=[pss[bC]], writes=[colst_s])
                    if q == 3:
                        dst = self.modcol[:, i * 4 + si, :]
                        if si in (0, 2):
                            op("dve", "tensor_copy", dst, colst, reads=[colst_s], writes=[self.modcol_s])
                        else:
                            s_ = 0 if si == 1 else 1
                            op("sp", "dma_start", out=ngT[0:16, :], in_=norm_g[i, s_:s_ + 1, :].rearrange("o (k p) -> (o k) p", p=128),
                               writes=[ngT_s], dma=DMA_MISC)
                            op("dve", "tensor_scalar", tmpc, colst, 1.0, None, ALU.add, reads=[colst_s], writes=[tmpc_s])
                            op("pe", "matmul", ps[bN][:, 64:80], ngT[0:16, :], self.identf[0:16, 0:16], start=True, stop=True,
                               reads=[ngT_s, cs], writes=[pss[bN]])
                            op("dve", "tensor_copy", tmpg, ps[bN][:, 64:80], reads=[pss[bN]], writes=[tmpg_s])
                            op("dve", "tensor_tensor", dst, tmpc, tmpg, ALU.mult, reads=[tmpc_s, tmpg_s],
                               writes=[self.modcol_s])
                yield

    def make_hT(self, x_src, x_key, tok0, ntiles, hT, hT_s, Scol, Gcol, bufs):
        op, ps, pss = self.op, self.ps, self.pss
        cs = self.cslot
        for tt in range(ntiles):
            xt, xt_s = bufs["xt"][tt % 2]
            xn, xn_s = bufs["xn"][tt % 2]
            sq, sq_s = bufs["sq"]
            ss, ss_s = bufs["ss"][tt % 2]
            r0 = tok0 + tt * 128
            op("sp", "dma_start", out=xt, in_=x_src[r0:r0 + 128, :], reads=[self.dslot(x_key)], writes=[xt_s], dma=DMA_LD)
            op("dve", "memset", ss, 0.0, writes=[ss_s])
            op("act", "activation", sq, xt, AF.Square, accum_out=ss[:, 0:1], reads=[xt_s], writes=[sq_s, ss_s])
            op("act", "activation", ss[:, 1:2], ss[:, 0:1], AF.Sqrt, bias=self.epsc[:, 0:1], scale=1.0 / D,
               reads=[ss_s, cs], writes=[ss_s])
            op("dve", "reciprocal", ss[:, 2:3], ss[:, 1:2], reads=[ss_s], writes=[ss_s])
            op("dve", "tensor_scalar", xn, xt, ss[:, 2:3], None, ALU.mult, reads=[xt_s, ss_s], writes=[xn_s])
            for half in range(2):
                bk = 6 + half
                pv = ps[bk][:].bitcast(BF16)
                for j in range(8):
                    k = half * 8 + j
                    op("pe", "transpose", pv[:, j * 128:(j + 1) * 128], xn[:, k * 128:(k + 1) * 128], self.identb,
                       reads=[xn_s, cs], writes=[pss[bk]])
                for j in range(8):
                    k = half * 8 + j
                    hs_ = hT_s[tt // 4] if isinstance(hT_s, list) else hT_s
                    if j % 2 == 0:
                        op("act", "activation", hT[:, k, tt * 128:(tt + 1) * 128], pv[:, j * 128:(j + 1) * 128], AF.Identity,
                           scale=Gcol[:, k:k + 1], bias=Scol[:, k:k + 1], reads=[pss[bk], self.modcol_s], writes=[hs_])
                    else:
                        op("dve", "tensor_scalar", hT[:, k, tt * 128:(tt + 1) * 128], pv[:, j * 128:(j + 1) * 128],
                           Gcol[:, k:k + 1], Scol[:, k:k + 1], ALU.mult, ALU.add, reads=[pss[bk], self.modcol_s], writes=[hs_])

    def hT_bufs(self):
        A = self.A
        return {
            "xt": [A.alloc([2048], F32, "xt%d" % i) for i in range(2)],
            "xn": [A.alloc([2048], BF16, "xn%d" % i) for i in range(2)],
            "sq": A.alloc([2048], BF16, "sq"),
            "ss": [A.alloc([4], F32, "ss%d" % i) for i in range(2)],
        }

    def phase_inproj(self, layer, x_src, x_key, W, NZ, zT_d, side=None, side_first=0):
        self.phase()
        A, op, ps, pss = self.A, self.op, self.ps, self.pss
        hT, hT_s = A.alloc([16, T], BF16, "hT")
        bufs = self.hT_bufs()
        wb = [A.alloc([16, 512], BF16, "w_in%d" % i) for i in range(2)]
        zo = [A.alloc([T], F32, "zo%d" % i) for i in range(2)]
        gen = None
        if side is not None:
            gen = side()
            for _ in range(side_first):
                next(gen)
        Scol = self.modcol[:, layer * 4 + 0, :]
        Gcol = self.modcol[:, layer * 4 + 1, :]
        Wv = W.rearrange("(k p) n -> p k n", p=128)
        ngrp = (NZ + 511) // 512
        op("pool", "dma_start", out=wb[0][0][:, :, 0:min(512, NZ)], in_=Wv[:, :, 0:min(512, NZ)], writes=[wb[0][1]], dma=DMA_W)
        hT_ss = [Slot("hT_tq%d" % i) for i in range(4)]
        self.make_hT(x_src, x_key, 0, 16, hT, hT_ss, Scol, Gcol, bufs)
        ci = 0
        for n in range(ngrp):
            ncols = min(512, NZ - n * 512)
            w, w_s = wb[n % 2]
            if n + 1 < ngrp:
                nc1 = min(512, NZ - (n + 1) * 512)
                op("pool", "dma_start", out=wb[(n + 1) % 2][0][:, :, 0:nc1], in_=Wv[:, :, (n + 1) * 512:(n + 1) * 512 + nc1],
                   writes=[wb[(n + 1) % 2][1]], dma=DMA_W)
            for j in range((ncols + 127) // 128):
                m = min(128, ncols - j * 128)
                z, z_s = zo[ci % 2]
                if gen is not None:
                    try:
                        next(gen)
                    except StopIteration:
                        gen = None
                for tq in range(4):
                    bk = (ci % 2) * 2 + (tq % 2)
                    for k in range(16):
                        op("pe", "matmul", ps[bk][0:m, :], w[:, k, j * 128:j * 128 + m], hT[:, k, tq * 512:(tq + 1) * 512],
                           start=(k == 0), stop=(k == 15), reads=[w_s, hT_ss[tq]], writes=[pss[bk]])
                    if tq % 2 == 0:
                        op("act", "copy", z[0:m, tq * 512:(tq + 1) * 512], ps[bk][0:m, :], reads=[pss[bk]], writes=[z_s])
                    else:
                        op("dve", "tensor_copy", z[0:m, tq * 512:(tq + 1) * 512], ps[bk][0:m, :], reads=[pss[bk]], writes=[z_s])
                row0 = n * 512 + j * 128
                op("sp", "dma_start", out=zT_d[row0:row0 + m, :], in_=z[0:m, :], reads=[z_s],
                   writes=[self.dslot(("zT", row0 // 128))], dma=DMA_ST)
                ci += 1
        if gen is not None:
            for _ in gen:
                pass

    def phase_outproj(self, mixT_d, W, x_src, x_key, gb_src, gb_key, x_dst, dst_key):
        self.phase()
        A, op, ps, pss = self.A, self.op, self.ps, self.pss
        mixT, mixT_s = A.alloc([16, T], BF16, "mixT")
        Wt, Wt_s = A.alloc([16, D], BF16, "Wout")
        gb, gb_s = A.alloc([D], F32, "gb")
        xt = [A.alloc([D], F32, "xt%d" % i) for i in range(2)]
        xo = [A.alloc([D], F32, "xo%d" % i) for i in range(2)]
        Wv = W.rearrange("(k p) n -> p k n", p=128)
        Wt_ss = [Slot("Wt%d" % i) for i in range(8)]
        mx_ss = [Slot("mx%d" % i) for i in range(16)]
        for q in range(8):
            op("pool", "dma_start", out=Wt[:, q * 2:(q + 1) * 2, :], in_=Wv[:, q * 2:(q + 1) * 2, :], writes=[Wt_ss[q]], dma=DMA_W)
        for k in range(16):
            op("sp", "dma_start", out=mixT[:, k, :], in_=mixT_d[k * 128:(k + 1) * 128, :], reads=[self.dslot(("mixT", k))],
               writes=[mx_ss[k]], dma=DMA_LD)
        op("sp", "dma_start", out=gb, in_=gb_src, reads=[self.dslot(gb_key)], writes=[gb_s], dma=DMA_MISC)
        it = 0
        for tt in range(16):
            x, x_s = xt[tt % 2]
            o, o_s = xo[tt % 2]
            op("sp", "dma_start", out=x, in_=x_src[tt * 128:(tt + 1) * 128, :], reads=[self.dslot(x_key)], writes=[x_s], dma=DMA_LD)
            for cb in range(4):
                bk = it % 4
                it += 1
                for k in range(16):
                    op("pe", "matmul", ps[bk][:, :], mixT[:, k, tt * 128:(tt + 1) * 128], Wt[:, k, cb * 512:(cb + 1) * 512],
                       start=(k == 0), stop=(k == 15), reads=[mx_ss[k], Wt_ss[k // 2]], writes=[pss[bk]])
                cs_ = slice(cb * 512, (cb + 1) * 512)
                op("dve", "tensor_tensor", o[:, cs_], ps[bk][:, :], gb[:, cs_], ALU.mult, reads=[pss[bk], gb_s], writes=[o_s])
                op("pool", "tensor_tensor", o[:, cs_], o[:, cs_], x[:, cs_], ALU.add, reads=[o_s, x_s], writes=[o_s])
            op("sp", "dma_start", out=x_dst[tt * 128:(tt + 1) * 128, :], in_=o, reads=[o_s], writes=[self.dslot(dst_key)], dma=DMA_ST)

    def rows_to_cols(self, rows_aps, ncols, dst, dst_s, bank):
        A, op, ps, pss = self.A, self.op, self.ps, self.pss
        R = len(rows_aps)
        nch = ncols // 128
        step = 2048
        done = 0
        for c0 in range(0, ncols, step):
            cw_ = min(step, ncols - c0)
            rb, rb_s = A.alloc([step], F32, "r2c")
            for r, ap in enumerate(rows_aps):
                op("sp", "dma_start", out=rb[r:r + 1, 0:cw_], in_=ap[:, c0:c0 + cw_], writes=[rb_s], dma=DMA_MISC)
            for j in range(cw_ // 128):
                ch = c0 // 128 + j
                op("pe", "matmul", ps[bank][:, ch * R:(ch + 1) * R], rb[0:R, j * 128:(j + 1) * 128], self.identf[0:R, 0:R],
                   start=True, stop=True, reads=[rb_s, self.cslot], writes=[pss[bank]])
        op("dve", "tensor_copy", dst.rearrange("p a b -> p (a b)"), ps[bank][:, 0:nch * R], reads=[pss[bank]], writes=[dst_s])

    def phase_ffn(self, layer, x_src, x_key, x_dst, dst_key, w_up, conv_w, conv_b, w_down, gb_src, gb_key):
        self.phase()
        A, op, ps, pss = self.A, self.op, self.ps, self.pss
        NJ = DFF // 128
        TB = 1024
        NBLK = T // TB
        cw, cw_s = A.alloc([2 * NJ, 4], F32, "cw")
        rows = [conv_w[0:1, :], conv_w[1:2, :], conv_w[2:3, :], conv_b]
        mark = A.off
        self.rows_to_cols(rows, 2 * DFF, cw, cw_s, 0)
        self.P.barrier()
        A.off = mark
        halo, halo_s = A.alloc([2 * NJ, 2], F32, "halo")
        op("dve", "memset", halo, 0.0, writes=[halo_s])
        gT, gT_s = A.alloc([NJ, TB], BF16, "gT")
        mark0 = A.off
        Scol = self.modcol[:, layer * 4 + 2, :]
        Gcol = self.modcol[:, layer * 4 + 3, :]
        Wu = w_up.rearrange("(k p) n -> p k n", p=128)
        Wd = w_down.rearrange("(j p) n -> p j n", p=128)
        NG = NJ // 2
        NPC = NJ // 4
        NWD = 4 * NPC
        for blk in range(NBLK):
            self.P.barrier()
            A.off = mark0
            hTb, hTb_s = A.alloc([16, TB], BF16, "hTb")
            bufs = {"xt": [A.alloc([2048], F32, "xt0")] * 2, "xn": [A.alloc([2048], BF16, "xn0")] * 2,
                    "sq": A.alloc([2048], BF16, "sq"), "ss": [A.alloc([4], F32, "ss%d" % i) for i in range(2)]}
            wup = [A.alloc([16, 2, 256], BF16, "wup%d" % i) for i in range(2)]
            zA, zA_s = A.alloc([TB + 2], F32, "za")
            zB, zB_s = A.alloc([TB + 2], F32, "zb")
            aA, aA_s = A.alloc([TB], F32, "aa")
            aB, aB_s = A.alloc([TB], F32, "ab")
            sA, sA_s = A.alloc([TB], F32, "sa")

            def load_wup(g_):
                w, w_s = wup[g_ % 2]
                op("pool", "dma_start", out=w[:, :, 0, :], in_=Wu[:, :, g_ * 256:(g_ + 1) * 256], writes=[w_s], dma=DMA_W)
                op("pool", "dma_start", out=w[:, :, 1, :], in_=Wu[:, :, DFF + g_ * 256:DFF + (g_ + 1) * 256], writes=[w_s], dma=DMA_W)

            load_wup(0)
            hTb_ss = [Slot("hTb_tq%d" % i) for i in range(TB // 512)]
            self.make_hT(x_src, x_key, blk * TB, TB // 128, hTb, hTb_ss, Scol, Gcol, bufs)
            for j in range(NJ):
                g_ = j // 2
                if j % 2 == 0 and g_ + 1 < NG:
                    load_wup(g_ + 1)
                w, w_s = wup[g_ % 2]
                jo = (j % 2) * 128
                b0 = (j % 2) * 4
                for ab_ in range(2):
                    for tq in range(2):
                        bk = b0 + ab_ * 2 + tq
                        for k in range(16):
                            op("pe", "matmul", ps[bk][:, :], w[:, k, ab_, jo:jo + 128], hTb[:, k, tq * 512:(tq + 1) * 512],
                               start=(k == 0), stop=(k == 15), reads=[w_s, hTb_ss[tq]], writes=[pss[bk]])
                for (z, z_s, ab_, cj, acc, acc_s) in ((zA, zA_s, 0, j, aA, aA_s), (zB, zB_s, 1, NJ + j, aB, aB_s)):
                    op("dve", "tensor_copy", z[:, 0:2], halo[:, cj, :], reads=[halo_s], writes=[z_s])
                    for tq in range(2):
                        bk = b0 + ab_ * 2 + tq
                        op("act", "copy", z[:, 2 + tq * 512:2 + (tq + 1) * 512], ps[bk][:, :], reads=[pss[bk]], writes=[z_s])
                    op("dve", "tensor_copy", halo[:, cj, :], z[:, TB:TB + 2], reads=[z_s], writes=[halo_s])
                    op("dve", "tensor_scalar", acc, z[:, 2:TB + 2], cw[:, cj, 2:3], cw[:, cj, 3:4], ALU.mult, ALU.add,
                       reads=[z_s, cw_s], writes=[acc_s])
                    op("dve", "scalar_tensor_tensor", acc, z[:, 1:TB + 1], cw[:, cj, 1:2], acc, ALU.mult, ALU.add,
                       reads=[z_s, cw_s, acc_s], writes=[acc_s])
                    op("dve", "scalar_tensor_tensor", acc, z[:, 0:TB], cw[:, cj, 0:1], acc, ALU.mult, ALU.add,
                       reads=[z_s, cw_s, acc_s], writes=[acc_s])
                op("act", "activation", sA, aA, AF.Silu, reads=[aA_s], writes=[sA_s])
                op("dve", "tensor_tensor", gT[:, j, :], sA, aB, ALU.mult, reads=[sA_s, aB_s], writes=[gT_s])
            self.P.barrier()
            A.off = mark0
            gb, gb_s = A.alloc([D], F32, "gb")
            op("sp", "dma_start", out=gb, in_=gb_src, reads=[self.dslot(gb_key)], writes=[gb_s], dma=DMA_MISC)
            wd = [A.alloc([4, 512], BF16, "wd%d" % i) for i in range(4)]
            xt = [A.alloc([512], F32, "fx%d" % i) for i in range(3)]
            xo = [A.alloc([512], F32, "fo%d" % i) for i in range(3)]

            def load_wd(idx):
                d_, d_s = wd[idx % 4]
                cb_, pc_ = idx // NPC, idx % NPC
                op("pool", "dma_start", out=d_, in_=Wd[:, pc_ * 4:(pc_ + 1) * 4, cb_ * 512:(cb_ + 1) * 512], writes=[d_s], dma=DMA_W)

            for q_ in range(3):
                load_wd(q_)
            ei = 0
            NTT = TB // 128
            for cb in range(4):
                for pc in range(NPC):
                    idx = cb * NPC + pc
                    d_, d_s = wd[idx % 4]
                    if idx + 3 < NWD:
                        load_wd(idx + 3)
                    for tt in range(NTT):
                        for jj in range(4):
                            j = pc * 4 + jj
                            op("pe", "matmul", ps[tt][:, :], gT[:, j, tt * 128:(tt + 1) * 128], d_[:, jj, :],
                               start=(j == 0), stop=(j == NJ - 1), reads=[gT_s, d_s], writes=[pss[tt]])
                cs_ = slice(cb * 512, (cb + 1) * 512)
                for tt in range(NTT):
                    x, x_s = xt[ei % 3]
                    o, o_s = xo[ei % 3]
                    ei += 1
                    r0 = blk * TB + tt * 128
                    op("sp", "dma_start", out=x, in_=x_src[r0:r0 + 128, cs_], reads=[self.dslot(x_key)], writes=[x_s], dma=DMA_LD2)
                    op("dve", "tensor_tensor", o, ps[tt][:, :], gb[:, cs_], ALU.mult, reads=[pss[tt], gb_s], writes=[o_s])
                    op("dve", "tensor_tensor", o, o, x, ALU.add, reads=[o_s, x_s], writes=[o_s])
                    op("sp", "dma_start", out=x_dst[r0:r0 + 128, cs_], in_=o, reads=[o_s], writes=[self.dslot(dst_key)], dma=DMA_ST)

    def sin_of(self, dst, ang, ang_s, dst_s, tmp, tmp_s, shift, tmpi, tmpi_s, tmpf, tmpf_s):
        op = self.op
        op("dve", "tensor_scalar", tmp, ang, float(1.0 / (2 * np.pi)), float(shift / (2 * np.pi)), ALU.mult, ALU.add,
           reads=[ang_s], writes=[tmp_s])
        op("dve", "tensor_copy", tmpi, tmp, reads=[tmp_s], writes=[tmpi_s])
        op("dve", "tensor_copy", tmpf, tmpi, reads=[tmpi_s], writes=[tmpf_s])
        op("dve", "tensor_tensor", tmp, tmp, tmpf, ALU.subtract, reads=[tmp_s, tmpf_s], writes=[tmp_s])
        op("act", "activation", dst, tmp, AF.Sin, scale=6.28318, reads=[tmp_s], writes=[dst_s])

    def phase_rope(self, pos_d, cs_d, C):
        self.phase()
        A, op = self.A, self.op
        posb, posb_s = A.alloc([T], I32, "posb")
        ang, ang_s = A.alloc([T], F32, "ang")
        tmp, tmp_s = A.alloc([T], F32, "tmp")
        res = [A.alloc([T], F32, "res%d" % i) for i in range(2)]
        tmpi, tmpi_s = A.alloc([T], I32, "tmpi")
        tmpf, tmpf_s = A.alloc([T], F32, "tmpf")
        invc, invc_s = A.alloc([1], F32, "invc")
        op("sp", "dma_start", out=invc, in_=C["invc"][:, :], writes=[invc_s], dma=DMA_MISC)
        op("sp", "dma_start", out=posb, in_=pos_d[0:1, :].partition_broadcast(128), writes=[posb_s], dma=DMA_MISC)
        op("dve", "tensor_copy", ang, posb, reads=[posb_s], writes=[ang_s])
        op("dve", "tensor_scalar", ang, ang, invc[:, 0:1], None, ALU.mult, reads=[ang_s, invc_s], writes=[ang_s])
        for i, shift in enumerate((np.pi / 2, 0.0)):
            r, r_s = res[i]
            self.sin_of(r, ang, ang_s, r_s, tmp, tmp_s, shift, tmpi, tmpi_s, tmpf, tmpf_s)
            op("sp", "dma_start", out=cs_d[i, :, :], in_=r, reads=[r_s], writes=[self.dslot(("cs", i))], dma=DMA_ST)

    def phase_qk(self, zT_d, cs_d, jobs, C):
        self.phase()
        A, op, ps, pss = self.A, self.op, self.ps, self.pss
        cs = self.cslot
        cosT, cos_s = A.alloc([T], F32, "cosT")
        sinT, sin_s = A.alloc([T], F32, "sinT")
        rotP, rotP_s = A.alloc([128], F32, "rotP")
        op("sp", "dma_start", out=cosT, in_=cs_d[0, :, :], reads=[self.dslot(("cs", 0))], writes=[cos_s], dma=DMA_MISC)
        op("sp", "dma_start", out=sinT, in_=cs_d[1, :, :], reads=[self.dslot(("cs", 1))], writes=[sin_s], dma=DMA_MISC)
        op("sp", "dma_start", out=rotP, in_=C["rotPT"][:, :], writes=[rotP_s], dma=DMA_MISC)
        xb = [A.alloc([T], F32, "qx%d" % i) for i in range(2)]
        sq = [A.alloc([T], F32, "qsq%d" % i) for i in range(2)]
        rs = [A.alloc([T], F32, "qrs%d" % i) for i in range(2)]
        xn = [A.alloc([T], F32, "qxn%d" % i) for i in range(2)]
        t1 = [A.alloc([T], F32, "qt1%d" % i) for i in range(2)]
        t2 = [A.alloc([T], F32, "qt2%d" % i) for i in range(2)]
        ob = [A.alloc([T], BF16, "qo%d" % i) for i in range(2)]
        gc = [A.alloc([2], F32, "qg%d" % i) for i in range(2)]

        def stage_a(ji):
            zc, gain_ap, scale, dst, dst_key = jobs[ji]
            x, x_s = xb[ji % 2]
            s2, s2_s = sq[ji % 2]
            g, g_s = gc[ji % 2]
            r_, r_s = rs[ji % 2]
            n_, n_s = xn[ji % 2]
            b_, b_s = t2[ji % 2]
            op("sp", "dma_start", out=x, in_=zT_d[zc * 128:(zc + 1) * 128, :], reads=[self.dslot(("zT", zc))], writes=[x_s], dma=DMA_LD)
            op("sp", "dma_start", out=g[:, 0:1], in_=gain_ap.rearrange("o d -> d o"), writes=[g_s], dma=DMA_MISC)
            op("dve", "tensor_scalar", g[:, 1:2], g[:, 0:1], float(scale), None, ALU.mult, reads=[g_s], writes=[g_s])
            op("act", "activation", s2, x, AF.Square, reads=[x_s], writes=[s2_s])
            for tq in range(4):
                sl = slice(tq * 512, (tq + 1) * 512)
                op("pe", "matmul", ps[tq][:, :], self.onesf, s2[:, sl], start=True, stop=True, reads=[cs, s2_s], writes=[pss[tq]])
            for tq in range(4):
                sl = slice(tq * 512, (tq + 1) * 512)
                op("act", "activation", r_[:, sl], ps[tq][:, :], AF.Sqrt, bias=self.epsc[:, 0:1], scale=1.0 / 128, reads=[pss[tq], cs], writes=[r_s])
            op("dve", "reciprocal", r_, r_, reads=[r_s], writes=[r_s])
            op("dve", "scalar_tensor_tensor", n_, x, g[:, 1:2], r_, ALU.mult, ALU.mult, reads=[x_s, g_s, r_s], writes=[n_s])
            for tq in range(4):
                sl = slice(tq * 512, (tq + 1) * 512)
                op("pe", "matmul", ps[4 + tq][:, :], rotP, n_[:, sl], start=True, stop=True, reads=[rotP_s, n_s], writes=[pss[4 + tq]])
            for tq in range(4):
                sl = slice(tq * 512, (tq + 1) * 512)
                op("dve", "tensor_tensor", b_[:, sl], ps[4 + tq][:, :], sinT[:, sl], ALU.mult, reads=[pss[4 + tq], sin_s], writes=[b_s])

        def stage_b(ji):
            zc, gain_ap, scale, dst, dst_key = jobs[ji]
            n_, n_s = xn[ji % 2]
            a_, a_s = t1[ji % 2]
            b_, b_s = t2[ji % 2]
            o, o_s = ob[ji % 2]
            op("pool", "tensor_tensor", a_, n_, cosT, ALU.mult, reads=[n_s, cos_s], writes=[a_s])
            op("pool", "tensor_tensor", o, a_, b_, ALU.add, reads=[a_s, b_s], writes=[o_s])
            op("sp", "dma_start", out=dst, in_=o, reads=[o_s], writes=[self.dslot(dst_key)], dma=DMA_ST)

        nj = len(jobs)
        stage_a(0)
        for ji in range(nj):
            if ji + 1 < nj:
                stage_a(ji + 1)
            stage_b(ji)

    def phase_vprep(self, zT_d, jobs):
        self.phase()
        A, op, ps, pss = self.A, self.op, self.ps, self.pss
        xb = [A.alloc([T], F32, "vx%d" % i) for i in range(2)]
        xh = [A.alloc([T], BF16, "vh%d" % i) for i in range(2)]
        vo = [A.alloc([T], BF16, "vo%d" % i) for i in range(2)]
        for ji, (zc, dst, dst_key) in enumerate(jobs):
            x, x_s = xb[ji % 2]
            h, h_s = xh[ji % 2]
            o, o_s = vo[ji % 2]
            op("sp", "dma_start", out=x, in_=zT_d[zc * 128:(zc + 1) * 128, :], reads=[self.dslot(("zT", zc))], writes=[x_s], dma=DMA_LD)
            op("act", "copy", h, x, reads=[x_s], writes=[h_s])
            for half in range(2):
                bk = (ji * 2 + half) % 4
                pv = ps[bk][:].bitcast(BF16)
                for j in range(8):
                    kt = half * 8 + j
                    op("pe", "transpose", pv[:, j * 128:(j + 1) * 128], h[:, kt * 128:(kt + 1) * 128], self.identb,
                       reads=[h_s, self.cslot], writes=[pss[bk]])
                op("dve", "tensor_copy", o[:, half * 1024:(half + 1) * 1024], pv[:, 0:1024], reads=[pss[bk]], writes=[o_s])
            op("sp", "dma_start", out=dst, in_=o, reads=[o_s], writes=[self.dslot(dst_key)], dma=DMA_ST)

    def ln_stats_acc(self, y, y_s, sqb, sqb_s, c, nchunks):
        op, ps, pss = self.op, self.ps, self.pss
        op("act", "activation", sqb, y, AF.Square, reads=[y_s], writes=[sqb_s])
        for tq in range(4):
            sl = slice(tq * 512, (tq + 1) * 512)
            op("pe", "matmul", ps[tq][:, :], self.onesf, y[:, sl], start=(c == 0), stop=(c == nchunks - 1),
               reads=[self.cslot, y_s], writes=[pss[tq]])
            op("pe", "matmul", ps[4 + tq][:, :], self.onesf, sqb[:, sl], start=(c == 0), stop=(c == nchunks - 1),
               reads=[self.cslot, sqb_s], writes=[pss[4 + tq]])

    def ln_finish(self, mean, mean_s, rstd, rstd_s, nfeat, src=None):
        op, ps, pss = self.op, self.ps, self.pss
        for tq in range(4):
            sl = slice(tq * 512, (tq + 1) * 512)
            if src is None:
                a1_, a1s, a2_, a2s = ps[tq][:, :], pss[tq], ps[4 + tq][:, :], pss[4 + tq]
            else:
                a1_, a1s, a2_, a2s = src[0][:, sl], src[1], src[2][:, sl], src[3]
            op("act", "activation", mean[:, sl], a1_, AF.Copy, scale=1.0 / nfeat, reads=[a1s], writes=[mean_s])
            op("dve", "tensor_tensor", rstd[:, sl], mean[:, sl], mean[:, sl], ALU.mult, reads=[mean_s], writes=[rstd_s])
            op("dve", "scalar_tensor_tensor", rstd[:, sl], a2_, 1.0 / nfeat, rstd[:, sl], ALU.mult, ALU.subtract,
               reads=[a2s, rstd_s], writes=[rstd_s])
            op("act", "activation", rstd[:, sl], rstd[:, sl], AF.Sqrt, bias=self.epsc[:, 0:1], scale=1.0, reads=[rstd_s, self.cslot], writes=[rstd_s])
            op("dve", "reciprocal", rstd[:, sl], rstd[:, sl], reads=[rstd_s], writes=[rstd_s])

    def phase_conformer(self, zT_d, mixT_d, conv_w, conv_b, ln_g, ln_b):
        self.phase()
        A, op, ps, pss = self.A, self.op, self.ps, self.pss
        NC = 8
        cw, cw_s = A.alloc([NC, 34], F32, "ccw")
        rows = [conv_w[k:k + 1, :] for k in range(31)] + [conv_b, ln_g, ln_b]
        mark = A.off
        self.rows_to_cols(rows, 1024, cw, cw_s, 0)
        self.P.barrier()
        A.off = mark
        ys = [A.alloc([T], F32, "cy%d" % c) for c in range(NC)]
        a1 = [A.alloc([T], F32, "ca1%d" % i) for i in range(2)]
        a2 = [A.alloc([T], F32, "ca2%d" % i) for i in range(2)]
        gp = [A.alloc([30 + T], BF16, "cgp%d" % i) for i in range(2)]
        dg = [A.alloc([31, 128], BF16, "cdg%d" % i) for i in range(2)]
        sqb, sqb_s = A.alloc([T], F32, "csq")
        s1a, s1a_s = A.alloc([T], F32, "cs1")
        s2a, s2a_s = A.alloc([T], F32, "cs2")
        cs = self.cslot
        for i in range(2):
            op("dve", "memset", gp[i][0][:, 0:30], 0.0, writes=[gp[i][1]])
        def stage_a(c):
            x1, x1_s = a1[c % 2]
            x2, x2_s = a2[c % 2]
            g, g_s = gp[c % 2]
            d_, d_s = dg[c % 2]
            op("sp", "dma_start", out=x1, in_=zT_d[c * 128:(c + 1) * 128, :], reads=[self.dslot(("zT", c))], writes=[x1_s], dma=DMA_LD)
            op("sp", "dma_start", out=x2, in_=zT_d[(8 + c) * 128:(9 + c) * 128, :], reads=[self.dslot(("zT", 8 + c))], writes=[x2_s], dma=DMA_LD)
            op("act", "activation", x2, x2, AF.Sigmoid, reads=[x2_s], writes=[x2_s])
            op("pool", "tensor_tensor", g[:, 30:30 + T], x1, x2, ALU.mult, reads=[x1_s, x2_s], writes=[g_s])
            for k in range(31):
                op("dve", "tensor_scalar", d_[:, k, :], self.identb, cw[:, c, k:k + 1], None, ALU.mult, reads=[cs, cw_s], writes=[d_s])

        def stage_b(c):
            g, g_s = gp[c % 2]
            d_, d_s = dg[c % 2]
            y, y_s = ys[c]
            for tq in range(4):
                sl = slice(tq * 512, (tq + 1) * 512)
                for k in range(31):
                    op("pe", "matmul", ps[tq][:, :], d_[:, k, :], g[:, k + tq * 512:k + tq * 512 + 512], start=(k == 0), stop=(k == 30),
                       reads=[d_s, g_s], writes=[pss[tq]])
                op("act", "activation", y[:, sl], ps[tq][:, :], AF.Identity, bias=cw[:, c, 31:32], scale=1.0, reads=[pss[tq], cw_s], writes=[y_s])
            op("act", "activation", sqb, y, AF.Square, reads=[y_s], writes=[sqb_s])
            for tq in range(4):
                sl = slice(tq * 512, (tq + 1) * 512)
                b1, b2 = 4 + tq % 2, 6 + tq % 2
                op("pe", "matmul", ps[b1][:, :], self.onesf, y[:, sl], start=True, stop=True, reads=[cs, y_s], writes=[pss[b1]])
                op("pe", "matmul", ps[b2][:, :], self.onesf, sqb[:, sl], start=True, stop=True, reads=[cs, sqb_s], writes=[pss[b2]])
                if c == 0:
                    op("dve", "tensor_copy", s1a[:, sl], ps[b1][:, :], reads=[pss[b1]], writes=[s1a_s])
                    op("dve", "tensor_copy", s2a[:, sl], ps[b2][:, :], reads=[pss[b2]], writes=[s2a_s])
                else:
                    op("dve", "tensor_tensor", s1a[:, sl], s1a[:, sl], ps[b1][:, :], ALU.add, reads=[s1a_s, pss[b1]], writes=[s1a_s])
                    op("dve", "tensor_tensor", s2a[:, sl], s2a[:, sl], ps[b2][:, :], ALU.add, reads=[s2a_s, pss[b2]], writes=[s2a_s])

        stage_a(0)
        for c in range(NC):
            if c + 1 < NC:
                stage_a(c + 1)
            stage_b(c)
        mean, mean_s = a1[0]
        rstd, rstd_s = a1[1]
        self.ln_finish(mean, mean_s, rstd, rstd_s, 1024, src=(s1a, s1a_s, s2a, s2a_s))
        ob = [A.alloc([T], BF16, "cob%d" % i) for i in range(2)]
        for c in range(NC):
            y, y_s = ys[c]
            o, o_s = ob[c % 2]
            op("dve", "tensor_tensor", y, y, mean, ALU.subtract, reads=[y_s, mean_s], writes=[y_s])
            op("pool", "tensor_tensor", y, y, rstd, ALU.mult, reads=[y_s, rstd_s], writes=[y_s])
            op("act", "activation", o, y, AF.Silu, bias=cw[:, c, 33:34], scale=cw[:, c, 32:33], reads=[y_s, cw_s], writes=[o_s])
            op("sp", "dma_start", out=mixT_d[c * 128:(c + 1) * 128, :], in_=o, reads=[o_s], writes=[self.dslot(("mixT", c))], dma=DMA_ST)

    def phase_compress(self, zT_d, pos_d, C, prm, kcT_d, vc_d):
        self.phase()
        A, op, ps, pss = self.A, self.op, self.ps, self.pss
        cs = self.cslot
        w1s, w1_s = A.alloc([32, 256], BF16, "w1s")
        w2s, w2_s = A.alloc([2, 128], BF16, "w2s")
        pe_, pe_s = A.alloc([128], F32, "pe")
        posT, posT_s = A.alloc([32], BF16, "posT")
        brow, brow_s = A.alloc([256], BF16, "brow")
        af, af_s = A.alloc([T], F32, "caf")
        ab, ab_s = A.alloc([T + 16], BF16, "cab")
        hid, hid_s = A.alloc([256], BF16, "hid")
        hidT, hidT_s = A.alloc([2, 128], BF16, "hidT")
        kc, kc_s = A.alloc([128], F32, "kc")
        sqj, sqj_s = A.alloc([128], F32, "sqj")
        st_, st_s = A.alloc([4], F32, "kst")
        gainb, gainb_s = A.alloc([128], F32, "gainb")
        invr, invr_s = A.alloc([64], F32, "invr")
        posE, posE_s = A.alloc([16], I32, "posE")
        posF, posF_s = A.alloc([1], F32, "posF")
        angE, angE_s = A.alloc([64], F32, "angE")
        tmpE, tmpE_s = A.alloc([64], F32, "tmpE")
        tmpEi, tmpEi_s = A.alloc([64], I32, "tmpEi")
        tmpEf, tmpEf_s = A.alloc([64], F32, "tmpEf")
        cosE, cosE_s = A.alloc([64], F32, "cosE")
        sinE, sinE_s = A.alloc([64], F32, "sinE")
        kn, kn_s = A.alloc([128], F32, "kn")
        tt_ = [A.alloc([64], F32, "ktt%d" % i) for i in range(4)]
        kr, kr_s = A.alloc([128], BF16, "kr")
        ob, ob_s = A.alloc([128], BF16, "cob")
        op("dve", "memset", hid, 0.0, writes=[hid_s])
        op("dve", "memset", ab[:, T:T + 16], 0.0, writes=[ab_s])
        op("dve", "memset", posE, 0, writes=[posE_s])
        op("sp", "dma_start", out=posE[0:127, :], in_=pos_d[0:1, 16:16 + 16 * 127].rearrange("o (n s) -> (o n) s", s=16),
           writes=[posE_s], dma=DMA_MISC)
        op("sp", "dma_start", out=invr, in_=C["invrow"][:, :], writes=[invr_s], dma=DMA_MISC)
        op("sp", "dma_start", out=gainb, in_=prm["k_gain"].partition_broadcast(128), writes=[gainb_s], dma=DMA_MISC)
        op("dve", "tensor_copy", posF, posE[:, 15:16], reads=[posE_s], writes=[posF_s])
        op("dve", "tensor_scalar", angE, invr, posF[:, 0:1], None, ALU.mult, reads=[invr_s, posF_s], writes=[angE_s])
        self.sin_of(cosE, angE, angE_s, cosE_s, tmpE, tmpE_s, np.pi / 2, tmpEi, tmpEi_s, tmpEf, tmpEf_s)
        self.sin_of(sinE, angE, angE_s, sinE_s, tmpE, tmpE_s, 0.0, tmpEi, tmpEi_s, tmpEf, tmpEf_s)
        for kind in ("k", "v"):
            w1 = prm[kind + "_w1"]
            w2 = prm[kind + "_w2"]
            pos = prm[kind + "_pos"]
            op("pool", "dma_start", out=w1s, in_=w1.rearrange("(l p) h -> p l h", p=128), writes=[w1_s], dma=DMA_W)
            op("pool", "dma_start", out=w2s, in_=w2.rearrange("(c p) d -> p c d", p=128), writes=[w2_s], dma=DMA_W)
            op("sp", "dma_start", out=pe_[0:32, :], in_=pos, writes=[pe_s], dma=DMA_MISC)
            op("pe", "matmul", ps[0][:, 0:32], pe_[0:32, :], self.identf[0:32, 0:32], start=True, stop=True, reads=[pe_s, cs], writes=[pss[0]])
            op("dve", "tensor_copy", posT, ps[0][:, 0:32], reads=[pss[0]], writes=[posT_s])
            for l in range(32):
                op("pe", "matmul", ps[1][0:1, 0:256], posT[:, l:l + 1], w1s[:, l, :], start=(l == 0), stop=(l == 31),
                   reads=[posT_s, w1_s], writes=[pss[1]])
            op("dve", "tensor_copy", brow[0:1, :], ps[1][0:1, 0:256], reads=[pss[1]], writes=[brow_s])
            for g in range(2):
                zc = (24 if kind == "k" else 26) + g
                op("sp", "dma_start", out=af, in_=zT_d[zc * 128:(zc + 1) * 128, :], reads=[self.dslot(("zT", zc))], writes=[af_s], dma=DMA_LD)
                op("act", "copy", ab[:, 0:T], af, reads=[af_s], writes=[ab_s])
                for l in range(32):
                    lv = ab[:, l:l + 16 * 127].rearrange("p (n s) -> p n s", s=16)[:, :, 0]
                    op("pe", "matmul", ps[2][0:127, 0:256], lv, w1s[:, l, :], start=(l == 0), stop=False,
                       reads=[ab_s, w1_s], writes=[pss[2]])
                op("pe", "matmul", ps[2][0:127, 0:256], self.onesb[0:1, 0:127], brow[0:1, :], start=False, stop=True,
                   reads=[cs, brow_s], writes=[pss[2]])
                op("act", "activation", hid[0:127, :], ps[2][0:127, 0:256], AF.Silu, reads=[pss[2]], writes=[hid_s])
                pv = ps[3][:].bitcast(BF16)
                for c in range(2):
                    op("pe", "transpose", pv[:, c * 128:(c + 1) * 128], hid[:, c * 128:(c + 1) * 128], self.identb,
                       reads=[hid_s, cs], writes=[pss[3]])
                op("dve", "tensor_copy", hidT.rearrange("p a b -> p (a b)"), pv[:, 0:256], reads=[pss[3]], writes=[hidT_s])
                for c in range(2):
                    op("pe", "matmul", ps[4][:, 0:128], hidT[:, c, :], w2s[:, c, :], start=(c == 0), stop=(c == 1),
                       reads=[hidT_s, w2_s], writes=[pss[4]])
                if kind == "v":
                    op("act", "copy", ob, ps[4][:, 0:128], reads=[pss[4]], writes=[ob_s])
                    op("sp", "dma_start", out=vc_d[g, :, :], in_=ob, reads=[ob_s], writes=[self.dslot(("vc", g))], dma=DMA_ST)
                    continue
                op("dve", "tensor_copy", kc, ps[4][:, 0:128], reads=[pss[4]], writes=[kc_s])
                op("dve", "memset", st_, 0.0, writes=[st_s])
                op("act", "activation", sqj, kc, AF.Square, accum_out=st_[:, 0:1], reads=[kc_s], writes=[sqj_s, st_s])
                op("act", "activation", st_[:, 1:2], st_[:, 0:1], AF.Sqrt, bias=self.epsc[:, 0:1], scale=1.0 / 128, reads=[st_s, cs], writes=[st_s])
                op("dve", "reciprocal", st_[:, 2:3], st_[:, 1:2], reads=[st_s], writes=[st_s])
                op("dve", "scalar_tensor_tensor", kn, kc, st_[:, 2:3], gainb, ALU.mult, ALU.mult, reads=[kc_s, st_s, gainb_s], writes=[kn_s])
                (ta, ta_s), (tb, tb_s), (tc_, tc_s), (td, td_s) = tt_
                op("dve", "tensor_tensor", ta, kn[:, 0:64], cosE, ALU.mult, reads=[kn_s, cosE_s], writes=[ta_s])
                op("dve", "tensor_tensor", tb, kn[:, 64:128], sinE, ALU.mult, reads=[kn_s, sinE_s], writes=[tb_s])
                op("dve", "tensor_tensor", kr[:, 0:64], ta, tb, ALU.subtract, reads=[ta_s, tb_s], writes=[kr_s])
                op("dve", "tensor_tensor", tc_, kn[:, 64:128], cosE, ALU.mult, reads=[kn_s, cosE_s], writes=[tc_s])
                op("dve", "tensor_tensor", td, kn[:, 0:64], sinE, ALU.mult, reads=[kn_s, sinE_s], writes=[td_s])
                op("dve", "tensor_tensor", kr[:, 64:128], tc_, td, ALU.add, reads=[tc_s, td_s], writes=[kr_s])
                pv5 = ps[5][:].bitcast(BF16)
                op("pe", "transpose", pv5[:, 0:128], kr, self.identb, reads=[kr_s, cs], writes=[pss[5]])
                op("dve", "tensor_copy", ob, pv5[:, 0:128], reads=[pss[5]], writes=[ob_s])
                op("sp", "dma_start", out=kcT_d[g, :, :], in_=ob, reads=[ob_s], writes=[self.dslot(("kcT", g))], dma=DMA_ST)

    def attend(self, W, qv, tiles, den_extra=None):
        op, ps, pss = self.op, self.ps, self.pss
        n = len(tiles)
        base = W["si"]
        W["si"] += n
        W["pair"] = 4 + 2 * (W["ci"] % 2)
        W["ci"] += 1
        bd, bo = W["pair"], W["pair"] + 1

        def issue_scores(i):
            t = tiles[i]
            sb = (base + i) % W["ns"]
            ms = t["masks"]
            op("pe", "matmul", ps[sb][:, :], t["kT"], qv, start=True, stop=(len(ms) == 0), reads=t["kslots"] + [W["q_s"]], writes=[pss[sb]])
            for mi, (lhsT, rhs, sl) in enumerate(ms):
                op("pe", "matmul", ps[sb][:, :], lhsT, rhs, start=False, stop=(mi == len(ms) - 1), reads=sl, writes=[pss[sb]])

        issue_scores(0)
        for i, t in enumerate(tiles):
            if i + 1 < n:
                issue_scores(i + 1)
            sb = (base + i) % W["ns"]
            e, e_s = W["e"][(base + i) % 3]
            op("act", "activation", e, ps[sb][:, :], AF.Exp, reads=[pss[sb]], writes=[e_s])
            op("pe", "matmul", ps[bd][:, :], self.onesb, e, start=(i == 0), stop=(i == n - 1), reads=[self.cslot, e_s], writes=[pss[bd]])
            op("pe", "matmul", ps[bo][:, :], t["v"], e, start=(i == 0), stop=(i == n - 1), reads=t["vslots"] + [e_s], writes=[pss[bo]])

    def attn_finish(self, W, acc, acc_s, first, gate=None, den_add=None):
        op, ps, pss = self.op, self.ps, self.pss
        bd, bo = W["pair"], W["pair"] + 1
        rd, rd_s = W["rd"][W["fi"] % 2]
        tm, tm_s = W["tm"][W["fi"] % 2]
        W["fi"] += 1
        if den_add is None:
            op("dve", "tensor_scalar", rd, ps[bd][:, :], 1e-30, None, ALU.max, reads=[pss[bd]], writes=[rd_s])
        else:
            op("dve", "tensor_scalar", rd, ps[bd][:, :], den_add[0], None, ALU.add, reads=[pss[bd], den_add[1]], writes=[rd_s])
        op("dve", "reciprocal", rd, rd, reads=[rd_s], writes=[rd_s])
        if gate is None:
            op("dve", "tensor_tensor", acc, ps[bo][:, :], rd, ALU.mult, reads=[pss[bo], rd_s], writes=[acc_s])
            return
        lhsT, rhs, sl = gate
        op("pe", "matmul", ps[3][:, :], lhsT, rhs, start=True, stop=True, reads=sl, writes=[pss[3]])
        op("dve", "tensor_tensor", tm, ps[bo][:, :], rd, ALU.mult, reads=[pss[bo], rd_s], writes=[tm_s])
        if first:
            op("dve", "tensor_tensor", acc, tm, ps[3][:, :], ALU.mult, reads=[tm_s, pss[3]], writes=[acc_s])
        else:
            op("dve", "tensor_tensor", tm, tm, ps[3][:, :], ALU.mult, reads=[tm_s, pss[3]], writes=[tm_s])
            op("pool", "tensor_tensor", acc, acc, tm, ALU.add, reads=[acc_s, tm_s], writes=[acc_s])

    def attn_work(self):
        A = self.A
        return {"e": [A.alloc([512], BF16, "e%d" % i) for i in range(3)],
                "rd": [A.alloc([512], F32, "rd%d" % i) for i in range(2)],
                "tm": [A.alloc([512], F32, "tm%d" % i) for i in range(2)],
                "acc": [A.alloc([512], F32, "acc%d" % i) for i in range(2)],
                "ob": [A.alloc([512], BF16, "aob%d" % i) for i in range(2)],
                "si": 0, "mi": 0, "fi": 0, "oi": 0, "ci": 0, "pair": 4, "ns": 3}

    def phase_nsa_attn(self, zT_d, qk_d, v_d, kcT_d, vc_d, mixT_d, C, side=None):
        self.phase()
        A, op, ps, pss = self.A, self.op, self.ps, self.pss
        cs = self.cslot
        q, q_s = A.alloc([8, T], BF16, "q")
        kk, kk_s = A.alloc([4, T], BF16, "kk")
        vv, vv_s = A.alloc([4, 16, 128], BF16, "vv")
        kcT, kcT_s = A.alloc([2, 128], BF16, "kcT")
        vc, vc_s = A.alloc([2, 128], BF16, "vc")
        sg, sg_s = A.alloc([T], BF16, "sg")
        validT, k1 = A.alloc([T], BF16, "validT")
        cm, k2 = A.alloc([4, 512], BF16, "cm")
        wm, k3 = A.alloc([6, 512], BF16, "wm")
        E, k4 = A.alloc([16, 128], BF16, "E")
        SelG, k5 = A.alloc([24, 128], BF16, "SelG")
        selT, selT_s = A.alloc([2, T], BF16, "selT")
        mark = A.off
        glf, glf_s = A.alloc([T], F32, "glf")
        ovl, k6 = A.alloc([32], F32, "ovl")
        topA, k7 = A.alloc([16, 32], F32, "topA")
        topB, k8 = A.alloc([16, 32], F32, "topB")
        ks_ = Slot("nsaconst")
        for dst, name in ((validT, "validT"), (cm.rearrange("p a b -> p (a b)"), "cm"), (wm.rearrange("p a b -> p (a b)"), "wm"),
                          (E.rearrange("p a b -> p (a b)"), "E"), (SelG.rearrange("p a b -> p (a b)"), "SelG"), (ovl, "ovl"),
                          (topA.rearrange("p a b -> p (a b)"), "topA"), (topB.rearrange("p a b -> p (a b)"), "topB")):
            op("sp", "dma_start", out=dst, in_=C[name][:, :], writes=[ks_], dma=DMA_MISC)
        for h in range(8):
            op("sp", "dma_start", out=q[:, h, :], in_=qk_d[h], reads=[self.dslot(("qk", h))], writes=[q_s], dma=DMA_LD)
        for i in range(4):
            op("sp", "dma_start", out=kk[:, i, :], in_=qk_d[8 + i], reads=[self.dslot(("qk", 8 + i))], writes=[kk_s], dma=DMA_LD)
            op("sp", "dma_start", out=vv[:, i, :, :].rearrange("p a b -> p (a b)"), in_=v_d[i], reads=[self.dslot(("v", i))], writes=[vv_s], dma=DMA_LD)
        for g in range(2):
            op("sp", "dma_start", out=kcT[:, g, :], in_=kcT_d[g, :, :], reads=[self.dslot(("kcT", g))], writes=[kcT_s], dma=DMA_LD)
            op("sp", "dma_start", out=vc[:, g, :], in_=vc_d[g, :, :], reads=[self.dslot(("vc", g))], writes=[vc_s], dma=DMA_LD)
        op("sp", "dma_start", out=glf[0:24, :], in_=zT_d[36 * 128:36 * 128 + 24, :], reads=[self.dslot(("zT", 36))], writes=[glf_s], dma=DMA_LD)
        op("act", "activation", sg[0:24, :], glf[0:24, :], AF.Sigmoid, reads=[glf_s], writes=[sg_s])
        impT, impT_s = A.alloc([2, T], F32, "impT")
        e32 = [A.alloc([512], F32, "e32%d" % i) for i in range(2)]
        rdn = [A.alloc([512], F32, "rdn%d" % i) for i in range(2)]
        items = [(g, tq, r) for g in range(2) for tq in range(4) for r in range(4)]

        def s1_a(i):
            g, tq, r = items[i]
            sl = slice(tq * 512, (tq + 1) * 512)
            h = g * 4 + r
            e, e_s = e32[i % 2]
            sb = i % 2
            db = 2 + i % 2
            op("pe", "matmul", ps[sb][:, :], kcT[:, g, :], q[:, h, sl], start=True, stop=False, reads=[kcT_s, q_s], writes=[pss[sb]])
            op("pe", "matmul", ps[sb][:, :], self.identb, validT[:, sl], start=False, stop=True, reads=[cs, ks_], writes=[pss[sb]])
            op("act", "activation", e, ps[sb][:, :], AF.Exp, reads=[pss[sb]], writes=[e_s])
            op("pe", "matmul", ps[db][:, :], self.onesf, e, start=True, stop=True, reads=[cs, e_s], writes=[pss[db]])

        def s1_b(i):
            g, tq, r = items[i]
            sl = slice(tq * 512, (tq + 1) * 512)
            e, e_s = e32[i % 2]
            rd, rd_s = rdn[i % 2]
            db = 2 + i % 2
            op("dve", "tensor_scalar", rd, ps[db][:, :], 1e-30, None, ALU.max, reads=[pss[db]], writes=[rd_s])
            op("dve", "reciprocal", rd, rd, reads=[rd_s], writes=[rd_s])
            op("pool", "tensor_tensor", e, e, rd, ALU.mult, reads=[e_s, rd_s], writes=[e_s])
            op("pe", "matmul", ps[4][0:32, :], ovl, e, start=(r == 0), stop=(r == 3), reads=[ks_, e_s], writes=[pss[4]])
            if r == 3:
                op("act", "copy", impT[0:32, g, sl], ps[4][0:32, :], reads=[pss[4]], writes=[impT_s])

        s1_a(0)
        for i in range(len(items)):
            if i + 1 < len(items):
                s1_a(i + 1)
            s1_b(i)
        sc = [A.alloc([32], F32, "sc%d" % i) for i in range(2)]
        m8 = [A.alloc([8], F32, "m8%d" % i) for i in range(2)]
        titems = [(g, tt) for g in range(2) for tt in range(16)]

        def tk_a(i):
            g, tt = titems[i]
            s_, s_s = sc[i % 2]
            m_, m_s = m8[i % 2]
            b5 = 5 + i % 2
            op("pe", "matmul", ps[b5][:, 0:32], impT[0:32, g, tt * 128:(tt + 1) * 128], self.identf[0:32, 0:32], start=True, stop=True,
               reads=[impT_s, cs], writes=[pss[b5]])
            op("dve", "tensor_tensor", s_, ps[b5][:, 0:32], topA[:, tt, :], ALU.mult, reads=[pss[b5], ks_], writes=[s_s])
            op("dve", "tensor_tensor", s_, s_, topB[:, tt, :], ALU.add, reads=[s_s, ks_], writes=[s_s])
            op("dve", "max", m_, s_, reads=[s_s], writes=[m_s])

        def tk_b(i):
            g, tt = titems[i]
            s_, s_s = sc[i % 2]
            m_, m_s = m8[i % 2]
            op("dve", "tensor_scalar", s_, s_, m_[:, 7:8], None, ALU.is_ge, reads=[s_s, m_s], writes=[s_s])
            op("pe", "matmul", ps[7][0:32, 0:128], s_, self.identf, start=True, stop=True, reads=[s_s, cs], writes=[pss[7]])
            op("act", "activation", selT[0:32, g, tt * 128:(tt + 1) * 128], ps[7][0:32, 0:128], AF.Identity, scale=30000.0,
               bias=self.epsc[0:32, 3:4], reads=[pss[7], cs], writes=[selT_s])

        tk_a(0)
        for i in range(len(titems)):
            if i + 1 < len(titems):
                tk_a(i + 1)
            tk_b(i)
        self.P.barrier()
        A.off = mark
        W = self.attn_work()
        W["q_s"] = q_s
        gen = None
        if side is not None:
            W["ns"] = 2
            gen = side()
        for h in range(8):
            g = h // 4
            for tq in range(4):
                if gen is not None:
                    try:
                        next(gen)
                    except StopIteration:
                        gen = None
                sl = slice(tq * 512, (tq + 1) * 512)
                qv = q[:, h, sl]
                acc, acc_s = W["acc"][W["oi"] % 2]
                ob, ob_s = W["ob"][W["oi"] % 2]
                W["oi"] += 1
                tiles = [dict(kT=kcT[:, g, :], kslots=[kcT_s], v=vc[:, g, :], vslots=[vc_s], masks=[(self.identb, validT[:, sl], [cs, ks_])])]
                self.attend(W, qv, tiles)
                self.attn_finish(W, acc, acc_s, True, gate=(SelG[0:24, h * 3 + 0, :], sg[0:24, sl], [ks_, sg_s]))
                tiles = []
                for kt in range(4 * tq + 4):
                    masks = [(E[0:32, kt, :], selT[0:32, g, sl], [ks_, selT_s])]
                    if kt >= 4 * tq:
                        masks.append((self.identb, cm[:, kt - 4 * tq, :], [cs, ks_]))
                    tiles.append(dict(kT=kk[:, g, kt * 128:(kt + 1) * 128], kslots=[kk_s], v=vv[:, g, kt, :], vslots=[vv_s], masks=masks))
                self.attend(W, qv, tiles)
                self.attn_finish(W, acc, acc_s, False, gate=(SelG[0:24, h * 3 + 1, :], sg[0:24, sl], [ks_, sg_s]))
                tiles = []
                for kt in range(max(0, 4 * tq - 2), 4 * tq + 4):
                    tiles.append(dict(kT=kk[:, 2 + g, kt * 128:(kt + 1) * 128], kslots=[kk_s], v=vv[:, 2 + g, kt, :], vslots=[vv_s],
                                      masks=[(self.identb, wm[:, kt - 4 * tq + 2, :], [cs, ks_])]))
                self.attend(W, qv, tiles)
                self.attn_finish(W, acc, acc_s, False, gate=(SelG[0:24, h * 3 + 2, :], sg[0:24, sl], [ks_, sg_s]))
                op("act", "copy", ob, acc, reads=[acc_s], writes=[ob_s])
                r0 = 1024 + h * 128
                op("sp", "dma_start", out=mixT_d[r0:r0 + 128, sl], in_=ob, reads=[ob_s], writes=[self.dslot(("mixT", 8 + h))], dma=DMA_ST)
        if gen is not None:
            for _ in gen:
                pass

    def phase_swa_attn(self, qk_d, v_d, sinks, mixT_d, C):
        self.phase()
        A, op, ps, pss = self.A, self.op, self.ps, self.pss
        q, q_s = A.alloc([8, T], BF16, "q")
        kk, kk_s = A.alloc([2, T], BF16, "kk")
        vv, vv_s = A.alloc([2, 16, 128], BF16, "vv")
        sm, ks_ = A.alloc([5, 512], BF16, "sm")
        es, es_s = A.alloc([8], F32, "es")
        op("sp", "dma_start", out=sm.rearrange("p a b -> p (a b)"), in_=C["sm"][:, :], writes=[ks_], dma=DMA_MISC)
        op("sp", "dma_start", out=es, in_=sinks.partition_broadcast(128), writes=[es_s], dma=DMA_MISC)
        op("act", "activation", es, es, AF.Exp, reads=[es_s], writes=[es_s])
        for h in range(8):
            op("sp", "dma_start", out=q[:, h, :], in_=qk_d[h], reads=[self.dslot(("qk", h))], writes=[q_s], dma=DMA_LD)
        for i in range(2):
            op("sp", "dma_start", out=kk[:, i, :], in_=qk_d[8 + i], reads=[self.dslot(("qk", 8 + i))], writes=[kk_s], dma=DMA_LD)
            op("sp", "dma_start", out=vv[:, i, :, :].rearrange("p a b -> p (a b)"), in_=v_d[i], reads=[self.dslot(("v", i))], writes=[vv_s], dma=DMA_LD)
        W = self.attn_work()
        W["q_s"] = q_s
        for h in range(8):
            g = h // 4
            for tq in range(4):
                sl = slice(tq * 512, (tq + 1) * 512)
                acc, acc_s = W["acc"][W["oi"] % 2]
                ob, ob_s = W["ob"][W["oi"] % 2]
                W["oi"] += 1
                tiles = []
                for kt in range(max(0, 4 * tq - 1), 4 * tq + 4):
                    tiles.append(dict(kT=kk[:, g, kt * 128:(kt + 1) * 128], kslots=[kk_s], v=vv[:, g, kt, :], vslots=[vv_s],
                                      masks=[(self.identb, sm[:, kt - 4 * tq + 1, :], [self.cslot, ks_])]))
                self.attend(W, q[:, h, sl], tiles)
                self.attn_finish(W, acc, acc_s, True, gate=None, den_add=(es[:, h:h + 1], es_s))
                op("act", "copy", ob, acc, reads=[acc_s], writes=[ob_s])
                op("sp", "dma_start", out=mixT_d[h * 128:(h + 1) * 128, sl], in_=ob, reads=[ob_s], writes=[self.dslot(("mixT", h))], dma=DMA_ST)

    def gelu_tanh(self, out, out_s, x, x_s, t, t_s):
        self.op("act", "activation", out, x, AF.Gelu_apprx_tanh, reads=[x_s], writes=[out_s])

    def phase_gmlp(self, zT_d, mixT_d, ln_g, ln_b, w_s, b_s, C):
        self.phase()
        A, op, ps, pss = self.A, self.op, self.ps, self.pss
        cs = self.cslot
        NC = 8
        cw, cw_s = A.alloc([NC, 2], F32, "gcw")
        mark = A.off
        self.rows_to_cols([ln_g, ln_b], 1024, cw, cw_s, 0)
        self.P.barrier()
        A.off = mark
        wT, wT_s = A.alloc([NC, 128], BF16, "wT")
        tril, tril_s = A.alloc([128], F32, "tril")
        wl = [A.alloc([128], F32, "wl%d" % i) for i in range(2)]
        bsb, bsb_s = A.alloc([NC, 512], F32, "bsb")
        op("sp", "dma_start", out=tril, in_=C["tril"][:, :], writes=[tril_s], dma=DMA_MISC)
        for r in range(4):
            op("sp", "dma_start", out=bsb[:, :, r * 128:(r + 1) * 128], in_=b_s.partition_broadcast(128), writes=[bsb_s], dma=DMA_MISC)
        for g in range(NC):
            w, w_s2 = wl[g % 2]
            op("sp", "dma_start", out=w, in_=w_s[g], writes=[w_s2], dma=DMA_MISC)
            op("dve", "tensor_tensor", w, w, tril, ALU.mult, reads=[w_s2, tril_s], writes=[w_s2])
            bk = g % 2
            op("pe", "matmul", ps[bk][:, 0:128], w, self.identf, start=True, stop=True, reads=[w_s2, cs], writes=[pss[bk]])
            op("act", "copy", wT[:, g, :], ps[bk][:, 0:128], reads=[pss[bk]], writes=[wT_s])
        self.P.barrier()
        vg = [A.alloc([T], F32, "vg%d" % c) for c in range(NC)]
        xl = [A.alloc([T], F32, "gx%d" % i) for i in range(2)]
        tb, tb_s = A.alloc([T], F32, "gt")
        sqb, sqb_s = A.alloc([T], F32, "gsq")
        for c in range(NC):
            x, x_s = xl[c % 2]
            y, y_s = vg[c]
            op("sp", "dma_start", out=x, in_=zT_d[(20 + c) * 128:(21 + c) * 128, :], reads=[self.dslot(("zT", 20 + c))], writes=[x_s], dma=DMA_LD)
            self.gelu_tanh(y, y_s, x, x_s, tb, tb_s)
            self.ln_stats_acc(y, y_s, sqb, sqb_s, c, NC)
        mean, mean_s = A.alloc([T], F32, "gmean")
        rstd, rstd_s = A.alloc([T], F32, "grstd")
        self.ln_finish(mean, mean_s, rstd, rstd_s, 1024)
        self.P.barrier()
        ugs = [(sqb, sqb_s), (tb, tb_s)]
        vnb = [A.alloc([T], BF16, "vnb%d" % i) for i in range(2)]
        vtok = [A.alloc([16, 128], BF16, "vtok%d" % i) for i in range(2)]
        tmp = [A.alloc([512], F32, "gtmp%d" % i) for i in range(2)]
        ob = [A.alloc([T], BF16, "gob%d" % i) for i in range(2)]

        def stage_a(c):
            y, y_s = vg[c]
            x, x_s = xl[c % 2]
            vn, vn_s = vnb[c % 2]
            vt, vt_s = vtok[c % 2]
            ug, ug_s = ugs[c % 2]
            op("sp", "dma_start", out=x, in_=zT_d[(12 + c) * 128:(13 + c) * 128, :], reads=[self.dslot(("zT", 12 + c))], writes=[x_s], dma=DMA_LD)
            op("act", "activation", ug, x, AF.Gelu_apprx_tanh, reads=[x_s], writes=[ug_s])
            op("dve", "tensor_tensor", y, y, mean, ALU.subtract, reads=[y_s, mean_s], writes=[y_s])
            op("pool", "tensor_tensor", y, y, rstd, ALU.mult, reads=[y_s, rstd_s], writes=[y_s])
            op("act", "activation", vn, y, AF.Identity, bias=cw[:, c, 1:2], scale=cw[:, c, 0:1], reads=[y_s, cw_s], writes=[vn_s])
            for half in range(2):
                bk = 6 + half
                pv = ps[bk][:].bitcast(BF16)
                for j in range(8):
                    n = half * 8 + j
                    op("pe", "transpose", pv[:, j * 128:(j + 1) * 128], vn[:, n * 128:(n + 1) * 128], self.identb,
                       reads=[vn_s, cs], writes=[pss[bk]])
                op("dve", "tensor_copy", vt[:, half * 8:(half + 1) * 8, :].rearrange("p a b -> p (a b)"), pv[:, 0:1024],
                   reads=[pss[bk]], writes=[vt_s])

        def stage_b(c):
            vt, vt_s = vtok[c % 2]
            o, o_s = ob[c % 2]
            ug, ug_s = ugs[c % 2]
            for tq in range(4):
                bk = tq
                t_, t_s = tmp[tq % 2]
                sl = slice(tq * 512, (tq + 1) * 512)
                for r in range(4):
                    n = tq * 4 + r
                    op("pe", "matmul", ps[bk][:, r * 128:(r + 1) * 128], vt[:, n, :], wT[:, c, :], start=True, stop=True,
                       reads=[vt_s, wT_s], writes=[pss[bk]])
                op("dve", "tensor_tensor", t_, ps[bk][:, :], bsb[:, c, :], ALU.add, reads=[pss[bk], bsb_s], writes=[t_s])
                op("pool", "tensor_tensor", o[:, sl], t_, ug[:, sl], ALU.mult, reads=[t_s, ug_s], writes=[o_s])
            r0 = 1024 + c * 128
            op("sp", "dma_start", out=mixT_d[r0:r0 + 128, :], in_=o, reads=[o_s], writes=[self.dslot(("mixT", 8 + c))], dma=DMA_ST)

        stage_a(0)
        for c in range(NC):
            if c + 1 < NC:
                stage_a(c + 1)
            stage_b(c)


def host_consts():
    c = {}
    c["identb"] = np.eye(128, dtype=np.float32).astype(ml_dtypes.bfloat16)
    c["identf"] = np.eye(128, dtype=np.float32)
    c["onesf"] = np.ones((128, 128), dtype=np.float32)
    c["onesb"] = np.ones((128, 128), dtype=np.float32).astype(ml_dtypes.bfloat16)
    e = np.zeros((128, 8), dtype=np.float32)
    e[:, 0] = EPS
    e[:, 1] = 1e-30
    e[:, 2] = -np.pi
    e[:, 3] = -30000.0
    c["epsc"] = e
    inv = (10000.0 ** (-np.arange(0, 128, 2, dtype=np.float32) / 128)).astype(np.float32)
    c["invc"] = np.concatenate([inv, inv])[:, None].astype(np.float32)
    c["invrow"] = np.broadcast_to(inv[None, :], (128, 64)).astype(np.float32).copy()
    rp = np.zeros((128, 128), np.float32)
    for k in range(128):
        if k >= 64:
            rp[k, k - 64] = -1.0
        else:
            rp[k, k + 64] = 1.0
    c["rotPT"] = rp
    bf = ml_dtypes.bfloat16
    tt = np.arange(T)
    ends = np.arange(127) * 16 + 31
    v = np.zeros((128, T), np.float32)
    v[:127] = (ends[:, None] <= tt[None, :])
    NEGM = 30000.0
    c["validT"] = ((v - 1.0) * NEGM).astype(bf)
    p = np.arange(128)[:, None]
    j = np.arange(512)[None, :]
    cmk = np.stack([((i * 128 + p) <= j) for i in range(4)], axis=1).astype(np.float32)
    c["cm"] = ((cmk - 1.0) * NEGM).reshape(128, 4 * 512).astype(bf)
    rels = [-256, -128, 0, 128, 256, 384]
    wmk = np.stack([((j - p - r) >= 0) & ((j - p - r) < 256) for r in rels], axis=1).astype(np.float32)
    c["wm"] = ((wmk - 1.0) * NEGM).reshape(128, 6 * 512).astype(bf)
    rels = [-128, 0, 128, 256, 384]
    smk = np.stack([((j - p - r) >= 0) & ((j - p - r) < 128) for r in rels], axis=1).astype(np.float32)
    c["sm"] = ((smk - 1.0) * NEGM).reshape(128, 5 * 512).astype(bf)
    E = np.zeros((128, 16, 128), np.float32)
    for kt in range(16):
        for pp in range(128):
            E[2 * kt + pp // 64, kt, pp] = 1.0
    c["E"] = E.reshape(128, 16 * 128).astype(bf)
    SG = np.zeros((128, 24, 128), np.float32)
    for i in range(24):
        SG[i, i, :] = 1.0
    c["SelG"] = SG.reshape(128, 24 * 128).astype(bf)
    starts = np.arange(127) * 16
    sel_start = np.arange(32) * 64
    ov = np.zeros((128, 32), np.float32)
    ov[:127] = ((starts[:, None] <= sel_start[None, :] + 63) & (ends[:, None] >= sel_start[None, :]))
    c["ovl"] = ov
    cur = tt // 64
    jb = np.arange(32)[None, :]
    forced = (jb == 0) | (jb == cur[:, None]) | (jb == cur[:, None] - 1)
    valid_s = sel_start[None, :] <= tt[:, None]
    Am = (valid_s & ~forced).astype(np.float32)
    Bm = np.where(forced, 1e6, np.where(valid_s, 0.0, -1.0)).astype(np.float32)
    c["topA"] = Am.reshape(16, 128, 32).transpose(1, 0, 2).reshape(128, 16 * 32).copy()
    c["topB"] = Bm.reshape(16, 128, 32).transpose(1, 0, 2).reshape(128, 16 * 32).copy()
    c["tril"] = np.tril(np.ones((128, 128), np.float32))
    return c


CONST_SPECS = {"identb": ([128, 128], BF16), "identf": ([128, 128], F32), "onesf": ([128, 128], F32),
               "onesb": ([128, 128], BF16), "epsc": ([128, 8], F32), "invc": ([128, 1], F32),
               "invrow": ([128, 64], F32), "rotPT": ([128, 128], F32),
               "validT": ([128, T], BF16), "cm": ([128, 2048], BF16), "wm": ([128, 3072], BF16), "sm": ([128, 2560], BF16),
               "E": ([128, 2048], BF16), "SelG": ([128, 3072], BF16), "ovl": ([128, 32], F32),
               "topA": ([128, 512], F32), "topB": ([128, 512], F32), "tril": ([128, 128], F32)}

IN_SPECS = [
    ("x", [T, D], F32), ("c", [1, D], F32), ("positions", [1, T], I32),
    ("ada_w", [2, D, 6 * D], F32), ("ada_b", [2, 6 * D], F32), ("norm_g", [2, 2, D], F32),
    ("ffn_w_up", [2, D, 2 * DFF], F32), ("ffn_conv_w", [2, 3, 2 * DFF], F32), ("ffn_conv_b", [2, 2 * DFF], F32),
    ("ffn_w_down", [2, DFF, D], F32),
    ("ev_w_in", [1, D, EVEN_IN], F32), ("ev_w_out", [1, D, D], F32), ("ev_conv_w", [1, 31, 1024], F32),
    ("ev_conv_b", [1, 1024], F32), ("ev_conv_ln_g", [1, 1024], F32), ("ev_conv_ln_b", [1, 1024], F32),
    ("ev_q_norm", [1, 128], F32), ("ev_k_norm", [1, 3, 128], F32),
    ("ev_cmp_k_pos", [1, 32, 128], F32), ("ev_cmp_k_w1", [1, 4096, 256], F32), ("ev_cmp_k_w2", [1, 256, 128], F32),
    ("ev_cmp_v_pos", [1, 32, 128], F32), ("ev_cmp_v_w1", [1, 4096, 256], F32), ("ev_cmp_v_w2", [1, 256, 128], F32),
    ("od_w_in", [1, D, ODD_IN], F32), ("od_w_out", [1, D, D], F32), ("od_q_norm", [1, 128], F32),
    ("od_k_norm", [1, 128], F32), ("od_sinks", [1, 8], F32), ("od_gmlp_ln_g", [1, 1024], F32),
    ("od_gmlp_ln_b", [1, 1024], F32), ("od_gmlp_w_s", [1, 8, 128, 128], F32), ("od_gmlp_b_s", [1, 8, 128], F32),
]


def build_nc(dbg_outs=(), stop=None):
    nc = bass.Bass("TRN2", target_bir_lowering=False)
    I = {}
    for name, shape, dt in IN_SPECS:
        I[name] = nc.dram_tensor(name, shape, dt, kind="ExternalInput").ap()
    C = {}
    for name, (shape, dt) in CONST_SPECS.items():
        C[name] = nc.dram_tensor("k_" + name, shape, dt, kind="ExternalInput").ap()
    out = nc.dram_tensor("out", [T, D], F32, kind="ExternalOutput").ap()

    def scratch(name, shape, dt):
        kind = "ExternalOutput" if name in dbg_outs else "Internal"
        return nc.dram_tensor(name, shape, dt, kind=kind).ap()

    gbc_d = scratch("gbc_d", [2, 2, 128, D], F32)
    zT_d = scratch("zT_d", [37 * 128, T], F32)
    mixT_d = scratch("mixT_d", [D, T], BF16)
    xa_d = scratch("xa_d", [T, D], F32)
    xb_d = scratch("xb_d", [T, D], F32)
    cs_d = scratch("cs_d", [2, 128, T], F32)
    qk_d = scratch("qk_d", [12, 128, T], BF16)
    v_d = scratch("v_d", [4, 128, T], BF16)
    kcT_d = scratch("kcT_d", [2, 128, 128], BF16)
    vc_d = scratch("vc_d", [2, 128, 128], BF16)

    with ExitStack() as st:
        B = Builder(nc, st)
        B.load_consts(C)
        B.phase_rope(I["positions"], cs_d, C)
        B.phase_inproj(0, I["x"], "x_in", I["ev_w_in"][0], EVEN_IN, zT_d,
                       side=lambda: B.mod_gen([0], I["c"], I["ada_w"], I["ada_b"], I["norm_g"], gbc_d, (4, 4, 4, 4)), side_first=8)
        B.phase_conformer(zT_d, mixT_d, I["ev_conv_w"][0], I["ev_conv_b"][0:1, :], I["ev_conv_ln_g"][0:1, :], I["ev_conv_ln_b"][0:1, :])
        jobs = []
        for h in range(8):
            jobs.append((16 + h, I["ev_q_norm"][0:1, :], 128 ** -0.5, qk_d[h], ("qk", h)))
        for g in range(2):
            jobs.append((28 + g, I["ev_k_norm"][0, 1:2, :], 1.0, qk_d[8 + g], ("qk", 8 + g)))
        for g in range(2):
            jobs.append((32 + g, I["ev_k_norm"][0, 2:3, :], 1.0, qk_d[10 + g], ("qk", 10 + g)))
        B.phase_qk(zT_d, cs_d, jobs, C)
        vj = []
        for g in range(2):
            vj.append((30 + g, v_d[g], ("v", g)))
        for g in range(2):
            vj.append((34 + g, v_d[2 + g], ("v", 2 + g)))
        B.phase_vprep(zT_d, vj)
        prm = {"k_pos": I["ev_cmp_k_pos"][0], "k_w1": I["ev_cmp_k_w1"][0], "k_w2": I["ev_cmp_k_w2"][0],
               "v_pos": I["ev_cmp_v_pos"][0], "v_w1": I["ev_cmp_v_w1"][0], "v_w2": I["ev_cmp_v_w2"][0],
               "k_gain": I["ev_k_norm"][0, 0:1, :]}
        B.phase_compress(zT_d, I["positions"], C, prm, kcT_d, vc_d)
        B.phase_nsa_attn(zT_d, qk_d, v_d, kcT_d, vc_d, mixT_d, C,
                         side=lambda: B.mod_gen([1], I["c"], I["ada_w"], I["ada_b"], I["norm_g"], gbc_d, (2, 2, 2, 2)))
        B.phase_outproj(mixT_d, I["ev_w_out"][0], I["x"], "x_in", gbc_d[0, 0], ("gbc", 0, 0), xa_d, "xa")
        if stop == "mix0":
            B.P.barrier()
            B.P.emit(st)
            return nc
        B.phase_ffn(0, xa_d, "xa", xb_d, "xb", I["ffn_w_up"][0], I["ffn_conv_w"][0], I["ffn_conv_b"][0:1, :], I["ffn_w_down"][0],
                    gbc_d[0, 1], ("gbc", 0, 1))
        B.phase_inproj(1, xb_d, "xb", I["od_w_in"][0], ODD_IN, zT_d)
        jobs = []
        for h in range(8):
            jobs.append((h, I["od_q_norm"][0:1, :], 128 ** -0.5, qk_d[h], ("qk", h)))
        for g in range(2):
            jobs.append((8 + g, I["od_k_norm"][0:1, :], 1.0, qk_d[8 + g], ("qk", 8 + g)))
        B.phase_qk(zT_d, cs_d, jobs, C)
        B.phase_vprep(zT_d, [(10 + g, v_d[g], ("v", g)) for g in range(2)])
        B.phase_swa_attn(qk_d, v_d, I["od_sinks"][0:1, :], mixT_d, C)
        B.phase_gmlp(zT_d, mixT_d, I["od_gmlp_ln_g"][0:1, :], I["od_gmlp_ln_b"][0:1, :], I["od_gmlp_w_s"][0],
                     I["od_gmlp_b_s"][0].rearrange("g t -> (g t)").rearrange("(o n) -> o n", o=1), C)
        B.phase_outproj(mixT_d, I["od_w_out"][0], xb_d, "xb", gbc_d[1, 0], ("gbc", 1, 0), xa_d, "xa")
        B.phase_ffn(1, xa_d, "xa", out, "out", I["ffn_w_up"][1], I["ffn_conv_w"][1], I["ffn_conv_b"][1:2, :], I["ffn_w_down"][1],
                    gbc_d[1, 1], ("gbc", 1, 1))
        B.P.barrier()
        B.P.emit(st)
    return nc


_NC_CACHE = {}


def kernel(**inputs):
    n = 8
    if "nc" not in _NC_CACHE:
        _NC_CACHE["nc"] = build_nc()
    nc = _NC_CACHE["nc"]
    consts = host_consts()
    shared = {}
    for name, shape, dt in IN_SPECS:
        if name in ("x", "c", "positions"):
            continue
        shared[name] = np.ascontiguousarray(inputs[name])
    for k, v in consts.items():
        shared["k_" + k] = v
    in_maps = []
    for b in range(n):
        m = dict(shared)
        m["x"] = np.ascontiguousarray(inputs["x"][b])
        m["c"] = np.ascontiguousarray(inputs["c"][b:b + 1])
        m["positions"] = np.ascontiguousarray(inputs["positions"][b:b + 1]).astype(np.int32)
        in_maps.append(m)
    res = run_bass_kernel_spmd(nc, in_maps, core_ids=list(range(n)))
    return np.stack([np.asarray(r["out"]) for r in res.results], axis=0).astype(np.float32)
```

```python
import numpy as np
import ml_dtypes
from contextlib import ExitStack
import concourse.bass as bass
import concourse.mybir as mybir
from concourse.bass_utils import run_bass_kernel_spmd

F32 = mybir.dt.float32
BF16 = mybir.dt.bfloat16
I32 = mybir.dt.int32
AF = mybir.ActivationFunctionType
ALU = mybir.AluOpType
AX = mybir.AxisListType

D = 2048
T = 2048
NCH = 16
DFF = 5632
EVEN_IN = 4632
ODD_IN = 3584
EPS = 1e-6
ENGS = ("pe", "act", "dve", "pool", "sp")
NDMA = 90


class Slot:
    __slots__ = ("name", "w", "r", "is_dram")

    def __init__(self, name="", is_dram=False):
        self.name = name
        self.w = None
        self.r = {}
        self.is_dram = is_dram


class Prog:
    def __init__(self, nc):
        self.nc = nc
        self.streams = {e: [] for e in ENGS}
        self.cnt = {e: 0 for e in ENGS}
        for i in range(NDMA):
            self.cnt["dma%d" % i] = 0
        self.waited = {e: {} for e in ENGS}
        self.nins = 0
        self.slot_sem = {}
        self.free_dma = ["dma%d" % i for i in range(NDMA)]

    def dma_sem_for(self, slots):
        cand = [s for s in slots if not s.is_dram]
        assert len(cand) >= 1, "DMA needs an SBUF-side slot"
        s = cand[0]
        k = self.slot_sem.get(id(s))
        if k is None:
            assert self.free_dma, "out of DMA semaphores in this phase"
            k = self.free_dma.pop(0)
            self.slot_sem[id(s)] = k
        for o in cand[1:]:
            self.slot_sem.setdefault(id(o), k)
        return k

    def _need(self, eng, ev, waits):
        if ev is None:
            return
        k, v = ev
        if self.waited[eng].get(k, 0) >= v:
            return
        waits[k] = max(waits.get(k, 0), v)

    def op(self, eng, meth, *args, reads=(), writes=(), dma=None, **kwargs):
        waits = {}
        for s in reads:
            self._need(eng, s.w, waits)
        for s in writes:
            self._need(eng, s.w, waits)
            for ev in s.r.items():
                self._need(eng, ev, waits)
        if eng == "pe":
            waits.pop("pe", None)
        for k, v in waits.items():
            self.waited[eng][k] = max(self.waited[eng].get(k, 0), v)
        if dma is None:
            self.cnt[eng] += 1
            ev = (eng, self.cnt[eng])
            inc = (eng, 1)
        else:
            k = self.dma_sem_for(list(writes) + list(reads))
            self.cnt[k] += 16
            ev = (k, self.cnt[k])
            inc = (k, 16)
        self.streams[eng].append((list(waits.items()), (meth, args, kwargs), inc))
        self.nins += 1
        for s in reads:
            if s.r.get(ev[0], 0) < ev[1]:
                s.r[ev[0]] = ev[1]
        for s in writes:
            s.w = ev
            s.r = {}
        return ev

    def wait_all(self, eng, events):
        waits = {}
        for ev in events:
            self._need(eng, ev, waits)
        for k, v in waits.items():
            self.waited[eng][k] = max(self.waited[eng].get(k, 0), v)
        self.streams[eng].append((list(waits.items()), None, None))

    def barrier(self):
        evs = [(k, v) for k, v in self.cnt.items() if v > 0]
        for e in ENGS:
            self.wait_all(e, evs)
        self.slot_sem = {}
        self.free_dma = ["dma%d" % i for i in range(NDMA)]

    def emit(self, stack):
        nc = self.nc
        sems = {k: stack.enter_context(nc.semaphore("s_" + k)) for k in self.cnt}
        block = stack.enter_context(nc.Block())
        streams = self.streams

        def run(name):
            def body(eng):
                for waits, fn, inc in streams[name]:
                    for k, v in waits:
                        eng.wait_ge(sems[k], v)
                    if fn is not None:
                        ins = getattr(eng, fn[0])(*fn[1], **fn[2])
                        ins.then_inc(sems[inc[0]], inc[1])
            return body

        block.tensor(run("pe"))
        block.scalar(run("act"))
        block.vector(run("dve"))
        block.gpsimd(run("pool"))
        block.sync(run("sp"))


class Arena:
    def __init__(self, t, words):
        self.t = t
        self.words = words
        self.base = 0
        self.off = 0

    def reset(self):
        self.off = self.base

    def alloc(self, free_shape, dt, name=""):
        n = int(np.prod(free_shape))
        esz = 4 if dt in (F32, I32) else 2
        w = (n * esz + 3) // 4
        w = (w + 7) // 8 * 8
        assert self.off + w <= self.words, "arena overflow %s need %d have %d" % (name, w, self.words - self.off)
        v = self.t[:, self.off:self.off + w]
        self.off += w
        if esz == 2:
            v = v.bitcast(dt)[:, 0:n]
        else:
            v = v[:, 0:n]
            if dt != F32:
                v = v.bitcast(dt)
        if len(free_shape) == 2:
            v = v.rearrange("p (a b) -> p a b", a=free_shape[0])
        elif len(free_shape) == 3:
            v = v.rearrange("p (a b c) -> p a b c", a=free_shape[0], b=free_shape[1])
        return v, Slot(name)


DMA_W, DMA_LD, DMA_ST, DMA_MISC, DMA_LD2, DMA_W2 = 0, 1, 2, 3, 4, 5


class Builder:
    def __init__(self, nc, st, dbg=None):
        self.nc = nc
        self.st = st
        self.P = Prog(nc)
        self.dbg = dbg
        self.arena_t = st.enter_context(nc.sbuf_tensor("arena", [128, 200 * 256], F32))
        self.A = Arena(self.arena_t, 200 * 256)
        self.ps = []
        self.pss = []
        for i in range(8):
            self.ps.append(st.enter_context(nc.psum_tensor("ps%d" % i, [128, 512], F32)))
            self.pss.append(Slot("ps%d" % i))
        self.dslots = {}

    def op(self, *a, **k):
        return self.P.op(*a, **k)

    def dslot(self, key):
        if key not in self.dslots:
            self.dslots[key] = Slot(str(key), is_dram=True)
        return self.dslots[key]

    def phase(self):
        self.P.barrier()
        self.A.reset()

    def load_consts(self, cd):
        A = self.A
        self.identb, s1 = A.alloc([128], BF16, "identb")
        self.identf, s2 = A.alloc([128], F32, "identf")
        self.onesf, s3 = A.alloc([128], F32, "onesf")
        self.onesb, s4 = A.alloc([128], BF16, "onesb")
        self.epsc, s5 = A.alloc([8], F32, "epsc")
        self.cslot = Slot("consts")
        for dst, src in ((self.identb, cd["identb"]), (self.identf, cd["identf"]),
                         (self.onesf, cd["onesf"]), (self.onesb, cd["onesb"]), (self.epsc, cd["epsc"])):
            self.op("sp", "dma_start", out=dst, in_=src[:, :], writes=[self.cslot], dma=DMA_MISC)
        self.modcol, self.modcol_s = A.alloc([2 * 4, 16], F32, "modcol")
        A.base = A.off

    def phase_mod(self, layers, c_d, ada_w, ada_b, norm_g, gbc_d):
        self.phase()
        for _ in self.mod_gen(layers, c_d, ada_w, ada_b, norm_g, gbc_d, (0, 1, 2, 3)):
            pass

    def mod_gen(self, layers, c_d, ada_w, ada_b, norm_g, gbc_d, banks):
        A, op, ps, pss = self.A, self.op, self.ps, self.pss
        cs = self.cslot
        bR, bB, bC, bN = banks
        cT, cT_s = A.alloc([128], F32, "cT")
        cact, cact_s = A.alloc([16], BF16, "cact")
        ngT, ngT_s = A.alloc([128], F32, "ngT")
        wb = [A.alloc([16, 512], BF16, "adaw%d" % i) for i in range(2)]
        br = [A.alloc([512], F32, "brow%d" % i) for i in range(2)]
        mr = [A.alloc([512], F32, "mrow%d" % i) for i in range(2)]
        gst = [A.alloc([512], F32, "gst%d" % i) for i in range(2)]
        tmpc, tmpc_s = A.alloc([16], F32, "tmpc")
        tmpg, tmpg_s = A.alloc([16], F32, "tmpg")
        colst, colst_s = A.alloc([16], F32, "colst")
        op("sp", "dma_start", out=cT[0:16, :], in_=c_d.rearrange("o (k p) -> (o k) p", p=128), writes=[cT_s], dma=DMA_MISC)
        op("pe", "matmul", ps[bC][:, 0:16], cT[0:16, :], self.identf[0:16, 0:16], start=True, stop=True,
           reads=[cT_s, cs], writes=[pss[bC]])
        op("act", "activation", cact, ps[bC][:, 0:16], AF.Silu, reads=[pss[bC]], writes=[cact_s])
        steps = [(i, n) for i in layers for n in range(24)]

        def issue_loads(idx):
            i_, n_ = steps[idx]
            awv_ = ada_w[i_].rearrange("(k p) n -> p k n", p=128)
            w_, w_s_ = wb[idx % 2]
            b_, b_s_ = br[idx % 2]
            op("pool", "dma_start", out=w_, in_=awv_[:, :, n_ * 512:(n_ + 1) * 512], writes=[w_s_], dma=DMA_W)
            op("sp", "dma_start", out=b_[0:1, :], in_=ada_b[i_:i_ + 1, n_ * 512:(n_ + 1) * 512], writes=[b_s_], dma=DMA_MISC)

        issue_loads(0)
        it = 0
        for i in layers:
            for n in range(24):
                w, w_s = wb[it % 2]
                b, b_s = br[it % 2]
                m, m_s = mr[it % 2]
                g, g_s = gst[it % 2]
                it += 1
                if it < len(steps):
                    issue_loads(it)
                for k in range(16):
                    op("pe", "matmul", ps[bR][0:1, :], cact[:, k:k + 1], w[:, k, :], start=(k == 0), stop=(k == 15),
                       reads=[cact_s, w_s], writes=[pss[bR]])
                op("dve", "tensor_tensor", m[0:1, :], ps[bR][0:1, :], b[0:1, :], ALU.add, reads=[pss[bR], b_s], writes=[m_s])
                seg, q = n // 4, n % 4
                if seg in (2, 5):
                    gi = 0 if seg == 2 else 1
                    op("pe", "matmul", ps[bB][:, :], self.onesf[0:1, 0:128], m[0:1, :], start=True, stop=True,
                       reads=[m_s, cs], writes=[pss[bB]])
                    op("act", "copy", g, ps[bB][:, :], reads=[pss[bB]], writes=[g_s])
                    op("sp", "dma_start", out=gbc_d[i, gi, :, q * 512:(q + 1) * 512], in_=g, reads=[g_s],
                       writes=[self.dslot(("gbc", i, gi))], dma=DMA_ST)
                else:
                    si = {0: 0, 1: 1, 3: 2, 4: 3}[seg]
                    for jj in range(4):
                        cix = q * 4 + jj
                        op("pe", "matmul", ps[bC][:, 32 + cix:32 + cix + 1], m[0:1, jj * 128:(jj + 1) * 128], self.onesf[0:1, 0:1],
                           start=True, stop=True, reads=[m_s, cs], writes=[pss[bC]])
                    op("dve", "tensor_copy", colst[:, q * 4:(q + 1) * 4], ps[bC][:, 32 + q * 4:32 + q * 4 + 4], reads=[pss[bC]], writes=[colst_s])
                    if q == 3:
                        dst = self.modcol[:, i * 4 + si, :]
                        if si in (0, 2):
                            op("dve", "tensor_copy", dst, colst, reads=[colst_s], writes=[self.modcol_s])
                        else:
                            s_ = 0 if si == 1 else 1
                            op("sp", "dma_start", out=ngT[0:16, :], in_=norm_g[i, s_:s_ + 1, :].rearrange("o (k p) -> (o k) p", p=128),
                               writes=[ngT_s], dma=DMA_MISC)
                            op("dve", "tensor_scalar", tmpc, colst, 1.0, None, ALU.add, reads=[colst_s], writes=[tmpc_s])
                            op("pe", "matmul", ps[bN][:, 64:80], ngT[0:16, :], self.identf[0:16, 0:16], start=True, stop=True,
                               reads=[ngT_s, cs], writes=[pss[bN]])
                            op("dve", "tensor_copy", tmpg, ps[bN][:, 64:80], reads=[pss[bN]], writes=[tmpg_s])
                            op("dve", "tensor_tensor", dst, tmpc, tmpg, ALU.mult, reads=[tmpc_s, tmpg_s],
                               writes=[self.modcol_s])
                yield

    def make_hT(self, x_src, x_key, tok0, ntiles, hT, hT_s, Scol, Gcol, bufs):
        op, ps, pss = self.op, self.ps, self.pss
        cs = self.cslot
        for tt in range(ntiles):
            xt, xt_s = bufs["xt"][tt % 2]
            xn, xn_s = bufs["xn"][tt % 2]
            sq, sq_s = bufs["sq"] if bufs.get("sq") is not None else (xn, xn_s)
            ss, ss_s = bufs["ss"][tt % 2]
            r0 = tok0 + tt * 128
            op("sp", "dma_start", out=xt, in_=x_src[r0:r0 + 128, :], reads=[self.dslot(x_key)], writes=[xt_s], dma=DMA_LD)
            op("dve", "memset", ss, 0.0, writes=[ss_s])
            op("act", "activation", sq, xt, AF.Square, accum_out=ss[:, 0:1], reads=[xt_s], writes=[sq_s, ss_s])
            op("act", "activation", ss[:, 1:2], ss[:, 0:1], AF.Sqrt, bias=self.epsc[:, 0:1], scale=1.0 / D,
               reads=[ss_s, cs], writes=[ss_s])
            op("dve", "reciprocal", ss[:, 2:3], ss[:, 1:2], reads=[ss_s], writes=[ss_s])
            op("dve", "tensor_scalar", xn, xt, ss[:, 2:3], None, ALU.mult, reads=[xt_s, ss_s], writes=[xn_s])
            for half in range(2):
                bk = 6 + half
                pv = ps[bk][:].bitcast(BF16)
                for j in range(8):
                    k = half * 8 + j
                    op("pe", "transpose", pv[:, j * 128:(j + 1) * 128], xn[:, k * 128:(k + 1) * 128], self.identb,
                       reads=[xn_s, cs], writes=[pss[bk]])
                for j in range(8):
                    k = half * 8 + j
                    hs_ = hT_s[tt // 4] if isinstance(hT_s, list) else hT_s
                    if j % 2 == 0:
                        op("act", "activation", hT[:, k, tt * 128:(tt + 1) * 128], pv[:, j * 128:(j + 1) * 128], AF.Identity,
                           scale=Gcol[:, k:k + 1], bias=Scol[:, k:k + 1], reads=[pss[bk], self.modcol_s], writes=[hs_])
                    else:
                        op("dve", "tensor_scalar", hT[:, k, tt * 128:(tt + 1) * 128], pv[:, j * 128:(j + 1) * 128],
                           Gcol[:, k:k + 1], Scol[:, k:k + 1], ALU.mult, ALU.add, reads=[pss[bk], self.modcol_s], writes=[hs_])

    def hT_bufs(self):
        A = self.A
        return {
            "xt": [A.alloc([2048], F32, "xt%d" % i) for i in range(2)],
            "xn": [A.alloc([2048], BF16, "xn%d" % i) for i in range(2)],
            "sq": A.alloc([2048], BF16, "sq"),
            "ss": [A.alloc([4], F32, "ss%d" % i) for i in range(2)],
        }

    def phase_inproj(self, layer, x_src, x_key, W, NZ, zT_d, side=None, side_first=0):
        self.phase()
        A, op, ps, pss = self.A, self.op, self.ps, self.pss
        hT, hT_s = A.alloc([16, T], BF16, "hT")
        bufs = self.hT_bufs()
        wb = [A.alloc([16, 512], BF16, "w_in%d" % i) for i in range(2)]
        zo = [A.alloc([T], F32, "zo%d" % i) for i in range(2)]
        gen = None
        if side is not None:
            gen = side()
            for _ in range(side_first):
                next(gen)
        Scol = self.modcol[:, layer * 4 + 0, :]
        Gcol = self.modcol[:, layer * 4 + 1, :]
        Wv = W.rearrange("(k p) n -> p k n", p=128)
        ngrp = (NZ + 511) // 512
        op("pool", "dma_start", out=wb[0][0][:, :, 0:min(512, NZ)], in_=Wv[:, :, 0:min(512, NZ)], writes=[wb[0][1]], dma=DMA_W)
        hT_ss = [Slot("hT_tq%d" % i) for i in range(4)]
        self.make_hT(x_src, x_key, 0, 16, hT, hT_ss, Scol, Gcol, bufs)
        ci = 0
        for n in range(ngrp):
            ncols = min(512, NZ - n * 512)
            w, w_s = wb[n % 2]
            if n + 1 < ngrp:
                nc1 = min(512, NZ - (n + 1) * 512)
                op("pool", "dma_start", out=wb[(n + 1) % 2][0][:, :, 0:nc1], in_=Wv[:, :, (n + 1) * 512:(n + 1) * 512 + nc1],
                   writes=[wb[(n + 1) % 2][1]], dma=DMA_W)
            for j in range((ncols + 127) // 128):
                m = min(128, ncols - j * 128)
                z, z_s = zo[ci % 2]
                if gen is not None:
                    try:
                        next(gen)
                    except StopIteration:
                        gen = None
                for tq in range(4):
                    bk = (ci % 2) * 2 + (tq % 2)
                    for k in range(16):
                        op("pe", "matmul", ps[bk][0:m, :], w[:, k, j * 128:j * 128 + m], hT[:, k, tq * 512:(tq + 1) * 512],
                           start=(k == 0), stop=(k == 15), reads=[w_s, hT_ss[tq]], writes=[pss[bk]])
                    if tq % 2 == 0:
                        op("act", "copy", z[0:m, tq * 512:(tq + 1) * 512], ps[bk][0:m, :], reads=[pss[bk]], writes=[z_s])
                    else:
                        op("dve", "tensor_copy", z[0:m, tq * 512:(tq + 1) * 512], ps[bk][0:m, :], reads=[pss[bk]], writes=[z_s])
                row0 = n * 512 + j * 128
                op("sp", "dma_start", out=zT_d[row0:row0 + m, :], in_=z[0:m, :], reads=[z_s],
                   writes=[self.dslot(("zT", row0 // 128))], dma=DMA_ST)
                ci += 1
        if gen is not None:
            for _ in gen:
                pass

    def phase_outproj(self, mixT_d, W, x_src, x_key, gb_src, gb_key, x_dst, dst_key):
        self.phase()
        A, op, ps, pss = self.A, self.op, self.ps, self.pss
        mixT, mixT_s = A.alloc([16, T], BF16, "mixT")
        Wt, Wt_s = A.alloc([16, D], BF16, "Wout")
        gb, gb_s = A.alloc([D], F32, "gb")
        xt = [A.alloc([D], F32, "xt%d" % i) for i in range(2)]
        xo = [A.alloc([D], F32, "xo%d" % i) for i in range(2)]
        Wv = W.rearrange("(k p) n -> p k n", p=128)
        Wt_ss = [Slot("Wt%d" % i) for i in range(8)]
        mx_ss = [Slot("mx%d" % i) for i in range(16)]
        for q in range(8):
            op("pool", "dma_start", out=Wt[:, q * 2:(q + 1) * 2, :], in_=Wv[:, q * 2:(q + 1) * 2, :], writes=[Wt_ss[q]], dma=DMA_W)
        for k in range(16):
            op("sp", "dma_start", out=mixT[:, k, :], in_=mixT_d[k * 128:(k + 1) * 128, :], reads=[self.dslot(("mixT", k))],
               writes=[mx_ss[k]], dma=DMA_LD)
        op("sp", "dma_start", out=gb, in_=gb_src, reads=[self.dslot(gb_key)], writes=[gb_s], dma=DMA_MISC)
        it = 0
        for tt in range(16):
            x, x_s = xt[tt % 2]
            o, o_s = xo[tt % 2]
            op("sp", "dma_start", out=x, in_=x_src[tt * 128:(tt + 1) * 128, :], reads=[self.dslot(x_key)], writes=[x_s], dma=DMA_LD)
            for cb in range(4):
                bk = it % 4
                it += 1
                for k in range(16):
                    op("pe", "matmul", ps[bk][:, :], mixT[:, k, tt * 128:(tt + 1) * 128], Wt[:, k, cb * 512:(cb + 1) * 512],
                       start=(k == 0), stop=(k == 15), reads=[mx_ss[k], Wt_ss[k // 2]], writes=[pss[bk]])
                cs_ = slice(cb * 512, (cb + 1) * 512)
                op("dve", "tensor_tensor", o[:, cs_], ps[bk][:, :], gb[:, cs_], ALU.mult, reads=[pss[bk], gb_s], writes=[o_s])
                op("pool", "tensor_tensor", o[:, cs_], o[:, cs_], x[:, cs_], ALU.add, reads=[o_s, x_s], writes=[o_s])
            op("sp", "dma_start", out=x_dst[tt * 128:(tt + 1) * 128, :], in_=o, reads=[o_s], writes=[self.dslot(dst_key)], dma=DMA_ST)

    def rows_to_cols(self, rows_aps, ncols, dst, dst_s, bank):
        A, op, ps, pss = self.A, self.op, self.ps, self.pss
        R = len(rows_aps)
        nch = ncols // 128
        step = 2048
        done = 0
        for c0 in range(0, ncols, step):
            cw_ = min(step, ncols - c0)
            rb, rb_s = A.alloc([step], F32, "r2c")
            for r, ap in enumerate(rows_aps):
                op("sp", "dma_start", out=rb[r:r + 1, 0:cw_], in_=ap[:, c0:c0 + cw_], writes=[rb_s], dma=DMA_MISC)
            for j in range(cw_ // 128):
                ch = c0 // 128 + j
                op("pe", "matmul", ps[bank][:, ch * R:(ch + 1) * R], rb[0:R, j * 128:(j + 1) * 128], self.identf[0:R, 0:R],
                   start=True, stop=True, reads=[rb_s, self.cslot], writes=[pss[bank]])
        op("dve", "tensor_copy", dst.rearrange("p a b -> p (a b)"), ps[bank][:, 0:nch * R], reads=[pss[bank]], writes=[dst_s])

    def phase_ffn(self, layer, x_src, x_key, x_dst, dst_key, w_up, conv_w, conv_b, w_down, gb_src, gb_key):
        self.phase()
        A, op, ps, pss = self.A, self.op, self.ps, self.pss
        NJ = DFF // 128
        TB = 1024
        NBLK = T // TB
        cw, cw_s = A.alloc([2 * NJ, 4], F32, "cw")
        rows = [conv_w[0:1, :], conv_w[1:2, :], conv_w[2:3, :], conv_b]
        mark = A.off
        self.rows_to_cols(rows, 2 * DFF, cw, cw_s, 0)
        self.P.barrier()
        A.off = mark
        halo, halo_s = A.alloc([2 * NJ, 2], F32, "halo")
        op("dve", "memset", halo, 0.0, writes=[halo_s])
        gT, gT_s = A.alloc([NJ, TB], BF16, "gT")
        mark0 = A.off
        Scol = self.modcol[:, layer * 4 + 2, :]
        Gcol = self.modcol[:, layer * 4 + 3, :]
        Wu = w_up.rearrange("(k p) n -> p k n", p=128)
        Wd = w_down.rearrange("(j p) n -> p j n", p=128)
        NG = NJ // 2
        NPC = NJ // 4
        NWD = 4 * NPC
        for blk in range(NBLK):
            self.P.barrier()
            A.off = mark0
            hTb, hTb_s = A.alloc([16, TB], BF16, "hTb")
            bufs = {"xt": [A.alloc([2048], F32, "xt%d" % i) for i in range(2)], "xn": [A.alloc([2048], BF16, "xn%d" % i) for i in range(2)],
                    "sq": None, "ss": [A.alloc([4], F32, "ss%d" % i) for i in range(2)]}
            wup = [A.alloc([16, 2, 256], BF16, "wup%d" % i) for i in range(2)]
            zA, zA_s = A.alloc([TB + 2], F32, "za")
            zB, zB_s = A.alloc([TB + 2], F32, "zb")
            aA, aA_s = A.alloc([TB], F32, "aa")
            aB, aB_s = A.alloc([TB], F32, "ab")
            sA, sA_s = aA, aA_s

            def load_wup(g_):
                w, w_s = wup[g_ % 2]
                op("pool", "dma_start", out=w[:, :, 0, :], in_=Wu[:, :, g_ * 256:(g_ + 1) * 256], writes=[w_s], dma=DMA_W)
                op("pool", "dma_start", out=w[:, :, 1, :], in_=Wu[:, :, DFF + g_ * 256:DFF + (g_ + 1) * 256], writes=[w_s], dma=DMA_W)

            load_wup(0)
            hTb_ss = [Slot("hTb_tq%d" % i) for i in range(TB // 512)]
            self.make_hT(x_src, x_key, blk * TB, TB // 128, hTb, hTb_ss, Scol, Gcol, bufs)
            for j in range(NJ):
                g_ = j // 2
                if j % 2 == 0 and g_ + 1 < NG:
                    load_wup(g_ + 1)
                w, w_s = wup[g_ % 2]
                jo = (j % 2) * 128
                b0 = (j % 2) * 4
                for ab_ in range(2):
                    for tq in range(2):
                        bk = b0 + ab_ * 2 + tq
                        for k in range(16):
                            op("pe", "matmul", ps[bk][:, :], w[:, k, ab_, jo:jo + 128], hTb[:, k, tq * 512:(tq + 1) * 512],
                               start=(k == 0), stop=(k == 15), reads=[w_s, hTb_ss[tq]], writes=[pss[bk]])
                for (z, z_s, ab_, cj, acc, acc_s) in ((zA, zA_s, 0, j, aA, aA_s), (zB, zB_s, 1, NJ + j, aB, aB_s)):
                    op("dve", "tensor_copy", z[:, 0:2], halo[:, cj, :], reads=[halo_s], writes=[z_s])
                    for tq in range(2):
                        bk = b0 + ab_ * 2 + tq
                        op("act", "copy", z[:, 2 + tq * 512:2 + (tq + 1) * 512], ps[bk][:, :], reads=[pss[bk]], writes=[z_s])
                    op("dve", "tensor_copy", halo[:, cj, :], z[:, TB:TB + 2], reads=[z_s], writes=[halo_s])
                    op("dve", "tensor_scalar", acc, z[:, 2:TB + 2], cw[:, cj, 2:3], cw[:, cj, 3:4], ALU.mult, ALU.add,
                       reads=[z_s, cw_s], writes=[acc_s])
                    op("dve", "scalar_tensor_tensor", acc, z[:, 1:TB + 1], cw[:, cj, 1:2], acc, ALU.mult, ALU.add,
                       reads=[z_s, cw_s, acc_s], writes=[acc_s])
                    op("dve", "scalar_tensor_tensor", acc, z[:, 0:TB], cw[:, cj, 0:1], acc, ALU.mult, ALU.add,
                       reads=[z_s, cw_s, acc_s], writes=[acc_s])
                op("act", "activation", sA, aA, AF.Silu, reads=[aA_s], writes=[sA_s])
                op("dve", "tensor_tensor", gT[:, j, :], sA, aB, ALU.mult, reads=[sA_s, aB_s], writes=[gT_s])
            self.P.barrier()
            A.off = mark0
            gb, gb_s = A.alloc([D], F32, "gb")
            op("sp", "dma_start", out=gb, in_=gb_src, reads=[self.dslot(gb_key)], writes=[gb_s], dma=DMA_MISC)
            wd = [A.alloc([4, 512], BF16, "wd%d" % i) for i in range(4)]
            xt = [A.alloc([512], F32, "fx%d" % i) for i in range(3)]
            xo = [A.alloc([512], F32, "fo%d" % i) for i in range(3)]

            def load_wd(idx):
                d_, d_s = wd[idx % 4]
                cb_, pc_ = idx // NPC, idx % NPC
                op("pool", "dma_start", out=d_, in_=Wd[:, pc_ * 4:(pc_ + 1) * 4, cb_ * 512:(cb_ + 1) * 512], writes=[d_s], dma=DMA_W)

            for q_ in range(3):
                load_wd(q_)
            ei = 0
            NTT = TB // 128
            for cb in range(4):
                for pc in range(NPC):
                    idx = cb * NPC + pc
                    d_, d_s = wd[idx % 4]
                    if idx + 3 < NWD:
                        load_wd(idx + 3)
                    for tt in range(NTT):
                        for jj in range(4):
                            j = pc * 4 + jj
                            op("pe", "matmul", ps[tt][:, :], gT[:, j, tt * 128:(tt + 1) * 128], d_[:, jj, :],
                               start=(j == 0), stop=(j == NJ - 1), reads=[gT_s, d_s], writes=[pss[tt]])
                cs_ = slice(cb * 512, (cb + 1) * 512)
                for tt in range(NTT):
                    x, x_s = xt[ei % 3]
                    o, o_s = xo[ei % 3]
                    ei += 1
                    r0 = blk * TB + tt * 128
                    op("sp", "dma_start", out=x, in_=x_src[r0:r0 + 128, cs_], reads=[self.dslot(x_key)], writes=[x_s], dma=DMA_LD2)
                    op("dve", "tensor_tensor", o, ps[tt][:, :], gb[:, cs_], ALU.mult, reads=[pss[tt], gb_s], writes=[o_s])
                    op("dve", "tensor_tensor", o, o, x, ALU.add, reads=[o_s, x_s], writes=[o_s])
                    op("sp", "dma_start", out=x_dst[r0:r0 + 128, cs_], in_=o, reads=[o_s], writes=[self.dslot(dst_key)], dma=DMA_ST)

    def sin_of(self, dst, ang, ang_s, dst_s, tmp, tmp_s, shift, tmpi, tmpi_s, tmpf, tmpf_s):
        op = self.op
        op("dve", "tensor_scalar", tmp, ang, float(1.0 / (2 * np.pi)), float(shift / (2 * np.pi)), ALU.mult, ALU.add,
           reads=[ang_s], writes=[tmp_s])
        op("dve", "tensor_copy", tmpi, tmp, reads=[tmp_s], writes=[tmpi_s])
        op("dve", "tensor_copy", tmpf, tmpi, reads=[tmpi_s], writes=[tmpf_s])
        op("dve", "tensor_tensor", tmp, tmp, tmpf, ALU.subtract, reads=[tmp_s, tmpf_s], writes=[tmp_s])
        op("act", "activation", dst, tmp, AF.Sin, scale=6.28318, reads=[tmp_s], writes=[dst_s])

    def phase_rope(self, pos_d, cs_d, C):
        self.phase()
        A, op = self.A, self.op
        posb, posb_s = A.alloc([T], I32, "posb")
        ang, ang_s = A.alloc([T], F32, "ang")
        tmp, tmp_s = A.alloc([T], F32, "tmp")
        res = [A.alloc([T], F32, "res%d" % i) for i in range(2)]
        tmpi, tmpi_s = A.alloc([T], I32, "tmpi")
        tmpf, tmpf_s = A.alloc([T], F32, "tmpf")
        invc, invc_s = A.alloc([1], F32, "invc")
        op("sp", "dma_start", out=invc, in_=C["invc"][:, :], writes=[invc_s], dma=DMA_MISC)
        op("sp", "dma_start", out=posb, in_=pos_d[0:1, :].partition_broadcast(128), writes=[posb_s], dma=DMA_MISC)
        op("dve", "tensor_copy", ang, posb, reads=[posb_s], writes=[ang_s])
        op("dve", "tensor_scalar", ang, ang, invc[:, 0:1], None, ALU.mult, reads=[ang_s, invc_s], writes=[ang_s])
        for i, shift in enumerate((np.pi / 2, 0.0)):
            r, r_s = res[i]
            self.sin_of(r, ang, ang_s, r_s, tmp, tmp_s, shift, tmpi, tmpi_s, tmpf, tmpf_s)
            op("sp", "dma_start", out=cs_d[i, :, :], in_=r, reads=[r_s], writes=[self.dslot(("cs", i))], dma=DMA_ST)

    def phase_qk(self, zT_d, cs_d, jobs, C):
        self.phase()
        A, op, ps, pss = self.A, self.op, self.ps, self.pss
        cs = self.cslot
        cosT, cos_s = A.alloc([T], F32, "cosT")
        sinT, sin_s = A.alloc([T], F32, "sinT")
        rotP, rotP_s = A.alloc([128], F32, "rotP")
        op("sp", "dma_start", out=cosT, in_=cs_d[0, :, :], reads=[self.dslot(("cs", 0))], writes=[cos_s], dma=DMA_MISC)
        op("sp", "dma_start", out=sinT, in_=cs_d[1, :, :], reads=[self.dslot(("cs", 1))], writes=[sin_s], dma=DMA_MISC)
        op("sp", "dma_start", out=rotP, in_=C["rotPT"][:, :], writes=[rotP_s], dma=DMA_MISC)
        xb = [A.alloc([T], F32, "qx%d" % i) for i in range(2)]
        sq = [A.alloc([T], F32, "qsq%d" % i) for i in range(2)]
        rs = [A.alloc([T], F32, "qrs%d" % i) for i in range(2)]
        xn = [A.alloc([T], F32, "qxn%d" % i) for i in range(2)]
        t1 = [A.alloc([T], F32, "qt1%d" % i) for i in range(2)]
        t2 = [A.alloc([T], F32, "qt2%d" % i) for i in range(2)]
        ob = [A.alloc([T], BF16, "qo%d" % i) for i in range(2)]
        gc = [A.alloc([2], F32, "qg%d" % i) for i in range(2)]

        def stage_a(ji):
            zc, gain_ap, scale, dst, dst_key = jobs[ji]
            x, x_s = xb[ji % 2]
            s2, s2_s = sq[ji % 2]
            g, g_s = gc[ji % 2]
            r_, r_s = rs[ji % 2]
            n_, n_s = xn[ji % 2]
            b_, b_s = t2[ji % 2]
            op("sp", "dma_start", out=x, in_=zT_d[zc * 128:(zc + 1) * 128, :], reads=[self.dslot(("zT", zc))], writes=[x_s], dma=DMA_LD)
            op("sp", "dma_start", out=g[:, 0:1], in_=gain_ap.rearrange("o d -> d o"), writes=[g_s], dma=DMA_MISC)
            op("dve", "tensor_scalar", g[:, 1:2], g[:, 0:1], float(scale), None, ALU.mult, reads=[g_s], writes=[g_s])
            op("act", "activation", s2, x, AF.Square, reads=[x_s], writes=[s2_s])
            for tq in range(4):
                sl = slice(tq * 512, (tq + 1) * 512)
                op("pe", "matmul", ps[tq][:, :], self.onesf, s2[:, sl], start=True, stop=True, reads=[cs, s2_s], writes=[pss[tq]])
            for tq in range(4):
                sl = slice(tq * 512, (tq + 1) * 512)
                op("act", "activation", r_[:, sl], ps[tq][:, :], AF.Sqrt, bias=self.epsc[:, 0:1], scale=1.0 / 128, reads=[pss[tq], cs], writes=[r_s])
            op("dve", "reciprocal", r_, r_, reads=[r_s], writes=[r_s])
            op("dve", "scalar_tensor_tensor", n_, x, g[:, 1:2], r_, ALU.mult, ALU.mult, reads=[x_s, g_s, r_s], writes=[n_s])
            for tq in range(4):
                sl = slice(tq * 512, (tq + 1) * 512)
                op("pe", "matmul", ps[4 + tq][:, :], rotP, n_[:, sl], start=True, stop=True, reads=[rotP_s, n_s], writes=[pss[4 + tq]])
            for tq in range(4):
                sl = slice(tq * 512, (tq + 1) * 512)
                op("dve", "tensor_tensor", b_[:, sl], ps[4 + tq][:, :], sinT[:, sl], ALU.mult, reads=[pss[4 + tq], sin_s], writes=[b_s])

        def stage_b(ji):
            zc, gain_ap, scale, dst, dst_key = jobs[ji]
            n_, n_s = xn[ji % 2]
            a_, a_s = t1[ji % 2]
            b_, b_s = t2[ji % 2]
            o, o_s = ob[ji % 2]
            op("pool", "tensor_tensor", a_, n_, cosT, ALU.mult, reads=[n_s, cos_s], writes=[a_s])
            op("pool", "tensor_tensor", o, a_, b_, ALU.add, reads=[a_s, b_s], writes=[o_s])
            op("sp", "dma_start", out=dst, in_=o, reads=[o_s], writes=[self.dslot(dst_key)], dma=DMA_ST)

        nj = len(jobs)
        stage_a(0)
        for ji in range(nj):
            if ji + 1 < nj:
                stage_a(ji + 1)
            stage_b(ji)

    def phase_vprep(self, zT_d, jobs):
        self.phase()
        A, op, ps, pss = self.A, self.op, self.ps, self.pss
        xb = [A.alloc([T], F32, "vx%d" % i) for i in range(2)]
        xh = [A.alloc([T], BF16, "vh%d" % i) for i in range(2)]
        vo = [A.alloc([T], BF16, "vo%d" % i) for i in range(2)]
        for ji, (zc, dst, dst_key) in enumerate(jobs):
            x, x_s = xb[ji % 2]
            h, h_s = xh[ji % 2]
            o, o_s = vo[ji % 2]
            op("sp", "dma_start", out=x, in_=zT_d[zc * 128:(zc + 1) * 128, :], reads=[self.dslot(("zT", zc))], writes=[x_s], dma=DMA_LD)
            op("act", "copy", h, x, reads=[x_s], writes=[h_s])
            for half in range(2):
                bk = (ji * 2 + half) % 4
                pv = ps[bk][:].bitcast(BF16)
                for j in range(8):
                    kt = half * 8 + j
                    op("pe", "transpose", pv[:, j * 128:(j + 1) * 128], h[:, kt * 128:(kt + 1) * 128], self.identb,
                       reads=[h_s, self.cslot], writes=[pss[bk]])
                op("dve", "tensor_copy", o[:, half * 1024:(half + 1) * 1024], pv[:, 0:1024], reads=[pss[bk]], writes=[o_s])
            op("sp", "dma_start", out=dst, in_=o, reads=[o_s], writes=[self.dslot(dst_key)], dma=DMA_ST)

    def ln_stats_acc(self, y, y_s, sqb, sqb_s, c, nchunks):
        op, ps, pss = self.op, self.ps, self.pss
        op("act", "activation", sqb, y, AF.Square, reads=[y_s], writes=[sqb_s])
        for tq in range(4):
            sl = slice(tq * 512, (tq + 1) * 512)
            op("pe", "matmul", ps[tq][:, :], self.onesf, y[:, sl], start=(c == 0), stop=(c == nchunks - 1),
               reads=[self.cslot, y_s], writes=[pss[tq]])
            op("pe", "matmul", ps[4 + tq][:, :], self.onesf, sqb[:, sl], start=(c == 0), stop=(c == nchunks - 1),
               reads=[self.cslot, sqb_s], writes=[pss[4 + tq]])

    def ln_finish(self, mean, mean_s, rstd, rstd_s, nfeat, src=None):
        op, ps, pss = self.op, self.ps, self.pss
        for tq in range(4):
            sl = slice(tq * 512, (tq + 1) * 512)
            if src is None:
                a1_, a1s, a2_, a2s = ps[tq][:, :], pss[tq], ps[4 + tq][:, :], pss[4 + tq]
            else:
                a1_, a1s, a2_, a2s = src[0][:, sl], src[1], src[2][:, sl], src[3]
            op("act", "activation", mean[:, sl], a1_, AF.Copy, scale=1.0 / nfeat, reads=[a1s], writes=[mean_s])
            op("dve", "tensor_tensor", rstd[:, sl], mean[:, sl], mean[:, sl], ALU.mult, reads=[mean_s], writes=[rstd_s])
            op("dve", "scalar_tensor_tensor", rstd[:, sl], a2_, 1.0 / nfeat, rstd[:, sl], ALU.mult, ALU.subtract,
               reads=[a2s, rstd_s], writes=[rstd_s])
            op("act", "activation", rstd[:, sl], rstd[:, sl], AF.Sqrt, bias=self.epsc[:, 0:1], scale=1.0, reads=[rstd_s, self.cslot], writes=[rstd_s])
            op("dve", "reciprocal", rstd[:, sl], rstd[:, sl], reads=[rstd_s], writes=[rstd_s])

    def phase_conformer(self, zT_d, mixT_d, conv_w, conv_b, ln_g, ln_b):
        self.phase()
        A, op, ps, pss = self.A, self.op, self.ps, self.pss
        NC = 8
        cw, cw_s = A.alloc([NC, 34], F32, "ccw")
        rows = [conv_w[k:k + 1, :] for k in range(31)] + [conv_b, ln_g, ln_b]
        mark = A.off
        self.rows_to_cols(rows, 1024, cw, cw_s, 0)
        self.P.barrier()
        A.off = mark
        ys = [A.alloc([T], F32, "cy%d" % c) for c in range(NC)]
        a1 = [A.alloc([T], F32, "ca1%d" % i) for i in range(2)]
        a2 = [A.alloc([T], F32, "ca2%d" % i) for i in range(2)]
        gp = [A.alloc([30 + T], BF16, "cgp%d" % i) for i in range(2)]
        dg = [A.alloc([31, 128], BF16, "cdg%d" % i) for i in range(2)]
        sqb, sqb_s = A.alloc([T], F32, "csq")
        s1a, s1a_s = A.alloc([T], F32, "cs1")
        s2a, s2a_s = A.alloc([T], F32, "cs2")
        cs = self.cslot
        for i in range(2):
            op("dve", "memset", gp[i][0][:, 0:30], 0.0, writes=[gp[i][1]])
        def stage_a(c):
            x1, x1_s = a1[c % 2]
            x2, x2_s = a2[c % 2]
            g, g_s = gp[c % 2]
            d_, d_s = dg[c % 2]
            op("sp", "dma_start", out=x1, in_=zT_d[c * 128:(c + 1) * 128, :], reads=[self.dslot(("zT", c))], writes=[x1_s], dma=DMA_LD)
            op("sp", "dma_start", out=x2, in_=zT_d[(8 + c) * 128:(9 + c) * 128, :], reads=[self.dslot(("zT", 8 + c))], writes=[x2_s], dma=DMA_LD)
            op("act", "activation", x2, x2, AF.Sigmoid, reads=[x2_s], writes=[x2_s])
            op("pool", "tensor_tensor", g[:, 30:30 + T], x1, x2, ALU.mult, reads=[x1_s, x2_s], writes=[g_s])
            for k in range(31):
                op("dve", "tensor_scalar", d_[:, k, :], self.identb, cw[:, c, k:k + 1], None, ALU.mult, reads=[cs, cw_s], writes=[d_s])

        def stage_b(c):
            g, g_s = gp[c % 2]
            d_, d_s = dg[c % 2]
            y, y_s = ys[c]
            for tq in range(4):
                sl = slice(tq * 512, (tq + 1) * 512)
                for k in range(31):
                    op("pe", "matmul", ps[tq][:, :], d_[:, k, :], g[:, k + tq * 512:k + tq * 512 + 512], start=(k == 0), stop=(k == 30),
                       reads=[d_s, g_s], writes=[pss[tq]])
                op("act", "activation", y[:, sl], ps[tq][:, :], AF.Identity, bias=cw[:, c, 31:32], scale=1.0, reads=[pss[tq], cw_s], writes=[y_s])
            op("act", "activation", sqb, y, AF.Square, reads=[y_s], writes=[sqb_s])
            for tq in range(4):
                sl = slice(tq * 512, (tq + 1) * 512)
                b1, b2 = 4 + tq % 2, 6 + tq % 2
                op("pe", "matmul", ps[b1][:, :], self.onesf, y[:, sl], start=True, stop=True, reads=[cs, y_s], writes=[pss[b1]])
                op("pe", "matmul", ps[b2][:, :], self.onesf, sqb[:, sl], start=True, stop=True, reads=[cs, sqb_s], writes=[pss[b2]])
                if c == 0:
                    op("dve", "tensor_copy", s1a[:, sl], ps[b1][:, :], reads=[pss[b1]], writes=[s1a_s])
                    op("dve", "tensor_copy", s2a[:, sl], ps[b2][:, :], reads=[pss[b2]], writes=[s2a_s])
                else:
                    op("dve", "tensor_tensor", s1a[:, sl], s1a[:, sl], ps[b1][:, :], ALU.add, reads=[s1a_s, pss[b1]], writes=[s1a_s])
                    op("dve", "tensor_tensor", s2a[:, sl], s2a[:, sl], ps[b2][:, :], ALU.add, reads=[s2a_s, pss[b2]], writes=[s2a_s])

        stage_a(0)
        for c in range(NC):
            if c + 1 < NC:
                stage_a(c + 1)
            stage_b(c)
        mean, mean_s = a1[0]
        rstd, rstd_s = a1[1]
        self.ln_finish(mean, mean_s, rstd, rstd_s, 1024, src=(s1a, s1a_s, s2a, s2a_s))
        ob = [A.alloc([T], BF16, "cob%d" % i) for i in range(2)]
        for c in range(NC):
            y, y_s = ys[c]
            o, o_s = ob[c % 2]
            op("dve", "tensor_tensor", y, y, mean, ALU.subtract, reads=[y_s, mean_s], writes=[y_s])
            op("pool", "tensor_tensor", y, y, rstd, ALU.mult, reads=[y_s, rstd_s], writes=[y_s])
            op("act", "activation", o, y, AF.Silu, bias=cw[:, c, 33:34], scale=cw[:, c, 32:33], reads=[y_s, cw_s], writes=[o_s])
            op("sp", "dma_start", out=mixT_d[c * 128:(c + 1) * 128, :], in_=o, reads=[o_s], writes=[self.dslot(("mixT", c))], dma=DMA_ST)

    def phase_compress(self, zT_d, pos_d, C, prm, kcT_d, vc_d):
        self.phase()
        A, op, ps, pss = self.A, self.op, self.ps, self.pss
        cs = self.cslot
        w1s, w1_s = A.alloc([32, 256], BF16, "w1s")
        w2s, w2_s = A.alloc([2, 128], BF16, "w2s")
        pe_, pe_s = A.alloc([128], F32, "pe")
        posT, posT_s = A.alloc([32], BF16, "posT")
        brow, brow_s = A.alloc([256], BF16, "brow")
        af, af_s = A.alloc([T], F32, "caf")
        ab, ab_s = A.alloc([T + 16], BF16, "cab")
        hid, hid_s = A.alloc([256], BF16, "hid")
        hidT, hidT_s = A.alloc([2, 128], BF16, "hidT")
        kc, kc_s = A.alloc([128], F32, "kc")
        sqj, sqj_s = A.alloc([128], F32, "sqj")
        st_, st_s = A.alloc([4], F32, "kst")
        gainb, gainb_s = A.alloc([128], F32, "gainb")
        invr, invr_s = A.alloc([64], F32, "invr")
        posE, posE_s = A.alloc([16], I32, "posE")
        posF, posF_s = A.alloc([1], F32, "posF")
        angE, angE_s = A.alloc([64], F32, "angE")
        tmpE, tmpE_s = A.alloc([64], F32, "tmpE")
        tmpEi, tmpEi_s = A.alloc([64], I32, "tmpEi")
        tmpEf, tmpEf_s = A.alloc([64], F32, "tmpEf")
        cosE, cosE_s = A.alloc([64], F32, "cosE")
        sinE, sinE_s = A.alloc([64], F32, "sinE")
        kn, kn_s = A.alloc([128], F32, "kn")
        tt_ = [A.alloc([64], F32, "ktt%d" % i) for i in range(4)]
        kr, kr_s = A.alloc([128], BF16, "kr")
        ob, ob_s = A.alloc([128], BF16, "cob")
        op("dve", "memset", hid, 0.0, writes=[hid_s])
        op("dve", "memset", ab[:, T:T + 16], 0.0, writes=[ab_s])
        op("dve", "memset", posE, 0, writes=[posE_s])
        op("sp", "dma_start", out=posE[0:127, :], in_=pos_d[0:1, 16:16 + 16 * 127].rearrange("o (n s) -> (o n) s", s=16),
           writes=[posE_s], dma=DMA_MISC)
        op("sp", "dma_start", out=invr, in_=C["invrow"][:, :], writes=[invr_s], dma=DMA_MISC)
        op("sp", "dma_start", out=gainb, in_=prm["k_gain"].partition_broadcast(128), writes=[gainb_s], dma=DMA_MISC)
        op("dve", "tensor_copy", posF, posE[:, 15:16], reads=[posE_s], writes=[posF_s])
        op("dve", "tensor_scalar", angE, invr, posF[:, 0:1], None, ALU.mult, reads=[invr_s, posF_s], writes=[angE_s])
        self.sin_of(cosE, angE, angE_s, cosE_s, tmpE, tmpE_s, np.pi / 2, tmpEi, tmpEi_s, tmpEf, tmpEf_s)
        self.sin_of(sinE, angE, angE_s, sinE_s, tmpE, tmpE_s, 0.0, tmpEi, tmpEi_s, tmpEf, tmpEf_s)
        for kind in ("k", "v"):
            w1 = prm[kind + "_w1"]
            w2 = prm[kind + "_w2"]
            pos = prm[kind + "_pos"]
            op("pool", "dma_start", out=w1s, in_=w1.rearrange("(l p) h -> p l h", p=128), writes=[w1_s], dma=DMA_W)
            op("pool", "dma_start", out=w2s, in_=w2.rearrange("(c p) d -> p c d", p=128), writes=[w2_s], dma=DMA_W)
            op("sp", "dma_start", out=pe_[0:32, :], in_=pos, writes=[pe_s], dma=DMA_MISC)
            op("pe", "matmul", ps[0][:, 0:32], pe_[0:32, :], self.identf[0:32, 0:32], start=True, stop=True, reads=[pe_s, cs], writes=[pss[0]])
            op("dve", "tensor_copy", posT, ps[0][:, 0:32], reads=[pss[0]], writes=[posT_s])
            for l in range(32):
                op("pe", "matmul", ps[1][0:1, 0:256], posT[:, l:l + 1], w1s[:, l, :], start=(l == 0), stop=(l == 31),
                   reads=[posT_s, w1_s], writes=[pss[1]])
            op("dve", "tensor_copy", brow[0:1, :], ps[1][0:1, 0:256], reads=[pss[1]], writes=[brow_s])
            for g in range(2):
                zc = (24 if kind == "k" else 26) + g
                op("sp", "dma_start", out=af, in_=zT_d[zc * 128:(zc + 1) * 128, :], reads=[self.dslot(("zT", zc))], writes=[af_s], dma=DMA_LD)
                op("act", "copy", ab[:, 0:T], af, reads=[af_s], writes=[ab_s])
                for l in range(32):
                    lv = ab[:, l:l + 16 * 127].rearrange("p (n s) -> p n s", s=16)[:, :, 0]
                    op("pe", "matmul", ps[2][0:127, 0:256], lv, w1s[:, l, :], start=(l == 0), stop=False,
                       reads=[ab_s, w1_s], writes=[pss[2]])
                op("pe", "matmul", ps[2][0:127, 0:256], self.onesb[0:1, 0:127], brow[0:1, :], start=False, stop=True,
                   reads=[cs, brow_s], writes=[pss[2]])
                op("act", "activation", hid[0:127, :], ps[2][0:127, 0:256], AF.Silu, reads=[pss[2]], writes=[hid_s])
                pv = ps[3][:].bitcast(BF16)
                for c in range(2):
                    op("pe", "transpose", pv[:, c * 128:(c + 1) * 128], hid[:, c * 128:(c + 1) * 128], self.identb,
                       reads=[hid_s, cs], writes=[pss[3]])
                op("dve", "tensor_copy", hidT.rearrange("p a b -> p (a b)"), pv[:, 0:256], reads=[pss[3]], writes=[hidT_s])
                for c in range(2):
                    op("pe", "matmul", ps[4][:, 0:128], hidT[:, c, :], w2s[:, c, :], start=(c == 0), stop=(c == 1),
                       reads=[hidT_s, w2_s], writes=[pss[4]])
                if kind == "v":
                    op("act", "copy", ob, ps[4][:, 0:128], reads=[pss[4]], writes=[ob_s])
                    op("sp", "dma_start", out=vc_d[g, :, :], in_=ob, reads=[ob_s], writes=[self.dslot(("vc", g))], dma=DMA_ST)
                    continue
                op("dve", "tensor_copy", kc, ps[4][:, 0:128], reads=[pss[4]], writes=[kc_s])
                op("dve", "memset", st_, 0.0, writes=[st_s])
                op("act", "activation", sqj, kc, AF.Square, accum_out=st_[:, 0:1], reads=[kc_s], writes=[sqj_s, st_s])
                op("act", "activation", st_[:, 1:2], st_[:, 0:1], AF.Sqrt, bias=self.epsc[:, 0:1], scale=1.0 / 128, reads=[st_s, cs], writes=[st_s])
                op("dve", "reciprocal", st_[:, 2:3], st_[:, 1:2], reads=[st_s], writes=[st_s])
                op("dve", "scalar_tensor_tensor", kn, kc, st_[:, 2:3], gainb, ALU.mult, ALU.mult, reads=[kc_s, st_s, gainb_s], writes=[kn_s])
                (ta, ta_s), (tb, tb_s), (tc_, tc_s), (td, td_s) = tt_
                op("dve", "tensor_tensor", ta, kn[:, 0:64], cosE, ALU.mult, reads=[kn_s, cosE_s], writes=[ta_s])
                op("dve", "tensor_tensor", tb, kn[:, 64:128], sinE, ALU.mult, reads=[kn_s, sinE_s], writes=[tb_s])
                op("dve", "tensor_tensor", kr[:, 0:64], ta, tb, ALU.subtract, reads=[ta_s, tb_s], writes=[kr_s])
                op("dve", "tensor_tensor", tc_, kn[:, 64:128], cosE, ALU.mult, reads=[kn_s, cosE_s], writes=[tc_s])
                op("dve", "tensor_tensor", td, kn[:, 0:64], sinE, ALU.mult, reads=[kn_s, sinE_s], writes=[td_s])
                op("dve", "tensor_tensor", kr[:, 64:128], tc_, td, ALU.add, reads=[tc_s, td_s], writes=[kr_s])
                pv5 = ps[5][:].bitcast(BF16)
                op("pe", "transpose", pv5[:, 0:128], kr, self.identb, reads=[kr_s, cs], writes=[pss[5]])
                op("dve", "tensor_copy", ob, pv5[:, 0:128], reads=[pss[5]], writes=[ob_s])
                op("sp", "dma_start", out=kcT_d[g, :, :], in_=ob, reads=[ob_s], writes=[self.dslot(("kcT", g))], dma=DMA_ST)

    def attend(self, W, qv, tiles, den_extra=None):
        op, ps, pss = self.op, self.ps, self.pss
        n = len(tiles)
        base = W["si"]
        W["si"] += n
        W["pair"] = 4 + 2 * (W["ci"] % 2)
        W["ci"] += 1
        bd, bo = W["pair"], W["pair"] + 1

        def issue_scores(i):
            t = tiles[i]
            sb = (base + i) % W["ns"]
            ms = t["masks"]
            op("pe", "matmul", ps[sb][:, :], t["kT"], qv, start=True, stop=(len(ms) == 0), reads=t["kslots"] + [W["q_s"]], writes=[pss[sb]])
            for mi, (lhsT, rhs, sl) in enumerate(ms):
                op("pe", "matmul", ps[sb][:, :], lhsT, rhs, start=False, stop=(mi == len(ms) - 1), reads=sl, writes=[pss[sb]])

        issue_scores(0)
        for i, t in enumerate(tiles):
            if i + 1 < n:
                issue_scores(i + 1)
            sb = (base + i) % W["ns"]
            e, e_s = W["e"][(base + i) % 3]
            op("act", "activation", e, ps[sb][:, :], AF.Exp, reads=[pss[sb]], writes=[e_s])
            op("pe", "matmul", ps[bd][:, :], self.onesb, e, start=(i == 0), stop=(i == n - 1), reads=[self.cslot, e_s], writes=[pss[bd]])
            op("pe", "matmul", ps[bo][:, :], t["v"], e, start=(i == 0), stop=(i == n - 1), reads=t["vslots"] + [e_s], writes=[pss[bo]])

    def attn_finish(self, W, acc, acc_s, first, gate=None, den_add=None):
        op, ps, pss = self.op, self.ps, self.pss
        bd, bo = W["pair"], W["pair"] + 1
        rd, rd_s = W["rd"][W["fi"] % 2]
        tm, tm_s = W["tm"][W["fi"] % 2]
        W["fi"] += 1
        if den_add is None:
            op("dve", "tensor_scalar", rd, ps[bd][:, :], 1e-30, None, ALU.max, reads=[pss[bd]], writes=[rd_s])
        else:
            op("dve", "tensor_scalar", rd, ps[bd][:, :], den_add[0], None, ALU.add, reads=[pss[bd], den_add[1]], writes=[rd_s])
        op("dve", "reciprocal", rd, rd, reads=[rd_s], writes=[rd_s])
        if gate is None:
            op("dve", "tensor_tensor", acc, ps[bo][:, :], rd, ALU.mult, reads=[pss[bo], rd_s], writes=[acc_s])
            return
        lhsT, rhs, sl = gate
        op("pe", "matmul", ps[3][:, :], lhsT, rhs, start=True, stop=True, reads=sl, writes=[pss[3]])
        op("dve", "tensor_tensor", tm, ps[bo][:, :], rd, ALU.mult, reads=[pss[bo], rd_s], writes=[tm_s])
        if first:
            op("dve", "tensor_tensor", acc, tm, ps[3][:, :], ALU.mult, reads=[tm_s, pss[3]], writes=[acc_s])
        else:
            op("dve", "tensor_tensor", tm, tm, ps[3][:, :], ALU.mult, reads=[tm_s, pss[3]], writes=[tm_s])
            op("pool", "tensor_tensor", acc, acc, tm, ALU.add, reads=[acc_s, tm_s], writes=[acc_s])

    def attn_work(self):
        A = self.A
        return {"e": [A.alloc([512], BF16, "e%d" % i) for i in range(3)],
                "rd": [A.alloc([512], F32, "rd%d" % i) for i in range(2)],
                "tm": [A.alloc([512], F32, "tm%d" % i) for i in range(2)],
                "acc": [A.alloc([512], F32, "acc%d" % i) for i in range(2)],
                "ob": [A.alloc([512], BF16, "aob%d" % i) for i in range(2)],
                "si": 0, "mi": 0, "fi": 0, "oi": 0, "ci": 0, "pair": 4, "ns": 3}

    def phase_nsa_attn(self, zT_d, qk_d, v_d, kcT_d, vc_d, mixT_d, C, side=None):
        self.phase()
        A, op, ps, pss = self.A, self.op, self.ps, self.pss
        cs = self.cslot
        q, q_s = A.alloc([8, T], BF16, "q")
        kk, kk_s = A.alloc([4, T], BF16, "kk")
        vv, vv_s = A.alloc([4, 16, 128], BF16, "vv")
        kcT, kcT_s = A.alloc([2, 128], BF16, "kcT")
        vc, vc_s = A.alloc([2, 128], BF16, "vc")
        sg, sg_s = A.alloc([T], BF16, "sg")
        validT, k1 = A.alloc([T], BF16, "validT")
        cm, k2 = A.alloc([4, 512], BF16, "cm")
        wm, k3 = A.alloc([6, 512], BF16, "wm")
        E, k4 = A.alloc([16, 128], BF16, "E")
        SelG, k5 = A.alloc([24, 128], BF16, "SelG")
        selT, selT_s = A.alloc([2, T], BF16, "selT")
        mark = A.off
        glf, glf_s = A.alloc([T], F32, "glf")
        ovl, k6 = A.alloc([32], F32, "ovl")
        topA, k7 = A.alloc([16, 32], F32, "topA")
        topB, k8 = A.alloc([16, 32], F32, "topB")
        ks_ = Slot("nsaconst")
        for dst, name in ((validT, "validT"), (cm.rearrange("p a b -> p (a b)"), "cm"), (wm.rearrange("p a b -> p (a b)"), "wm"),
                          (E.rearrange("p a b -> p (a b)"), "E"), (SelG.rearrange("p a b -> p (a b)"), "SelG"), (ovl, "ovl"),
                          (topA.rearrange("p a b -> p (a b)"), "topA"), (topB.rearrange("p a b -> p (a b)"), "topB")):
            op("sp", "dma_start", out=dst, in_=C[name][:, :], writes=[ks_], dma=DMA_MISC)
        for h in range(8):
            op("sp", "dma_start", out=q[:, h, :], in_=qk_d[h], reads=[self.dslot(("qk", h))], writes=[q_s], dma=DMA_LD)
        for i in range(4):
            op("sp", "dma_start", out=kk[:, i, :], in_=qk_d[8 + i], reads=[self.dslot(("qk", 8 + i))], writes=[kk_s], dma=DMA_LD)
            op("sp", "dma_start", out=vv[:, i, :, :].rearrange("p a b -> p (a b)"), in_=v_d[i], reads=[self.dslot(("v", i))], writes=[vv_s], dma=DMA_LD)
        for g in range(2):
            op("sp", "dma_start", out=kcT[:, g, :], in_=kcT_d[g, :, :], reads=[self.dslot(("kcT", g))], writes=[kcT_s], dma=DMA_LD)
            op("sp", "dma_start", out=vc[:, g, :], in_=vc_d[g, :, :], reads=[self.dslot(("vc", g))], writes=[vc_s], dma=DMA_LD)
        op("sp", "dma_start", out=glf[0:24, :], in_=zT_d[36 * 128:36 * 128 + 24, :], reads=[self.dslot(("zT", 36))], writes=[glf_s], dma=DMA_LD)
        op("act", "activation", sg[0:24, :], glf[0:24, :], AF.Sigmoid, reads=[glf_s], writes=[sg_s])
        impT, impT_s = A.alloc([2, T], F32, "impT")
        e32 = [A.alloc([512], F32, "e32%d" % i) for i in range(2)]
        rdn = [A.alloc([512], F32, "rdn%d" % i) for i in range(2)]
        items = [(g, tq, r) for g in range(2) for tq in range(4) for r in range(4)]

        def s1_a(i):
            g, tq, r = items[i]
            sl = slice(tq * 512, (tq + 1) * 512)
            h = g * 4 + r
            e, e_s = e32[i % 2]
            sb = i % 2
            db = 2 + i % 2
            op("pe", "matmul", ps[sb][:, :], kcT[:, g, :], q[:, h, sl], start=True, stop=False, reads=[kcT_s, q_s], writes=[pss[sb]])
            op("pe", "matmul", ps[sb][:, :], self.identb, validT[:, sl], start=False, stop=True, reads=[cs, ks_], writes=[pss[sb]])
            op("act", "activation", e, ps[sb][:, :], AF.Exp, reads=[pss[sb]], writes=[e_s])
            op("pe", "matmul", ps[db][:, :], self.onesf, e, start=True, stop=True, reads=[cs, e_s], writes=[pss[db]])

        def s1_b(i):
            g, tq, r = items[i]
            sl = slice(tq * 512, (tq + 1) * 512)
            e, e_s = e32[i % 2]
            rd, rd_s = rdn[i % 2]
            db = 2 + i % 2
            op("dve", "tensor_scalar", rd, ps[db][:, :], 1e-30, None, ALU.max, reads=[pss[db]], writes=[rd_s])
            op("dve", "reciprocal", rd, rd, reads=[rd_s], writes=[rd_s])
            op("pool", "tensor_tensor", e, e, rd, ALU.mult, reads=[e_s, rd_s], writes=[e_s])
            op("pe", "matmul", ps[4][0:32, :], ovl, e, start=(r == 0), stop=(r == 3), reads=[ks_, e_s], writes=[pss[4]])
            if r == 3:
                op("act", "copy", impT[0:32, g, sl], ps[4][0:32, :], reads=[pss[4]], writes=[impT_s])

        s1_a(0)
        for i in range(len(items)):
            if i + 1 < len(items):
                s1_a(i + 1)
            s1_b(i)
        sc = [A.alloc([32], F32, "sc%d" % i) for i in range(2)]
        m8 = [A.alloc([8], F32, "m8%d" % i) for i in range(2)]
        titems = [(g, tt) for g in range(2) for tt in range(16)]

        def tk_a(i):
            g, tt = titems[i]
            s_, s_s = sc[i % 2]
            m_, m_s = m8[i % 2]
            b5 = 5 + i % 2
            op("pe", "matmul", ps[b5][:, 0:32], impT[0:32, g, tt * 128:(tt + 1) * 128], self.identf[0:32, 0:32], start=True, stop=True,
               reads=[impT_s, cs], writes=[pss[b5]])
            op("dve", "tensor_tensor", s_, ps[b5][:, 0:32], topA[:, tt, :], ALU.mult, reads=[pss[b5], ks_], writes=[s_s])
            op("dve", "tensor_tensor", s_, s_, topB[:, tt, :], ALU.add, reads=[s_s, ks_], writes=[s_s])
            op("dve", "max", m_, s_, reads=[s_s], writes=[m_s])

        def tk_b(i):
            g, tt = titems[i]
            s_, s_s = sc[i % 2]
            m_, m_s = m8[i % 2]
            op("dve", "tensor_scalar", s_, s_, m_[:, 7:8], None, ALU.is_ge, reads=[s_s, m_s], writes=[s_s])
            op("pe", "matmul", ps[7][0:32, 0:128], s_, self.identf, start=True, stop=True, reads=[s_s, cs], writes=[pss[7]])
            op("act", "activation", selT[0:32, g, tt * 128:(tt + 1) * 128], ps[7][0:32, 0:128], AF.Identity, scale=30000.0,
               bias=self.epsc[0:32, 3:4], reads=[pss[7], cs], writes=[selT_s])

        tk_a(0)
        for i in range(len(titems)):
            if i + 1 < len(titems):
                tk_a(i + 1)
            tk_b(i)
        self.P.barrier()
        A.off = mark
        W = self.attn_work()
        W["q_s"] = q_s
        gen = None
        if side is not None:
            W["ns"] = 2
            gen = side()
        for h in range(8):
            g = h // 4
            for tq in range(4):
                if gen is not None:
                    try:
                        next(gen)
                    except StopIteration:
                        gen = None
                sl = slice(tq * 512, (tq + 1) * 512)
                qv = q[:, h, sl]
                acc, acc_s = W["acc"][W["oi"] % 2]
                ob, ob_s = W["ob"][W["oi"] % 2]
                W["oi"] += 1
                tiles = [dict(kT=kcT[:, g, :], kslots=[kcT_s], v=vc[:, g, :], vslots=[vc_s], masks=[(self.identb, validT[:, sl], [cs, ks_])])]
                self.attend(W, qv, tiles)
                self.attn_finish(W, acc, acc_s, True, gate=(SelG[0:24, h * 3 + 0, :], sg[0:24, sl], [ks_, sg_s]))
                tiles = []
                for kt in range(4 * tq + 4):
                    masks = [(E[0:32, kt, :], selT[0:32, g, sl], [ks_, selT_s])]
                    if kt >= 4 * tq:
                        masks.append((self.identb, cm[:, kt - 4 * tq, :], [cs, ks_]))
                    tiles.append(dict(kT=kk[:, g, kt * 128:(kt + 1) * 128], kslots=[kk_s], v=vv[:, g, kt, :], vslots=[vv_s], masks=masks))
                self.attend(W, qv, tiles)
                self.attn_finish(W, acc, acc_s, False, gate=(SelG[0:24, h * 3 + 1, :], sg[0:24, sl], [ks_, sg_s]))
                tiles = []
                for kt in range(max(0, 4 * tq - 2), 4 * tq + 4):
                    tiles.append(dict(kT=kk[:, 2 + g, kt * 128:(kt + 1) * 128], kslots=[kk_s], v=vv[:, 2 + g, kt, :], vslots=[vv_s],
                                      masks=[(self.identb, wm[:, kt - 4 * tq + 2, :], [cs, ks_])]))
                self.attend(W, qv, tiles)
                self.attn_finish(W, acc, acc_s, False, gate=(SelG[0:24, h * 3 + 2, :], sg[0:24, sl], [ks_, sg_s]))
                op("act", "copy", ob, acc, reads=[acc_s], writes=[ob_s])
                r0 = 1024 + h * 128
                op("sp", "dma_start", out=mixT_d[r0:r0 + 128, sl], in_=ob, reads=[ob_s], writes=[self.dslot(("mixT", 8 + h))], dma=DMA_ST)
        if gen is not None:
            for _ in gen:
                pass

    def phase_swa_attn(self, qk_d, v_d, sinks, mixT_d, C):
        self.phase()
        A, op, ps, pss = self.A, self.op, self.ps, self.pss
        q, q_s = A.alloc([8, T], BF16, "q")
        kk, kk_s = A.alloc([2, T], BF16, "kk")
        vv, vv_s = A.alloc([2, 16, 128], BF16, "vv")
        sm, ks_ = A.alloc([5, 512], BF16, "sm")
        es, es_s = A.alloc([8], F32, "es")
        op("sp", "dma_start", out=sm.rearrange("p a b -> p (a b)"), in_=C["sm"][:, :], writes=[ks_], dma=DMA_MISC)
        op("sp", "dma_start", out=es, in_=sinks.partition_broadcast(128), writes=[es_s], dma=DMA_MISC)
        op("act", "activation", es, es, AF.Exp, reads=[es_s], writes=[es_s])
        for h in range(8):
            op("sp", "dma_start", out=q[:, h, :], in_=qk_d[h], reads=[self.dslot(("qk", h))], writes=[q_s], dma=DMA_LD)
        for i in range(2):
            op("sp", "dma_start", out=kk[:, i, :], in_=qk_d[8 + i], reads=[self.dslot(("qk", 8 + i))], writes=[kk_s], dma=DMA_LD)
            op("sp", "dma_start", out=vv[:, i, :, :].rearrange("p a b -> p (a b)"), in_=v_d[i], reads=[self.dslot(("v", i))], writes=[vv_s], dma=DMA_LD)
        W = self.attn_work()
        W["q_s"] = q_s
        for h in range(8):
            g = h // 4
            for tq in range(4):
                sl = slice(tq * 512, (tq + 1) * 512)
                acc, acc_s = W["acc"][W["oi"] % 2]
                ob, ob_s = W["ob"][W["oi"] % 2]
                W["oi"] += 1
                tiles = []
                for kt in range(max(0, 4 * tq - 1), 4 * tq + 4):
                    tiles.append(dict(kT=kk[:, g, kt * 128:(kt + 1) * 128], kslots=[kk_s], v=vv[:, g, kt, :], vslots=[vv_s],
                                      masks=[(self.identb, sm[:, kt - 4 * tq + 1, :], [self.cslot, ks_])]))
                self.attend(W, q[:, h, sl], tiles)
                self.attn_finish(W, acc, acc_s, True, gate=None, den_add=(es[:, h:h + 1], es_s))
                op("act", "copy", ob, acc, reads=[acc_s], writes=[ob_s])
                op("sp", "dma_start", out=mixT_d[h * 128:(h + 1) * 128, sl], in_=ob, reads=[ob_s], writes=[self.dslot(("mixT", h))], dma=DMA_ST)

    def gelu_tanh(self, out, out_s, x, x_s, t, t_s):
        self.op("act", "activation", out, x, AF.Gelu_apprx_tanh, reads=[x_s], writes=[out_s])

    def phase_gmlp(self, zT_d, mixT_d, ln_g, ln_b, w_s, b_s, C):
        self.phase()
        A, op, ps, pss = self.A, self.op, self.ps, self.pss
        cs = self.cslot
        NC = 8
        cw, cw_s = A.alloc([NC, 2], F32, "gcw")
        mark = A.off
        self.rows_to_cols([ln_g, ln_b], 1024, cw, cw_s, 0)
        self.P.barrier()
        A.off = mark
        wT, wT_s = A.alloc([NC, 128], BF16, "wT")
        tril, tril_s = A.alloc([128], F32, "tril")
        wl = [A.alloc([128], F32, "wl%d" % i) for i in range(2)]
        bsb, bsb_s = A.alloc([NC, 512], F32, "bsb")
        op("sp", "dma_start", out=tril, in_=C["tril"][:, :], writes=[tril_s], dma=DMA_MISC)
        for r in range(4):
            op("sp", "dma_start", out=bsb[:, :, r * 128:(r + 1) * 128], in_=b_s.partition_broadcast(128), writes=[bsb_s], dma=DMA_MISC)
        for g in range(NC):
            w, w_s2 = wl[g % 2]
            op("sp", "dma_start", out=w, in_=w_s[g], writes=[w_s2], dma=DMA_MISC)
            op("dve", "tensor_tensor", w, w, tril, ALU.mult, reads=[w_s2, tril_s], writes=[w_s2])
            bk = g % 2
            op("pe", "matmul", ps[bk][:, 0:128], w, self.identf, start=True, stop=True, reads=[w_s2, cs], writes=[pss[bk]])
            op("act", "copy", wT[:, g, :], ps[bk][:, 0:128], reads=[pss[bk]], writes=[wT_s])
        self.P.barrier()
        vg = [A.alloc([T], F32, "vg%d" % c) for c in range(NC)]
        xl = [A.alloc([T], F32, "gx%d" % i) for i in range(2)]
        tb, tb_s = A.alloc([T], F32, "gt")
        sqb, sqb_s = A.alloc([T], F32, "gsq")
        for c in range(NC):
            x, x_s = xl[c % 2]
            y, y_s = vg[c]
            op("sp", "dma_start", out=x, in_=zT_d[(20 + c) * 128:(21 + c) * 128, :], reads=[self.dslot(("zT", 20 + c))], writes=[x_s], dma=DMA_LD)
            self.gelu_tanh(y, y_s, x, x_s, tb, tb_s)
            self.ln_stats_acc(y, y_s, sqb, sqb_s, c, NC)
        mean, mean_s = A.alloc([T], F32, "gmean")
        rstd, rstd_s = A.alloc([T], F32, "grstd")
        self.ln_finish(mean, mean_s, rstd, rstd_s, 1024)
        self.P.barrier()
        ugs = [(sqb, sqb_s), (tb, tb_s)]
        vnb = [A.alloc([T], BF16, "vnb%d" % i) for i in range(2)]
        vtok = [A.alloc([16, 128], BF16, "vtok%d" % i) for i in range(2)]
        tmp = [A.alloc([512], F32, "gtmp%d" % i) for i in range(2)]
        ob = [A.alloc([T], BF16, "gob%d" % i) for i in range(2)]

        def stage_a(c):
            y, y_s = vg[c]
            x, x_s = xl[c % 2]
            vn, vn_s = vnb[c % 2]
            vt, vt_s = vtok[c % 2]
            ug, ug_s = ugs[c % 2]
            op("sp", "dma_start", out=x, in_=zT_d[(12 + c) * 128:(13 + c) * 128, :], reads=[self.dslot(("zT", 12 + c))], writes=[x_s], dma=DMA_LD)
            op("act", "activation", ug, x, AF.Gelu_apprx_tanh, reads=[x_s], writes=[ug_s])
            op("dve", "tensor_tensor", y, y, mean, ALU.subtract, reads=[y_s, mean_s], writes=[y_s])
            op("pool", "tensor_tensor", y, y, rstd, ALU.mult, reads=[y_s, rstd_s], writes=[y_s])
            op("act", "activation", vn, y, AF.Identity, bias=cw[:, c, 1:2], scale=cw[:, c, 0:1], reads=[y_s, cw_s], writes=[vn_s])
            for half in range(2):
                bk = 6 + half
                pv = ps[bk][:].bitcast(BF16)
                for j in range(8):
                    n = half * 8 + j
                    op("pe", "transpose", pv[:, j * 128:(j + 1) * 128], vn[:, n * 128:(n + 1) * 128], self.identb,
                       reads=[vn_s, cs], writes=[pss[bk]])
                op("dve", "tensor_copy", vt[:, half * 8:(half + 1) * 8, :].rearrange("p a b -> p (a b)"), pv[:, 0:1024],
                   reads=[pss[bk]], writes=[vt_s])

        def stage_b(c):
            vt, vt_s = vtok[c % 2]
            o, o_s = ob[c % 2]
            ug, ug_s = ugs[c % 2]
            for tq in range(4):
                bk = tq
                t_, t_s = tmp[tq % 2]
                sl = slice(tq * 512, (tq + 1) * 512)
                for r in range(4):
                    n = tq * 4 + r
                    op("pe", "matmul", ps[bk][:, r * 128:(r + 1) * 128], vt[:, n, :], wT[:, c, :], start=True, stop=True,
                       reads=[vt_s, wT_s], writes=[pss[bk]])
                op("dve", "tensor_tensor", t_, ps[bk][:, :], bsb[:, c, :], ALU.add, reads=[pss[bk], bsb_s], writes=[t_s])
                op("pool", "tensor_tensor", o[:, sl], t_, ug[:, sl], ALU.mult, reads=[t_s, ug_s], writes=[o_s])
            r0 = 1024 + c * 128
            op("sp", "dma_start", out=mixT_d[r0:r0 + 128, :], in_=o, reads=[o_s], writes=[self.dslot(("mixT", 8 + c))], dma=DMA_ST)

        stage_a(0)
        for c in range(NC):
            if c + 1 < NC:
                stage_a(c + 1)
            stage_b(c)


def host_consts():
    c = {}
    c["identb"] = np.eye(128, dtype=np.float32).astype(ml_dtypes.bfloat16)
    c["identf"] = np.eye(128, dtype=np.float32)
    c["onesf"] = np.ones((128, 128), dtype=np.float32)
    c["onesb"] = np.ones((128, 128), dtype=np.float32).astype(ml_dtypes.bfloat16)
    e = np.zeros((128, 8), dtype=np.float32)
    e[:, 0] = EPS
    e[:, 1] = 1e-30
    e[:, 2] = -np.pi
    e[:, 3] = -30000.0
    c["epsc"] = e
    inv = (10000.0 ** (-np.arange(0, 128, 2, dtype=np.float32) / 128)).astype(np.float32)
    c["invc"] = np.concatenate([inv, inv])[:, None].astype(np.float32)
    c["invrow"] = np.broadcast_to(inv[None, :], (128, 64)).astype(np.float32).copy()
    rp = np.zeros((128, 128), np.float32)
    for k in range(128):
        if k >= 64:
            rp[k, k - 64] = -1.0
        else:
            rp[k, k + 64] = 1.0
    c["rotPT"] = rp
    bf = ml_dtypes.bfloat16
    tt = np.arange(T)
    ends = np.arange(127) * 16 + 31
    v = np.zeros((128, T), np.float32)
    v[:127] = (ends[:, None] <= tt[None, :])
    NEGM = 30000.0
    c["validT"] = ((v - 1.0) * NEGM).astype(bf)
    p = np.arange(128)[:, None]
    j = np.arange(512)[None, :]
    cmk = np.stack([((i * 128 + p) <= j) for i in range(4)], axis=1).astype(np.float32)
    c["cm"] = ((cmk - 1.0) * NEGM).reshape(128, 4 * 512).astype(bf)
    rels = [-256, -128, 0, 128, 256, 384]
    wmk = np.stack([((j - p - r) >= 0) & ((j - p - r) < 256) for r in rels], axis=1).astype(np.float32)
    c["wm"] = ((wmk - 1.0) * NEGM).reshape(128, 6 * 512).astype(bf)
    rels = [-128, 0, 128, 256, 384]
    smk = np.stack([((j - p - r) >= 0) & ((j - p - r) < 128) for r in rels], axis=1).astype(np.float32)
    c["sm"] = ((smk - 1.0) * NEGM).reshape(128, 5 * 512).astype(bf)
    E = np.zeros((128, 16, 128), np.float32)
    for kt in range(16):
        for pp in range(128):
            E[2 * kt + pp // 64, kt, pp] = 1.0
    c["E"] = E.reshape(128, 16 * 128).astype(bf)
    SG = np.zeros((128, 24, 128), np.float32)
    for i in range(24):
        SG[i, i, :] = 1.0
    c["SelG"] = SG.reshape(128, 24 * 128).astype(bf)
    starts = np.arange(127) * 16
    sel_start = np.arange(32) * 64
    ov = np.zeros((128, 32), np.float32)
    ov[:127] = ((starts[:, None] <= sel_start[None, :] + 63) & (ends[:, None] >= sel_start[None, :]))
    c["ovl"] = ov
    cur = tt // 64
    jb = np.arange(32)[None, :]
    forced = (jb == 0) | (jb == cur[:, None]) | (jb == cur[:, None] - 1)
    valid_s = sel_start[None, :] <= tt[:, None]
    Am = (valid_s & ~forced).astype(np.float32)
    Bm = np.where(forced, 1e6, np.where(valid_s, 0.0, -1.0)).astype(np.float32)
    c["topA"] = Am.reshape(16, 128, 32).transpose(1, 0, 2).reshape(128, 16 * 32).copy()
    c["topB"] = Bm.reshape(16, 128, 32).transpose(1, 0, 2).reshape(128, 16 * 32).copy()
    c["tril"] = np.tril(np.ones((128, 128), np.float32))
    return c


CONST_SPECS = {"identb": ([128, 128], BF16), "identf": ([128, 128], F32), "onesf": ([128, 128], F32),
               "onesb": ([128, 128], BF16), "epsc": ([128, 8], F32), "invc": ([128, 1], F32),
               "invrow": ([128, 64], F32), "rotPT": ([128, 128], F32),
               "validT": ([128, T], BF16), "cm": ([128, 2048], BF16), "wm": ([128, 3072], BF16), "sm": ([128, 2560], BF16),
               "E": ([128, 2048], BF16), "SelG": ([128, 3072], BF16), "ovl": ([128, 32], F32),
               "topA": ([128, 512], F32), "topB": ([128, 512], F32), "tril": ([128, 128], F32)}

IN_SPECS = [
    ("x", [T, D], F32), ("c", [1, D], F32), ("positions", [1, T], I32),
    ("ada_w", [2, D, 6 * D], F32), ("ada_b", [2, 6 * D], F32), ("norm_g", [2, 2, D], F32),
    ("ffn_w_up", [2, D, 2 * DFF], F32), ("ffn_conv_w", [2, 3, 2 * DFF], F32), ("ffn_conv_b", [2, 2 * DFF], F32),
    ("ffn_w_down", [2, DFF, D], F32),
    ("ev_w_in", [1, D, EVEN_IN], F32), ("ev_w_out", [1, D, D], F32), ("ev_conv_w", [1, 31, 1024], F32),
    ("ev_conv_b", [1, 1024], F32), ("ev_conv_ln_g", [1, 1024], F32), ("ev_conv_ln_b", [1, 1024], F32),
    ("ev_q_norm", [1, 128], F32), ("ev_k_norm", [1, 3, 128], F32),
    ("ev_cmp_k_pos", [1, 32, 128], F32), ("ev_cmp_k_w1", [1, 4096, 256], F32), ("ev_cmp_k_w2", [1, 256, 128], F32),
    ("ev_cmp_v_pos", [1, 32, 128], F32), ("ev_cmp_v_w1", [1, 4096, 256], F32), ("ev_cmp_v_w2", [1, 256, 128], F32),
    ("od_w_in", [1, D, ODD_IN], F32), ("od_w_out", [1, D, D], F32), ("od_q_norm", [1, 128], F32),
    ("od_k_norm", [1, 128], F32), ("od_sinks", [1, 8], F32), ("od_gmlp_ln_g", [1, 1024], F32),
    ("od_gmlp_ln_b", [1, 1024], F32), ("od_gmlp_w_s", [1, 8, 128, 128], F32), ("od_gmlp_b_s", [1, 8, 128], F32),
]


def build_nc(dbg_outs=(), stop=None):
    nc = bass.Bass("TRN2", target_bir_lowering=False)
    I = {}
    for name, shape, dt in IN_SPECS:
        I[name] = nc.dram_tensor(name, shape, dt, kind="ExternalInput").ap()
    C = {}
    for name, (shape, dt) in CONST_SPECS.items():
        C[name] = nc.dram_tensor("k_" + name, shape, dt, kind="ExternalInput").ap()
    out = nc.dram_tensor("out", [T, D], F32, kind="ExternalOutput").ap()

    def scratch(name, shape, dt):
        kind = "ExternalOutput" if name in dbg_outs else "Internal"
        return nc.dram_tensor(name, shape, dt, kind=kind).ap()

    gbc_d = scratch("gbc_d", [2, 2, 128, D], F32)
    zT_d = scratch("zT_d", [37 * 128, T], F32)
    mixT_d = scratch("mixT_d", [D, T], BF16)
    xa_d = scratch("xa_d", [T, D], F32)
    xb_d = scratch("xb_d", [T, D], F32)
    cs_d = scratch("cs_d", [2, 128, T], F32)
    qk_d = scratch("qk_d", [12, 128, T], BF16)
    v_d = scratch("v_d", [4, 128, T], BF16)
    kcT_d = scratch("kcT_d", [2, 128, 128], BF16)
    vc_d = scratch("vc_d", [2, 128, 128], BF16)

    with ExitStack() as st:
        B = Builder(nc, st)
        B.load_consts(C)
        B.phase_rope(I["positions"], cs_d, C)
        B.phase_inproj(0, I["x"], "x_in", I["ev_w_in"][0], EVEN_IN, zT_d,
                       side=lambda: B.mod_gen([0], I["c"], I["ada_w"], I["ada_b"], I["norm_g"], gbc_d, (4, 4, 4, 4)), side_first=8)
        B.phase_conformer(zT_d, mixT_d, I["ev_conv_w"][0], I["ev_conv_b"][0:1, :], I["ev_conv_ln_g"][0:1, :], I["ev_conv_ln_b"][0:1, :])
        jobs = []
        for h in range(8):
            jobs.append((16 + h, I["ev_q_norm"][0:1, :], 128 ** -0.5, qk_d[h], ("qk", h)))
        for g in range(2):
            jobs.append((28 + g, I["ev_k_norm"][0, 1:2, :], 1.0, qk_d[8 + g], ("qk", 8 + g)))
        for g in range(2):
            jobs.append((32 + g, I["ev_k_norm"][0, 2:3, :], 1.0, qk_d[10 + g], ("qk", 10 + g)))
        B.phase_qk(zT_d, cs_d, jobs, C)
        vj = []
        for g in range(2):
            vj.append((30 + g, v_d[g], ("v", g)))
        for g in range(2):
            vj.append((34 + g, v_d[2 + g], ("v", 2 + g)))
        B.phase_vprep(zT_d, vj)
        prm = {"k_pos": I["ev_cmp_k_pos"][0], "k_w1": I["ev_cmp_k_w1"][0], "k_w2": I["ev_cmp_k_w2"][0],
               "v_pos": I["ev_cmp_v_pos"][0], "v_w1": I["ev_cmp_v_w1"][0], "v_w2": I["ev_cmp_v_w2"][0],
               "k_gain": I["ev_k_norm"][0, 0:1, :]}
        B.phase_compress(zT_d, I["positions"], C, prm, kcT_d, vc_d)
        B.phase_nsa_attn(zT_d, qk_d, v_d, kcT_d, vc_d, mixT_d, C,
                         side=lambda: B.mod_gen([1], I["c"], I["ada_w"], I["ada_b"], I["norm_g"], gbc_d, (2, 2, 2, 2)))
        B.phase_outproj(mixT_d, I["ev_w_out"][0], I["x"], "x_in", gbc_d[0, 0], ("gbc", 0, 0), xa_d, "xa")
        if stop == "mix0":
            B.P.barrier()
            B.P.emit(st)
            return nc
        B.phase_ffn(0, xa_d, "xa", xb_d, "xb", I["ffn_w_up"][0], I["ffn_conv_w"][0], I["ffn_conv_b"][0:1, :], I["ffn_w_down"][0],
                    gbc_d[0, 1], ("gbc", 0, 1))
        B.phase_inproj(1, xb_d, "xb", I["od_w_in"][0], ODD_IN, zT_d)
        jobs = []
        for h in range(8):
            jobs.append((h, I["od_q_norm"][0:1, :], 128 ** -0.5, qk_d[h], ("qk", h)))
        for g in range(2):
            jobs.append((8 + g, I["od_k_norm"][0:1, :], 1.0, qk_d[8 + g], ("qk", 8 + g)))
        B.phase_qk(zT_d, cs_d, jobs, C)
        B.phase_vprep(zT_d, [(10 + g, v_d[g], ("v", g)) for g in range(2)])
        B.phase_swa_attn(qk_d, v_d, I["od_sinks"][0:1, :], mixT_d, C)
        B.phase_gmlp(zT_d, mixT_d, I["od_gmlp_ln_g"][0:1, :], I["od_gmlp_ln_b"][0:1, :], I["od_gmlp_w_s"][0],
                     I["od_gmlp_b_s"][0].rearrange("g t -> (g t)").rearrange("(o n) -> o n", o=1), C)
        B.phase_outproj(mixT_d, I["od_w_out"][0], xb_d, "xb", gbc_d[1, 0], ("gbc", 1, 0), xa_d, "xa")
        B.phase_ffn(1, xa_d, "xa", out, "out", I["ffn_w_up"][1], I["ffn_conv_w"][1], I["ffn_conv_b"][1:2, :], I["ffn_w_down"][1],
                    gbc_d[1, 1], ("gbc", 1, 1))
        B.P.barrier()
        B.P.emit(st)
    return nc


_NC_CACHE = {}


def kernel(**inputs):
    n = 8
    if "nc" not in _NC_CACHE:
        _NC_CACHE["nc"] = build_nc()
    nc = _NC_CACHE["nc"]
    consts = host_consts()
    shared = {}
    for name, shape, dt in IN_SPECS:
        if name in ("x", "c", "positions"):
            continue
        shared[name] = np.ascontiguousarray(inputs[name])
    for k, v in consts.items():
        shared["k_" + k] = v
    in_maps = []
    for b in range(n):
        m = dict(shared)
        m["x"] = np.ascontiguousarray(inputs["x"][b])
        m["c"] = np.ascontiguousarray(inputs["c"][b:b + 1])
        m["positions"] = np.ascontiguousarray(inputs["positions"][b:b + 1]).astype(np.int32)
        in_maps.append(m)
    res = run_bass_kernel_spmd(nc, in_maps, core_ids=list(range(n)))
    return np.stack([np.asarray(r["out"]) for r in res.results], axis=0).astype(np.float32)
```

```python
import numpy as np
import ml_dtypes
from contextlib import ExitStack
import concourse.bass as bass
import concourse.mybir as mybir
from concourse.bass_utils import run_bass_kernel_spmd

F32 = mybir.dt.float32
BF16 = mybir.dt.bfloat16
I32 = mybir.dt.int32
AF = mybir.ActivationFunctionType
ALU = mybir.AluOpType
AX = mybir.AxisListType

D = 2048
T = 2048
NCH = 16
DFF = 5632
EVEN_IN = 4632
ODD_IN = 3584
EPS = 1e-6
ENGS = ("pe", "act", "dve", "pool", "sp")
NDMA = 90


class Slot:
    __slots__ = ("name", "w", "r", "is_dram")

    def __init__(self, name="", is_dram=False):
        self.name = name
        self.w = None
        self.r = {}
        self.is_dram = is_dram


class Prog:
    def __init__(self, nc):
        self.nc = nc
        self.streams = {e: [] for e in ENGS}
        self.cnt = {e: 0 for e in ENGS}
        for i in range(NDMA):
            self.cnt["dma%d" % i] = 0
        self.waited = {e: {} for e in ENGS}
        self.nins = 0
        self.slot_sem = {}
        self.free_dma = ["dma%d" % i for i in range(NDMA)]

    def dma_sem_for(self, slots):
        cand = [s for s in slots if not s.is_dram]
        assert len(cand) >= 1, "DMA needs an SBUF-side slot"
        s = cand[0]
        k = self.slot_sem.get(id(s))
        if k is None:
            assert self.free_dma, "out of DMA semaphores in this phase"
            k = self.free_dma.pop(0)
            self.slot_sem[id(s)] = k
        for o in cand[1:]:
            self.slot_sem.setdefault(id(o), k)
        return k

    def _need(self, eng, ev, waits):
        if ev is None:
            return
        k, v = ev
        if self.waited[eng].get(k, 0) >= v:
            return
        waits[k] = max(waits.get(k, 0), v)

    def op(self, eng, meth, *args, reads=(), writes=(), dma=None, **kwargs):
        waits = {}
        for s in reads:
            self._need(eng, s.w, waits)
        for s in writes:
            self._need(eng, s.w, waits)
            for ev in s.r.items():
                self._need(eng, ev, waits)
        if eng == "pe":
            waits.pop("pe", None)
        for k, v in waits.items():
            self.waited[eng][k] = max(self.waited[eng].get(k, 0), v)
        if dma is None:
            self.cnt[eng] += 1
            ev = (eng, self.cnt[eng])
            inc = (eng, 1)
        else:
            k = self.dma_sem_for(list(writes) + list(reads))
            self.cnt[k] += 16
            ev = (k, self.cnt[k])
            inc = (k, 16)
        self.streams[eng].append((list(waits.items()), (meth, args, kwargs), inc))
        self.nins += 1
        for s in reads:
            if s.r.get(ev[0], 0) < ev[1]:
                s.r[ev[0]] = ev[1]
        for s in writes:
            s.w = ev
            s.r = {}
        return ev

    def wait_all(self, eng, events):
        waits = {}
        for ev in events:
            self._need(eng, ev, waits)
        for k, v in waits.items():
            self.waited[eng][k] = max(self.waited[eng].get(k, 0), v)
        self.streams[eng].append((list(waits.items()), None, None))

    def barrier(self):
        evs = [(k, v) for k, v in self.cnt.items() if v > 0]
        for e in ENGS:
            self.wait_all(e, evs)
        self.slot_sem = {}
        self.free_dma = ["dma%d" % i for i in range(NDMA)]

    def emit(self, stack):
        nc = self.nc
        sems = {k: stack.enter_context(nc.semaphore("s_" + k)) for k in self.cnt}
        block = stack.enter_context(nc.Block())
        streams = self.streams

        def run(name):
            def body(eng):
                for waits, fn, inc in streams[name]:
                    for k, v in waits:
                        eng.wait_ge(sems[k], v)
                    if fn is not None:
                        ins = getattr(eng, fn[0])(*fn[1], **fn[2])
                        ins.then_inc(sems[inc[0]], inc[1])
            return body

        block.tensor(run("pe"))
        block.scalar(run("act"))
        block.vector(run("dve"))
        block.gpsimd(run("pool"))
        block.sync(run("sp"))


class Arena:
    def __init__(self, t, words):
        self.t = t
        self.words = words
        self.base = 0
        self.off = 0

    def reset(self):
        self.off = self.base

    def alloc(self, free_shape, dt, name=""):
        n = int(np.prod(free_shape))
        esz = 4 if dt in (F32, I32) else 2
        w = (n * esz + 3) // 4
        w = (w + 7) // 8 * 8
        assert self.off + w <= self.words, "arena overflow %s need %d have %d" % (name, w, self.words - self.off)
        v = self.t[:, self.off:self.off + w]
        self.off += w
        if esz == 2:
            v = v.bitcast(dt)[:, 0:n]
        else:
            v = v[:, 0:n]
            if dt != F32:
                v = v.bitcast(dt)
        if len(free_shape) == 2:
            v = v.rearrange("p (a b) -> p a b", a=free_shape[0])
        elif len(free_shape) == 3:
            v = v.rearrange("p (a b c) -> p a b c", a=free_shape[0], b=free_shape[1])
        return v, Slot(name)


DMA_W, DMA_LD, DMA_ST, DMA_MISC, DMA_LD2, DMA_W2 = 0, 1, 2, 3, 4, 5


class Builder:
    def __init__(self, nc, st, dbg=None):
        self.nc = nc
        self.st = st
        self.P = Prog(nc)
        self.dbg = dbg
        self.arena_t = st.enter_context(nc.sbuf_tensor("arena", [128, 196 * 256], F32))
        self.A = Arena(self.arena_t, 196 * 256)
        self.ps = []
        self.pss = []
        for i in range(8):
            self.ps.append(st.enter_context(nc.psum_tensor("ps%d" % i, [128, 512], F32)))
            self.pss.append(Slot("ps%d" % i))
        self.dslots = {}

    def op(self, *a, **k):
        return self.P.op(*a, **k)

    def dslot(self, key):
        if key not in self.dslots:
            self.dslots[key] = Slot(str(key), is_dram=True)
        return self.dslots[key]

    def phase(self):
        self.P.barrier()
        self.A.reset()

    def load_consts(self, cd):
        A = self.A
        self.identb, s1 = A.alloc([128], BF16, "identb")
        self.identf, s2 = A.alloc([128], F32, "identf")
        self.onesf, s3 = A.alloc([128], F32, "onesf")
        self.onesb, s4 = A.alloc([128], BF16, "onesb")
        self.epsc, s5 = A.alloc([8], F32, "epsc")
        self.cslot = Slot("consts")
        for dst, src in ((self.identb, cd["identb"]), (self.identf, cd["identf"]),
                         (self.onesf, cd["onesf"]), (self.onesb, cd["onesb"]), (self.epsc, cd["epsc"])):
            self.op("sp", "dma_start", out=dst, in_=src[:, :], writes=[self.cslot], dma=DMA_MISC)
        self.modcol, self.modcol_s = A.alloc([2 * 4, 16], F32, "modcol")
        A.base = A.off

    def phase_mod(self, layers, c_d, ada_w, ada_b, norm_g, gbc_d):
        self.phase()
        for _ in self.mod_gen(layers, c_d, ada_w, ada_b, norm_g, gbc_d, (0, 1, 2, 3)):
            pass

    def mod_gen(self, layers, c_d, ada_w, ada_b, norm_g, gbc_d, banks):
        A, op, ps, pss = self.A, self.op, self.ps, self.pss
        cs = self.cslot
        bR, bB, bC, bN = banks
        cT, cT_s = A.alloc([128], F32, "cT")
        cact, cact_s = A.alloc([16], BF16, "cact")
        ngT, ngT_s = A.alloc([128], F32, "ngT")
        wb = [A.alloc([16, 512], BF16, "adaw%d" % i) for i in range(2)]
        br = [A.alloc([512], F32, "brow%d" % i) for i in range(2)]
        mr = [A.alloc([512], F32, "mrow%d" % i) for i in range(2)]
        gst = [A.alloc([512], F32, "gst%d" % i) for i in range(2)]
        tmpc, tmpc_s = A.alloc([16], F32, "tmpc")
        tmpg, tmpg_s = A.alloc([16], F32, "tmpg")
        colst, colst_s = A.alloc([16], F32, "colst")
        op("sp", "dma_start", out=cT[0:16, :], in_=c_d.rearrange("o (k p) -> (o k) p", p=128), writes=[cT_s], dma=DMA_MISC)
        op("pe", "matmul", ps[bC][:, 0:16], cT[0:16, :], self.identf[0:16, 0:16], start=True, stop=True,
           reads=[cT_s, cs], writes=[pss[bC]])
        op("act", "activation", cact, ps[bC][:, 0:16], AF.Silu, reads=[pss[bC]], writes=[cact_s])
        steps = [(i, n) for i in layers for n in range(24)]

        def issue_loads(idx):
            i_, n_ = steps[idx]
            awv_ = ada_w[i_].rearrange("(k p) n -> p k n", p=128)
            w_, w_s_ = wb[idx % 2]
            b_, b_s_ = br[idx % 2]
            op("pool", "dma_start", out=w_, in_=awv_[:, :, n_ * 512:(n_ + 1) * 512], writes=[w_s_], dma=DMA_W)
            op("sp", "dma_start", out=b_[0:1, :], in_=ada_b[i_:i_ + 1, n_ * 512:(n_ + 1) * 512], writes=[b_s_], dma=DMA_MISC)

        issue_loads(0)
        it = 0
        for i in layers:
            for n in range(24):
                w, w_s = wb[it % 2]
                b, b_s = br[it % 2]
                m, m_s = mr[it % 2]
                g, g_s = gst[it % 2]
                it += 1
                if it < len(steps):
                    issue_loads(it)
                for k in range(16):
                    op("pe", "matmul", ps[bR][0:1, :], cact[:, k:k + 1], w[:, k, :], start=(k == 0), stop=(k == 15),
                       reads=[cact_s, w_s], writes=[pss[bR]])
                op("dve", "tensor_tensor", m[0:1, :], ps[bR][0:1, :], b[0:1, :], ALU.add, reads=[pss[bR], b_s], writes=[m_s])
                seg, q = n // 4, n % 4
                if seg in (2, 5):
                    gi = 0 if seg == 2 else 1
                    op("pe", "matmul", ps[bB][:, :], self.onesf[0:1, 0:128], m[0:1, :], start=True, stop=True,
                       reads=[m_s, cs], writes=[pss[bB]])
                    op("act", "copy", g, ps[bB][:, :], reads=[pss[bB]], writes=[g_s])
                    op("sp", "dma_start", out=gbc_d[i, gi, :, q * 512:(q + 1) * 512], in_=g, reads=[g_s],
                       writes=[self.dslot(("gbc", i, gi))], dma=DMA_ST)
                else:
                    si = {0: 0, 1: 1, 3: 2, 4: 3}[seg]
                    for jj in range(4):
                        cix = q * 4 + jj
                        op("pe", "matmul", ps[bC][:, 32 + cix:32 + cix + 1], m[0:1, jj * 128:(jj + 1) * 128], self.onesf[0:1, 0:1],
                           start=True, stop=True, reads=[m_s, cs], writes=[pss[bC]])
                    op("dve", "tensor_copy", colst[:, q * 4:(q + 1) * 4], ps[bC][:, 32 + q * 4:32 + q * 4 + 4], reads=[pss[bC]], writes=[colst_s])
                    if q == 3:
                        dst = self.modcol[:, i * 4 + si, :]
                        if si in (0, 2):
                            op("dve", "tensor_copy", dst, colst, reads=[colst_s], writes=[self.modcol_s])
                        else:
                            s_ = 0 if si == 1 else 1
                            op("sp", "dma_start", out=ngT[0:16, :], in_=norm_g[i, s_:s_ + 1, :].rearrange("o (k p) -> (o k) p", p=128),
                               writes=[ngT_s], dma=DMA_MISC)
                            op("dve", "tensor_scalar", tmpc, colst, 1.0, None, ALU.add, reads=[colst_s], writes=[tmpc_s])
                            op("pe", "matmul", ps[bN][:, 64:80], ngT[0:16, :], self.identf[0:16, 0:16], start=True, stop=True,
                               reads=[ngT_s, cs], writes=[pss[bN]])
                            op("dve", "tensor_copy", tmpg, ps[bN][:, 64:80], reads=[pss[bN]], writes=[tmpg_s])
                            op("dve", "tensor_tensor", dst, tmpc, tmpg, ALU.mult, reads=[tmpc_s, tmpg_s],
                               writes=[self.modcol_s])
                yield

    def make_hT(self, x_src, x_key, tok0, ntiles, hT, hT_s, Scol, Gcol, bufs):
        op, ps, pss = self.op, self.ps, self.pss
        cs = self.cslot
        for tt in range(ntiles):
            xt, xt_s = bufs["xt"][tt % 2]
            xn, xn_s = bufs["xn"][tt % 2]
            sq, sq_s = bufs["sq"]
            ss, ss_s = bufs["ss"][tt % 2]
            r0 = tok0 + tt * 128
            op("sp", "dma_start", out=xt, in_=x_src[r0:r0 + 128, :], reads=[self.dslot(x_key)], writes=[xt_s], dma=DMA_LD)
            op("dve", "memset", ss, 0.0, writes=[ss_s])
            op("act", "activation", sq, xt, AF.Square, accum_out=ss[:, 0:1], reads=[xt_s], writes=[sq_s, ss_s])
            op("act", "activation", ss[:, 1:2], ss[:, 0:1], AF.Sqrt, bias=self.epsc[:, 0:1], scale=1.0 / D,
               reads=[ss_s, cs], writes=[ss_s])
            op("dve", "reciprocal", ss[:, 2:3], ss[:, 1:2], reads=[ss_s], writes=[ss_s])
            op("dve", "tensor_scalar", xn, xt, ss[:, 2:3], None, ALU.mult, reads=[xt_s, ss_s], writes=[xn_s])
            for half in range(2):
                bk = 6 + half
                pv = ps[bk][:].bitcast(BF16)
                for j in range(8):
                    k = half * 8 + j
                    op("pe", "transpose", pv[:, j * 128:(j + 1) * 128], xn[:, k * 128:(k + 1) * 128], self.identb,
                       reads=[xn_s, cs], writes=[pss[bk]])
                for j in range(8):
                    k = half * 8 + j
                    hs_ = hT_s[tt // 4] if isinstance(hT_s, list) else hT_s
                    if j % 2 == 0:
                        op("act", "activation", hT[:, k, tt * 128:(tt + 1) * 128], pv[:, j * 128:(j + 1) * 128], AF.Identity,
                           scale=Gcol[:, k:k + 1], bias=Scol[:, k:k + 1], reads=[pss[bk], self.modcol_s], writes=[hs_])
                    else:
                        op("dve", "tensor_scalar", hT[:, k, tt * 128:(tt + 1) * 128], pv[:, j * 128:(j + 1) * 128],
                           Gcol[:, k:k + 1], Scol[:, k:k + 1], ALU.mult, ALU.add, reads=[pss[bk], self.modcol_s], writes=[hs_])

    def hT_bufs(self):
        A = self.A
        return {
            "xt": [A.alloc([2048], F32, "xt%d" % i) for i in range(2)],
            "xn": [A.alloc([2048], BF16, "xn%d" % i) for i in range(2)],
            "sq": A.alloc([2048], BF16, "sq"),
            "ss": [A.alloc([4], F32, "ss%d" % i) for i in range(2)],
        }

    def phase_inproj(self, layer, x_src, x_key, W, NZ, zT_d, side=None, side_first=0):
        self.phase()
        A, op, ps, pss = self.A, self.op, self.ps, self.pss
        hT, hT_s = A.alloc([16, T], BF16, "hT")
        bufs = self.hT_bufs()
        wb = [A.alloc([16, 512], BF16, "w_in%d" % i) for i in range(2)]
        zo = [A.alloc([T], F32, "zo%d" % i) for i in range(2)]
        gen = None
        if side is not None:
            gen = side()
            for _ in range(side_first):
                next(gen)
        Scol = self.modcol[:, layer * 4 + 0, :]
        Gcol = self.modcol[:, layer * 4 + 1, :]
        Wv = W.rearrange("(k p) n -> p k n", p=128)
        ngrp = (NZ + 511) // 512
        op("pool", "dma_start", out=wb[0][0][:, :, 0:min(512, NZ)], in_=Wv[:, :, 0:min(512, NZ)], writes=[wb[0][1]], dma=DMA_W)
        hT_ss = [Slot("hT_tq%d" % i) for i in range(4)]
        self.make_hT(x_src, x_key, 0, 16, hT, hT_ss, Scol, Gcol, bufs)
        ci = 0
        for n in range(ngrp):
            ncols = min(512, NZ - n * 512)
            w, w_s = wb[n % 2]
            if n + 1 < ngrp:
                nc1 = min(512, NZ - (n + 1) * 512)
                op("pool", "dma_start", out=wb[(n + 1) % 2][0][:, :, 0:nc1], in_=Wv[:, :, (n + 1) * 512:(n + 1) * 512 + nc1],
                   writes=[wb[(n + 1) % 2][1]], dma=DMA_W)
            for j in range((ncols + 127) // 128):
                m = min(128, ncols - j * 128)
                z, z_s = zo[ci % 2]
                if gen is not None:
                    try:
                        next(gen)
                    except StopIteration:
                        gen = None
                for tq in range(4):
                    bk = (ci % 2) * 2 + (tq % 2)
                    for k in range(16):
                        op("pe", "matmul", ps[bk][0:m, :], w[:, k, j * 128:j * 128 + m], hT[:, k, tq * 512:(tq + 1) * 512],
                           start=(k == 0), stop=(k == 15), reads=[w_s, hT_ss[tq]], writes=[pss[bk]])
                    if tq % 2 == 0:
                        op("act", "copy", z[0:m, tq * 512:(tq + 1) * 512], ps[bk][0:m, :], reads=[pss[bk]], writes=[z_s])
                    else:
                        op("dve", "tensor_copy", z[0:m, tq * 512:(tq + 1) * 512], ps[bk][0:m, :], reads=[pss[bk]], writes=[z_s])
                row0 = n * 512 + j * 128
                op("sp", "dma_start", out=zT_d[row0:row0 + m, :], in_=z[0:m, :], reads=[z_s],
                   writes=[self.dslot(("zT", row0 // 128))], dma=DMA_ST)
                ci += 1
        if gen is not None:
            for _ in gen:
                pass

    def phase_outproj(self, mixT_d, W, x_src, x_key, gb_src, gb_key, x_dst, dst_key):
        self.phase()
        A, op, ps, pss = self.A, self.op, self.ps, self.pss
        mixT, mixT_s = A.alloc([16, T], BF16, "mixT")
        Wt, Wt_s = A.alloc([16, D], BF16, "Wout")
        gb, gb_s = A.alloc([D], F32, "gb")
        xt = [A.alloc([D], F32, "xt%d" % i) for i in range(2)]
        xo = [A.alloc([D], F32, "xo%d" % i) for i in range(2)]
        Wv = W.rearrange("(k p) n -> p k n", p=128)
        Wt_ss = [Slot("Wt%d" % i) for i in range(8)]
        mx_ss = [Slot("mx%d" % i) for i in range(16)]
        for q in range(8):
            op("pool", "dma_start", out=Wt[:, q * 2:(q + 1) * 2, :], in_=Wv[:, q * 2:(q + 1) * 2, :], writes=[Wt_ss[q]], dma=DMA_W)
        for k in range(16):
            op("sp", "dma_start", out=mixT[:, k, :], in_=mixT_d[k * 128:(k + 1) * 128, :], reads=[self.dslot(("mixT", k))],
               writes=[mx_ss[k]], dma=DMA_LD)
        op("sp", "dma_start", out=gb, in_=gb_src, reads=[self.dslot(gb_key)], writes=[gb_s], dma=DMA_MISC)
        it = 0
        for tt in range(16):
            x, x_s = xt[tt % 2]
            o, o_s = xo[tt % 2]
            op("act", "dma_start", out=x, in_=x_src[tt * 128:(tt + 1) * 128, :], reads=[self.dslot(x_key)], writes=[x_s], dma=DMA_LD)
            for cb in range(4):
                bk = it % 4
                it += 1
                for k in range(16):
                    op("pe", "matmul", ps[bk][:, :], mixT[:, k, tt * 128:(tt + 1) * 128], Wt[:, k, cb * 512:(cb + 1) * 512],
                       start=(k == 0), stop=(k == 15), reads=[mx_ss[k], Wt_ss[k // 2]], writes=[pss[bk]])
                cs_ = slice(cb * 512, (cb + 1) * 512)
                op("dve", "tensor_tensor", o[:, cs_], ps[bk][:, :], gb[:, cs_], ALU.mult, reads=[pss[bk], gb_s], writes=[o_s])
                op("pool", "tensor_tensor", o[:, cs_], o[:, cs_], x[:, cs_], ALU.add, reads=[o_s, x_s], writes=[o_s])
            op("sp", "dma_start", out=x_dst[tt * 128:(tt + 1) * 128, :], in_=o, reads=[o_s], writes=[self.dslot(dst_key)], dma=DMA_ST)

    def rows_to_cols(self, rows_aps, ncols, dst, dst_s, bank):
        A, op, ps, pss = self.A, self.op, self.ps, self.pss
        R = len(rows_aps)
        nch = ncols // 128
        step = 2048
        done = 0
        for c0 in range(0, ncols, step):
            cw_ = min(step, ncols - c0)
            rb, rb_s = A.alloc([step], F32, "r2c")
            for r, ap in enumerate(rows_aps):
                op("sp", "dma_start", out=rb[r:r + 1, 0:cw_], in_=ap[:, c0:c0 + cw_], writes=[rb_s], dma=DMA_MISC)
            for j in range(cw_ // 128):
                ch = c0 // 128 + j
                op("pe", "matmul", ps[bank][:, ch * R:(ch + 1) * R], rb[0:R, j * 128:(j + 1) * 128], self.identf[0:R, 0:R],
                   start=True, stop=True, reads=[rb_s, self.cslot], writes=[pss[bank]])
        op("dve", "tensor_copy", dst.rearrange("p a b -> p (a b)"), ps[bank][:, 0:nch * R], reads=[pss[bank]], writes=[dst_s])

    def phase_ffn(self, layer, x_src, x_key, x_dst, dst_key, w_up, conv_w, conv_b, w_down, gb_src, gb_key):
        self.phase()
        A, op, ps, pss = self.A, self.op, self.ps, self.pss
        NJ = DFF // 128
        TB = 1024
        NBLK = T // TB
        cw, cw_s = A.alloc([2 * NJ, 4], F32, "cw")
        rows = [conv_w[0:1, :], conv_w[1:2, :], conv_w[2:3, :], conv_b]
        mark = A.off
        self.rows_to_cols(rows, 2 * DFF, cw, cw_s, 0)
        self.P.barrier()
        A.off = mark
        halo, halo_s = A.alloc([2 * NJ, 2], F32, "halo")
        op("dve", "memset", halo, 0.0, writes=[halo_s])
        gT, gT_s = A.alloc([NJ, TB], BF16, "gT")
        mark0 = A.off
        Scol = self.modcol[:, layer * 4 + 2, :]
        Gcol = self.modcol[:, layer * 4 + 3, :]
        Wu = w_up.rearrange("(k p) n -> p k n", p=128)
        Wd = w_down.rearrange("(j p) n -> p j n", p=128)
        NG = NJ // 2
        NPC = NJ // 4
        NWD = 4 * NPC
        for blk in range(NBLK):
            self.P.barrier()
            A.off = mark0
            hTb, hTb_s = A.alloc([16, TB], BF16, "hTb")
            bufs = {"xt": [A.alloc([2048], F32, "xt0")] * 2, "xn": [A.alloc([2048], BF16, "xn0")] * 2,
                    "sq": A.alloc([2048], BF16, "sq"), "ss": [A.alloc([4], F32, "ss%d" % i) for i in range(2)]}
            wup = [A.alloc([16, 2, 256], BF16, "wup%d" % i) for i in range(2)]
            zA, zA_s = A.alloc([TB + 2], F32, "za")
            zB, zB_s = A.alloc([TB + 2], F32, "zb")
            aA, aA_s = A.alloc([TB], F32, "aa")
            aB, aB_s = A.alloc([TB], F32, "ab")
            sA, sA_s = A.alloc([TB], F32, "sa")

            def load_wup(g_):
                w, w_s = wup[g_ % 2]
                op("pool", "dma_start", out=w[:, :, 0, :], in_=Wu[:, :, g_ * 256:(g_ + 1) * 256], writes=[w_s], dma=DMA_W)
                op("pool", "dma_start", out=w[:, :, 1, :], in_=Wu[:, :, DFF + g_ * 256:DFF + (g_ + 1) * 256], writes=[w_s], dma=DMA_W)

            load_wup(0)
            hTb_ss = [Slot("hTb_tq%d" % i) for i in range(TB // 512)]
            self.make_hT(x_src, x_key, blk * TB, TB // 128, hTb, hTb_ss, Scol, Gcol, bufs)
            for j in range(NJ):
                g_ = j // 2
                if j % 2 == 0 and g_ + 1 < NG:
                    load_wup(g_ + 1)
                w, w_s = wup[g_ % 2]
                jo = (j % 2) * 128
                b0 = (j % 2) * 4
                for ab_ in range(2):
                    for tq in range(2):
                        bk = b0 + ab_ * 2 + tq
                        for k in range(16):
                            op("pe", "matmul", ps[bk][:, :], w[:, k, ab_, jo:jo + 128], hTb[:, k, tq * 512:(tq + 1) * 512],
                               start=(k == 0), stop=(k == 15), reads=[w_s, hTb_ss[tq]], writes=[pss[bk]])
                for (z, z_s, ab_, cj, acc, acc_s) in ((zA, zA_s, 0, j, aA, aA_s), (zB, zB_s, 1, NJ + j, aB, aB_s)):
                    op("dve", "tensor_copy", z[:, 0:2], halo[:, cj, :], reads=[halo_s], writes=[z_s])
                    for tq in range(2):
                        bk = b0 + ab_ * 2 + tq
                        op("act", "copy", z[:, 2 + tq * 512:2 + (tq + 1) * 512], ps[bk][:, :], reads=[pss[bk]], writes=[z_s])
                    op("dve", "tensor_copy", halo[:, cj, :], z[:, TB:TB + 2], reads=[z_s], writes=[halo_s])
                    op("dve", "tensor_scalar", acc, z[:, 2:TB + 2], cw[:, cj, 2:3], cw[:, cj, 3:4], ALU.mult, ALU.add,
                       reads=[z_s, cw_s], writes=[acc_s])
                    op("dve", "scalar_tensor_tensor", acc, z[:, 1:TB + 1], cw[:, cj, 1:2], acc, ALU.mult, ALU.add,
                       reads=[z_s, cw_s, acc_s], writes=[acc_s])
                    op("dve", "scalar_tensor_tensor", acc, z[:, 0:TB], cw[:, cj, 0:1], acc, ALU.mult, ALU.add,
                       reads=[z_s, cw_s, acc_s], writes=[acc_s])
                op("act", "activation", sA, aA, AF.Silu, reads=[aA_s], writes=[sA_s])
                op("dve", "tensor_tensor", gT[:, j, :], sA, aB, ALU.mult, reads=[sA_s, aB_s], writes=[gT_s])
            self.P.barrier()
            A.off = mark0
            gb, gb_s = A.alloc([D], F32, "gb")
            op("sp", "dma_start", out=gb, in_=gb_src, reads=[self.dslot(gb_key)], writes=[gb_s], dma=DMA_MISC)
            wd = [A.alloc([4, 512], BF16, "wd%d" % i) for i in range(4)]
            xt = [A.alloc([512], F32, "fx%d" % i) for i in range(3)]
            xo = [A.alloc([512], F32, "fo%d" % i) for i in range(3)]

            def load_wd(idx):
                d_, d_s = wd[idx % 4]
                cb_, pc_ = idx // NPC, idx % NPC
                op("pool", "dma_start", out=d_, in_=Wd[:, pc_ * 4:(pc_ + 1) * 4, cb_ * 512:(cb_ + 1) * 512], writes=[d_s], dma=DMA_W)

            for q_ in range(3):
                load_wd(q_)
            ei = 0
            NTT = TB // 128
            for cb in range(4):
                for pc in range(NPC):
                    idx = cb * NPC + pc
                    d_, d_s = wd[idx % 4]
                    if idx + 3 < NWD:
                        load_wd(idx + 3)
                    for tt in range(NTT):
                        for jj in range(4):
                            j = pc * 4 + jj
                            op("pe", "matmul", ps[tt][:, :], gT[:, j, tt * 128:(tt + 1) * 128], d_[:, jj, :],
                               start=(j == 0), stop=(j == NJ - 1), reads=[gT_s, d_s], writes=[pss[tt]])
                cs_ = slice(cb * 512, (cb + 1) * 512)
                for tt in range(NTT):
                    x, x_s = xt[ei % 3]
                    o, o_s = xo[ei % 3]
                    ei += 1
                    r0 = blk * TB + tt * 128
                    op("act", "dma_start", out=x, in_=x_src[r0:r0 + 128, cs_], reads=[self.dslot(x_key)], writes=[x_s], dma=DMA_LD2)
                    op("dve", "tensor_tensor", o, ps[tt][:, :], gb[:, cs_], ALU.mult, reads=[pss[tt], gb_s], writes=[o_s])
                    op("dve", "tensor_tensor", o, o, x, ALU.add, reads=[o_s, x_s], writes=[o_s])
                    op("sp", "dma_start", out=x_dst[r0:r0 + 128, cs_], in_=o, reads=[o_s], writes=[self.dslot(dst_key)], dma=DMA_ST)

    def sin_of(self, dst, ang, ang_s, dst_s, tmp, tmp_s, shift, tmpi, tmpi_s, tmpf, tmpf_s):
        op = self.op
        op("dve", "tensor_scalar", tmp, ang, float(1.0 / (2 * np.pi)), float(shift / (2 * np.pi)), ALU.mult, ALU.add,
           reads=[ang_s], writes=[tmp_s])
        op("dve", "tensor_copy", tmpi, tmp, reads=[tmp_s], writes=[tmpi_s])
        op("dve", "tensor_copy", tmpf, tmpi, reads=[tmpi_s], writes=[tmpf_s])
        op("dve", "tensor_tensor", tmp, tmp, tmpf, ALU.subtract, reads=[tmp_s, tmpf_s], writes=[tmp_s])
        op("act", "activation", dst, tmp, AF.Sin, scale=6.28318, reads=[tmp_s], writes=[dst_s])

    def phase_rope(self, pos_d, cs_d, C):
        self.phase()
        A, op = self.A, self.op
        posb, posb_s = A.alloc([T], I32, "posb")
        ang, ang_s = A.alloc([T], F32, "ang")
        tmp, tmp_s = A.alloc([T], F32, "tmp")
        res = [A.alloc([T], F32, "res%d" % i) for i in range(2)]
        tmpi, tmpi_s = A.alloc([T], I32, "tmpi")
        tmpf, tmpf_s = A.alloc([T], F32, "tmpf")
        invc, invc_s = A.alloc([1], F32, "invc")
        op("sp", "dma_start", out=invc, in_=C["invc"][:, :], writes=[invc_s], dma=DMA_MISC)
        op("sp", "dma_start", out=posb, in_=pos_d[0:1, :].partition_broadcast(128), writes=[posb_s], dma=DMA_MISC)
        op("dve", "tensor_copy", ang, posb, reads=[posb_s], writes=[ang_s])
        op("dve", "tensor_scalar", ang, ang, invc[:, 0:1], None, ALU.mult, reads=[ang_s, invc_s], writes=[ang_s])
        for i, shift in enumerate((np.pi / 2, 0.0)):
            r, r_s = res[i]
            self.sin_of(r, ang, ang_s, r_s, tmp, tmp_s, shift, tmpi, tmpi_s, tmpf, tmpf_s)
            op("sp", "dma_start", out=cs_d[i, :, :], in_=r, reads=[r_s], writes=[self.dslot(("cs", i))], dma=DMA_ST)

    def phase_qk(self, zT_d, cs_d, jobs, C):
        self.phase()
        A, op, ps, pss = self.A, self.op, self.ps, self.pss
        cs = self.cslot
        cosT, cos_s = A.alloc([T], F32, "cosT")
        sinT, sin_s = A.alloc([T], F32, "sinT")
        rotP, rotP_s = A.alloc([128], F32, "rotP")
        op("sp", "dma_start", out=cosT, in_=cs_d[0, :, :], reads=[self.dslot(("cs", 0))], writes=[cos_s], dma=DMA_MISC)
        op("sp", "dma_start", out=sinT, in_=cs_d[1, :, :], reads=[self.dslot(("cs", 1))], writes=[sin_s], dma=DMA_MISC)
        op("sp", "dma_start", out=rotP, in_=C["rotPT"][:, :], writes=[rotP_s], dma=DMA_MISC)
        xb = [A.alloc([T], F32, "qx%d" % i) for i in range(2)]
        sq = [A.alloc([T], F32, "qsq%d" % i) for i in range(2)]
        rs = [A.alloc([T], F32, "qrs%d" % i) for i in range(2)]
        xn = [A.alloc([T], F32, "qxn%d" % i) for i in range(2)]
        t1 = [A.alloc([T], F32, "qt1%d" % i) for i in range(2)]
        t2 = [A.alloc([T], F32, "qt2%d" % i) for i in range(2)]
        ob = [A.alloc([T], BF16, "qo%d" % i) for i in range(2)]
        gc = [A.alloc([2], F32, "qg%d" % i) for i in range(2)]

        def stage_a(ji):
            zc, gain_ap, scale, dst, dst_key = jobs[ji]
            x, x_s = xb[ji % 2]
            s2, s2_s = sq[ji % 2]
            g, g_s = gc[ji % 2]
            r_, r_s = rs[ji % 2]
            n_, n_s = xn[ji % 2]
            b_, b_s = t2[ji % 2]
            op("sp", "dma_start", out=x, in_=zT_d[zc * 128:(zc + 1) * 128, :], reads=[self.dslot(("zT", zc))], writes=[x_s], dma=DMA_LD)
            op("sp", "dma_start", out=g[:, 0:1], in_=gain_ap.rearrange("o d -> d o"), writes=[g_s], dma=DMA_MISC)
            op("dve", "tensor_scalar", g[:, 1:2], g[:, 0:1], float(scale), None, ALU.mult, reads=[g_s], writes=[g_s])
            op("act", "activation", s2, x, AF.Square, reads=[x_s], writes=[s2_s])
            for tq in range(4):
                sl = slice(tq * 512, (tq + 1) * 512)
                op("pe", "matmul", ps[tq][:, :], self.onesf, s2[:, sl], start=True, stop=True, reads=[cs, s2_s], writes=[pss[tq]])
            for tq in range(4):
                sl = slice(tq * 512, (tq + 1) * 512)
                op("act", "activation", r_[:, sl], ps[tq][:, :], AF.Sqrt, bias=self.epsc[:, 0:1], scale=1.0 / 128, reads=[pss[tq], cs], writes=[r_s])
            op("dve", "reciprocal", r_, r_, reads=[r_s], writes=[r_s])
            op("dve", "scalar_tensor_tensor", n_, x, g[:, 1:2], r_, ALU.mult, ALU.mult, reads=[x_s, g_s, r_s], writes=[n_s])
            for tq in range(4):
                sl = slice(tq * 512, (tq + 1) * 512)
                op("pe", "matmul", ps[4 + tq][:, :], rotP, n_[:, sl], start=True, stop=True, reads=[rotP_s, n_s], writes=[pss[4 + tq]])
            for tq in range(4):
                sl = slice(tq * 512, (tq + 1) * 512)
                op("dve", "tensor_tensor", b_[:, sl], ps[4 + tq][:, :], sinT[:, sl], ALU.mult, reads=[pss[4 + tq], sin_s], writes=[b_s])

        def stage_b(ji):
            zc, gain_ap, scale, dst, dst_key = jobs[ji]
            n_, n_s = xn[ji % 2]
            a_, a_s = t1[ji % 2]
            b_, b_s = t2[ji % 2]
            o, o_s = ob[ji % 2]
            op("pool", "tensor_tensor", a_, n_, cosT, ALU.mult, reads=[n_s, cos_s], writes=[a_s])
            op("pool", "tensor_tensor", o, a_, b_, ALU.add, reads=[a_s, b_s], writes=[o_s])
            op("sp", "dma_start", out=dst, in_=o, reads=[o_s], writes=[self.dslot(dst_key)], dma=DMA_ST)

        nj = len(jobs)
        stage_a(0)
        for ji in range(nj):
            if ji + 1 < nj:
                stage_a(ji + 1)
            stage_b(ji)

    def phase_vprep(self, zT_d, jobs):
        self.phase()
        A, op, ps, pss = self.A, self.op, self.ps, self.pss
        xb = [A.alloc([T], F32, "vx%d" % i) for i in range(2)]
        xh = [A.alloc([T], BF16, "vh%d" % i) for i in range(2)]
        vo = [A.alloc([T], BF16, "vo%d" % i) for i in range(2)]
        for ji, (zc, dst, dst_key) in enumerate(jobs):
            x, x_s = xb[ji % 2]
            h, h_s = xh[ji % 2]
            o, o_s = vo[ji % 2]
            op("sp", "dma_start", out=x, in_=zT_d[zc * 128:(zc + 1) * 128, :], reads=[self.dslot(("zT", zc))], writes=[x_s], dma=DMA_LD)
            op("act", "copy", h, x, reads=[x_s], writes=[h_s])
            for half in range(2):
                bk = (ji * 2 + half) % 4
                pv = ps[bk][:].bitcast(BF16)
                for j in range(8):
                    kt = half * 8 + j
                    op("pe", "transpose", pv[:, j * 128:(j + 1) * 128], h[:, kt * 128:(kt + 1) * 128], self.identb,
                       reads=[h_s, self.cslot], writes=[pss[bk]])
                op("dve", "tensor_copy", o[:, half * 1024:(half + 1) * 1024], pv[:, 0:1024], reads=[pss[bk]], writes=[o_s])
            op("sp", "dma_start", out=dst, in_=o, reads=[o_s], writes=[self.dslot(dst_key)], dma=DMA_ST)

    def ln_stats_acc(self, y, y_s, sqb, sqb_s, c, nchunks):
        op, ps, pss = self.op, self.ps, self.pss
        op("act", "activation", sqb, y, AF.Square, reads=[y_s], writes=[sqb_s])
        for tq in range(4):
            sl = slice(tq * 512, (tq + 1) * 512)
            op("pe", "matmul", ps[tq][:, :], self.onesf, y[:, sl], start=(c == 0), stop=(c == nchunks - 1),
               reads=[self.cslot, y_s], writes=[pss[tq]])
            op("pe", "matmul", ps[4 + tq][:, :], self.onesf, sqb[:, sl], start=(c == 0), stop=(c == nchunks - 1),
               reads=[self.cslot, sqb_s], writes=[pss[4 + tq]])

    def ln_finish(self, mean, mean_s, rstd, rstd_s, nfeat, src=None):
        op, ps, pss = self.op, self.ps, self.pss
        for tq in range(4):
            sl = slice(tq * 512, (tq + 1) * 512)
            if src is None:
                a1_, a1s, a2_, a2s = ps[tq][:, :], pss[tq], ps[4 + tq][:, :], pss[4 + tq]
            else:
                a1_, a1s, a2_, a2s = src[0][:, sl], src[1], src[2][:, sl], src[3]
            op("act", "activation", mean[:, sl], a1_, AF.Copy, scale=1.0 / nfeat, reads=[a1s], writes=[mean_s])
            op("dve", "tensor_tensor", rstd[:, sl], mean[:, sl], mean[:, sl], ALU.mult, reads=[mean_s], writes=[rstd_s])
            op("dve", "scalar_tensor_tensor", rstd[:, sl], a2_, 1.0 / nfeat, rstd[:, sl], ALU.mult, ALU.subtract,
               reads=[a2s, rstd_s], writes=[rstd_s])
            op("act", "activation", rstd[:, sl], rstd[:, sl], AF.Sqrt, bias=self.epsc[:, 0:1], scale=1.0, reads=[rstd_s, self.cslot], writes=[rstd_s])
            op("dve", "reciprocal", rstd[:, sl], rstd[:, sl], reads=[rstd_s], writes=[rstd_s])

    def phase_conformer(self, zT_d, mixT_d, conv_w, conv_b, ln_g, ln_b):
        self.phase()
        A, op, ps, pss = self.A, self.op, self.ps, self.pss
        NC = 8
        cw, cw_s = A.alloc([NC, 34], F32, "ccw")
        rows = [conv_w[k:k + 1, :] for k in range(31)] + [conv_b, ln_g, ln_b]
        mark = A.off
        self.rows_to_cols(rows, 1024, cw, cw_s, 0)
        self.P.barrier()
        A.off = mark
        ys = [A.alloc([T], F32, "cy%d" % c) for c in range(NC)]
        a1 = [A.alloc([T], F32, "ca1%d" % i) for i in range(2)]
        a2 = [A.alloc([T], F32, "ca2%d" % i) for i in range(2)]
        gp = [A.alloc([30 + T], BF16, "cgp%d" % i) for i in range(2)]
        dg = [A.alloc([31, 128], BF16, "cdg%d" % i) for i in range(2)]
        sqb, sqb_s = A.alloc([T], F32, "csq")
        s1a, s1a_s = A.alloc([T], F32, "cs1")
        s2a, s2a_s = A.alloc([T], F32, "cs2")
        cs = self.cslot
        for i in range(2):
            op("dve", "memset", gp[i][0][:, 0:30], 0.0, writes=[gp[i][1]])
        def stage_a(c):
            x1, x1_s = a1[c % 2]
            x2, x2_s = a2[c % 2]
            g, g_s = gp[c % 2]
            d_, d_s = dg[c % 2]
            op("sp", "dma_start", out=x1, in_=zT_d[c * 128:(c + 1) * 128, :], reads=[self.dslot(("zT", c))], writes=[x1_s], dma=DMA_LD)
            op("sp", "dma_start", out=x2, in_=zT_d[(8 + c) * 128:(9 + c) * 128, :], reads=[self.dslot(("zT", 8 + c))], writes=[x2_s], dma=DMA_LD)
            op("act", "activation", x2, x2, AF.Sigmoid, reads=[x2_s], writes=[x2_s])
            op("pool", "tensor_tensor", g[:, 30:30 + T], x1, x2, ALU.mult, reads=[x1_s, x2_s], writes=[g_s])
            for k in range(31):
                op("dve", "tensor_scalar", d_[:, k, :], self.identb, cw[:, c, k:k + 1], None, ALU.mult, reads=[cs, cw_s], writes=[d_s])

        def stage_b(c):
            g, g_s = gp[c % 2]
            d_, d_s = dg[c % 2]
            y, y_s = ys[c]
            for tq in range(4):
                sl = slice(tq * 512, (tq + 1) * 512)
                for k in range(31):
                    op("pe", "matmul", ps[tq][:, :], d_[:, k, :], g[:, k + tq * 512:k + tq * 512 + 512], start=(k == 0), stop=(k == 30),
                       reads=[d_s, g_s], writes=[pss[tq]])
                op("act", "activation", y[:, sl], ps[tq][:, :], AF.Identity, bias=cw[:, c, 31:32], scale=1.0, reads=[pss[tq], cw_s], writes=[y_s])
            op("act", "activation", sqb, y, AF.Square, reads=[y_s], writes=[sqb_s])
            for tq in range(4):
                sl = slice(tq * 512, (tq + 1) * 512)
                b1, b2 = 4 + tq % 2, 6 + tq % 2
                op("pe", "matmul", ps[b1][:, :], self.onesf, y[:, sl], start=True, stop=True, reads=[cs, y_s], writes=[pss[b1]])
                op("pe", "matmul", ps[b2][:, :], self.onesf, sqb[:, sl], start=True, stop=True, reads=[cs, sqb_s], writes=[pss[b2]])
                if c == 0:
                    op("dve", "tensor_copy", s1a[:, sl], ps[b1][:, :], reads=[pss[b1]], writes=[s1a_s])
                    op("dve", "tensor_copy", s2a[:, sl], ps[b2][:, :], reads=[pss[b2]], writes=[s2a_s])
                else:
                    op("dve", "tensor_tensor", s1a[:, sl], s1a[:, sl], ps[b1][:, :], ALU.add, reads=[s1a_s, pss[b1]], writes=[s1a_s])
                    op("dve", "tensor_tensor", s2a[:, sl], s2a[:, sl], ps[b2][:, :], ALU.add, reads=[s2a_s, pss[b2]], writes=[s2a_s])

        stage_a(0)
        for c in range(NC):
            if c + 1 < NC:
                stage_a(c + 1)
            stage_b(c)
        mean, mean_s = a1[0]
        rstd, rstd_s = a1[1]
        self.ln_finish(mean, mean_s, rstd, rstd_s, 1024, src=(s1a, s1a_s, s2a, s2a_s))
        ob = [A.alloc([T], BF16, "cob%d" % i) for i in range(2)]
        for c in range(NC):
            y, y_s = ys[c]
            o, o_s = ob[c % 2]
            op("dve", "tensor_tensor", y, y, mean, ALU.subtract, reads=[y_s, mean_s], writes=[y_s])
            op("pool", "tensor_tensor", y, y, rstd, ALU.mult, reads=[y_s, rstd_s], writes=[y_s])
            op("act", "activation", o, y, AF.Silu, bias=cw[:, c, 33:34], scale=cw[:, c, 32:33], reads=[y_s, cw_s], writes=[o_s])
            op("sp", "dma_start", out=mixT_d[c * 128:(c + 1) * 128, :], in_=o, reads=[o_s], writes=[self.dslot(("mixT", c))], dma=DMA_ST)

    def phase_compress(self, zT_d, pos_d, C, prm, kcT_d, vc_d):
        self.phase()
        A, op, ps, pss = self.A, self.op, self.ps, self.pss
        cs = self.cslot
        w1s, w1_s = A.alloc([32, 256], BF16, "w1s")
        w2s, w2_s = A.alloc([2, 128], BF16, "w2s")
        pe_, pe_s = A.alloc([128], F32, "pe")
        posT, posT_s = A.alloc([32], BF16, "posT")
        brow, brow_s = A.alloc([256], BF16, "brow")
        af, af_s = A.alloc([T], F32, "caf")
        ab, ab_s = A.alloc([T + 16], BF16, "cab")
        hid, hid_s = A.alloc([256], BF16, "hid")
        hidT, hidT_s = A.alloc([2, 128], BF16, "hidT")
        kc, kc_s = A.alloc([128], F32, "kc")
        sqj, sqj_s = A.alloc([128], F32, "sqj")
        st_, st_s = A.alloc([4], F32, "kst")
        gainb, gainb_s = A.alloc([128], F32, "gainb")
        invr, invr_s = A.alloc([64], F32, "invr")
        posE, posE_s = A.alloc([16], I32, "posE")
        posF, posF_s = A.alloc([1], F32, "posF")
        angE, angE_s = A.alloc([64], F32, "angE")
        tmpE, tmpE_s = A.alloc([64], F32, "tmpE")
        tmpEi, tmpEi_s = A.alloc([64], I32, "tmpEi")
        tmpEf, tmpEf_s = A.alloc([64], F32, "tmpEf")
        cosE, cosE_s = A.alloc([64], F32, "cosE")
        sinE, sinE_s = A.alloc([64], F32, "sinE")
        kn, kn_s = A.alloc([128], F32, "kn")
        tt_ = [A.alloc([64], F32, "ktt%d" % i) for i in range(4)]
        kr, kr_s = A.alloc([128], BF16, "kr")
        ob, ob_s = A.alloc([128], BF16, "cob")
        op("dve", "memset", hid, 0.0, writes=[hid_s])
        op("dve", "memset", ab[:, T:T + 16], 0.0, writes=[ab_s])
        op("dve", "memset", posE, 0, writes=[posE_s])
        op("sp", "dma_start", out=posE[0:127, :], in_=pos_d[0:1, 16:16 + 16 * 127].rearrange("o (n s) -> (o n) s", s=16),
           writes=[posE_s], dma=DMA_MISC)
        op("sp", "dma_start", out=invr, in_=C["invrow"][:, :], writes=[invr_s], dma=DMA_MISC)
        op("sp", "dma_start", out=gainb, in_=prm["k_gain"].partition_broadcast(128), writes=[gainb_s], dma=DMA_MISC)
        op("dve", "tensor_copy", posF, posE[:, 15:16], reads=[posE_s], writes=[posF_s])
        op("dve", "tensor_scalar", angE, invr, posF[:, 0:1], None, ALU.mult, reads=[invr_s, posF_s], writes=[angE_s])
        self.sin_of(cosE, angE, angE_s, cosE_s, tmpE, tmpE_s, np.pi / 2, tmpEi, tmpEi_s, tmpEf, tmpEf_s)
        self.sin_of(sinE, angE, angE_s, sinE_s, tmpE, tmpE_s, 0.0, tmpEi, tmpEi_s, tmpEf, tmpEf_s)
        for kind in ("k", "v"):
            w1 = prm[kind + "_w1"]
            w2 = prm[kind + "_w2"]
            pos = prm[kind + "_pos"]
            op("pool", "dma_start", out=w1s, in_=w1.rearrange("(l p) h -> p l h", p=128), writes=[w1_s], dma=DMA_W)
            op("pool", "dma_start", out=w2s, in_=w2.rearrange("(c p) d -> p c d", p=128), writes=[w2_s], dma=DMA_W)
            op("sp", "dma_start", out=pe_[0:32, :], in_=pos, writes=[pe_s], dma=DMA_MISC)
            op("pe", "matmul", ps[0][:, 0:32], pe_[0:32, :], self.identf[0:32, 0:32], start=True, stop=True, reads=[pe_s, cs], writes=[pss[0]])
            op("dve", "tensor_copy", posT, ps[0][:, 0:32], reads=[pss[0]], writes=[posT_s])
            for l in range(32):
                op("pe", "matmul", ps[1][0:1, 0:256], posT[:, l:l + 1], w1s[:, l, :], start=(l == 0), stop=(l == 31),
                   reads=[posT_s, w1_s], writes=[pss[1]])
            op("dve", "tensor_copy", brow[0:1, :], ps[1][0:1, 0:256], reads=[pss[1]], writes=[brow_s])
            for g in range(2):
                zc = (24 if kind == "k" else 26) + g
                op("sp", "dma_start", out=af, in_=zT_d[zc * 128:(zc + 1) * 128, :], reads=[self.dslot(("zT", zc))], writes=[af_s], dma=DMA_LD)
                op("act", "copy", ab[:, 0:T], af, reads=[af_s], writes=[ab_s])
                for l in range(32):
                    lv = ab[:, l:l + 16 * 127].rearrange("p (n s) -> p n s", s=16)[:, :, 0]
                    op("pe", "matmul", ps[2][0:127, 0:256], lv, w1s[:, l, :], start=(l == 0), stop=False,
                       reads=[ab_s, w1_s], writes=[pss[2]])
                op("pe", "matmul", ps[2][0:127, 0:256], self.onesb[0:1, 0:127], brow[0:1, :], start=False, stop=True,
                   reads=[cs, brow_s], writes=[pss[2]])
                op("act", "activation", hid[0:127, :], ps[2][0:127, 0:256], AF.Silu, reads=[pss[2]], writes=[hid_s])
                pv = ps[3][:].bitcast(BF16)
                for c in range(2):
                    op("pe", "transpose", pv[:, c * 128:(c + 1) * 128], hid[:, c * 128:(c + 1) * 128], self.identb,
                       reads=[hid_s, cs], writes=[pss[3]])
                op("dve", "tensor_copy", hidT.rearrange("p a b -> p (a b)"), pv[:, 0:256], reads=[pss[3]], writes=[hidT_s])
                for c in range(2):
                    op("pe", "matmul", ps[4][:, 0:128], hidT[:, c, :], w2s[:, c, :], start=(c == 0), stop=(c == 1),
                       reads=[hidT_s, w2_s], writes=[pss[4]])
                if kind == "v":
                    op("act", "copy", ob, ps[4][:, 0:128], reads=[pss[4]], writes=[ob_s])
                    op("sp", "dma_start", out=vc_d[g, :, :], in_=ob, reads=[ob_s], writes=[self.dslot(("vc", g))], dma=DMA_ST)
                    continue
                op("dve", "tensor_copy", kc, ps[4][:, 0:128], reads=[pss[4]], writes=[kc_s])
                op("dve", "memset", st_, 0.0, writes=[st_s])
                op("act", "activation", sqj, kc, AF.Square, accum_out=st_[:, 0:1], reads=[kc_s], writes=[sqj_s, st_s])
                op("act", "activation", st_[:, 1:2], st_[:, 0:1], AF.Sqrt, bias=self.epsc[:, 0:1], scale=1.0 / 128, reads=[st_s, cs], writes=[st_s])
                op("dve", "reciprocal", st_[:, 2:3], st_[:, 1:2], reads=[st_s], writes=[st_s])
                op("dve", "scalar_tensor_tensor", kn, kc, st_[:, 2:3], gainb, ALU.mult, ALU.mult, reads=[kc_s, st_s, gainb_s], writes=[kn_s])
                (ta, ta_s), (tb, tb_s), (tc_, tc_s), (td, td_s) = tt_
                op("dve", "tensor_tensor", ta, kn[:, 0:64], cosE, ALU.mult, reads=[kn_s, cosE_s], writes=[ta_s])
                op("dve", "tensor_tensor", tb, kn[:, 64:128], sinE, ALU.mult, reads=[kn_s, sinE_s], writes=[tb_s])
                op("dve", "tensor_tensor", kr[:, 0:64], ta, tb, ALU.subtract, reads=[ta_s, tb_s], writes=[kr_s])
                op("dve", "tensor_tensor", tc_, kn[:, 64:128], cosE, ALU.mult, reads=[kn_s, cosE_s], writes=[tc_s])
                op("dve", "tensor_tensor", td, kn[:, 0:64], sinE, ALU.mult, reads=[kn_s, sinE_s], writes=[td_s])
                op("dve", "tensor_tensor", kr[:, 64:128], tc_, td, ALU.add, reads=[tc_s, td_s], writes=[kr_s])
                pv5 = ps[5][:].bitcast(BF16)
                op("pe", "transpose", pv5[:, 0:128], kr, self.identb, reads=[kr_s, cs], writes=[pss[5]])
                op("dve", "tensor_copy", ob, pv5[:, 0:128], reads=[pss[5]], writes=[ob_s])
                op("sp", "dma_start", out=kcT_d[g, :, :], in_=ob, reads=[ob_s], writes=[self.dslot(("kcT", g))], dma=DMA_ST)

    def attend(self, W, qv, tiles, den_extra=None):
        op, ps, pss = self.op, self.ps, self.pss
        n = len(tiles)
        base = W["si"]
        W["si"] += n
        W["pair"] = 4 + 2 * (W["ci"] % 2)
        W["ci"] += 1
        bd, bo = W["pair"], W["pair"] + 1

        def issue_scores(i):
            t = tiles[i]
            sb = (base + i) % W["ns"]
            ms = t["masks"]
            op("pe", "matmul", ps[sb][:, :], t["kT"], qv, start=True, stop=(len(ms) == 0), reads=t["kslots"] + [W["q_s"]], writes=[pss[sb]])
            for mi, (lhsT, rhs, sl) in enumerate(ms):
                op("pe", "matmul", ps[sb][:, :], lhsT, rhs, start=False, stop=(mi == len(ms) - 1), reads=sl, writes=[pss[sb]])

        issue_scores(0)
        for i, t in enumerate(tiles):
            if i + 1 < n:
                issue_scores(i + 1)
            sb = (base + i) % W["ns"]
            e, e_s = W["e"][(base + i) % 3]
            op("act", "activation", e, ps[sb][:, :], AF.Exp, reads=[pss[sb]], writes=[e_s])
            op("pe", "matmul", ps[bd][:, :], self.onesb, e, start=(i == 0), stop=(i == n - 1), reads=[self.cslot, e_s], writes=[pss[bd]])
            op("pe", "matmul", ps[bo][:, :], t["v"], e, start=(i == 0), stop=(i == n - 1), reads=t["vslots"] + [e_s], writes=[pss[bo]])

    def attn_finish(self, W, acc, acc_s, first, gate=None, den_add=None):
        op, ps, pss = self.op, self.ps, self.pss
        bd, bo = W["pair"], W["pair"] + 1
        rd, rd_s = W["rd"][W["fi"] % 2]
        tm, tm_s = W["tm"][W["fi"] % 2]
        W["fi"] += 1
        if den_add is None:
            op("dve", "tensor_scalar", rd, ps[bd][:, :], 1e-30, None, ALU.max, reads=[pss[bd]], writes=[rd_s])
        else:
            op("dve", "tensor_scalar", rd, ps[bd][:, :], den_add[0], None, ALU.add, reads=[pss[bd], den_add[1]], writes=[rd_s])
        op("dve", "reciprocal", rd, rd, reads=[rd_s], writes=[rd_s])
        if gate is None:
            op("dve", "tensor_tensor", acc, ps[bo][:, :], rd, ALU.mult, reads=[pss[bo], rd_s], writes=[acc_s])
            return
        lhsT, rhs, sl = gate
        op("pe", "matmul", ps[3][:, :], lhsT, rhs, start=True, stop=True, reads=sl, writes=[pss[3]])
        op("dve", "tensor_tensor", tm, ps[bo][:, :], rd, ALU.mult, reads=[pss[bo], rd_s], writes=[tm_s])
        if first:
            op("dve", "tensor_tensor", acc, tm, ps[3][:, :], ALU.mult, reads=[tm_s, pss[3]], writes=[acc_s])
        else:
            op("dve", "tensor_tensor", tm, tm, ps[3][:, :], ALU.mult, reads=[tm_s, pss[3]], writes=[tm_s])
            op("pool", "tensor_tensor", acc, acc, tm, ALU.add, reads=[acc_s, tm_s], writes=[acc_s])

    def attn_work(self):
        A = self.A
        return {"e": [A.alloc([512], BF16, "e%d" % i) for i in range(3)],
                "rd": [A.alloc([512], F32, "rd%d" % i) for i in range(2)],
                "tm": [A.alloc([512], F32, "tm%d" % i) for i in range(2)],
                "acc": [A.alloc([512], F32, "acc%d" % i) for i in range(2)],
                "ob": [A.alloc([512], BF16, "aob%d" % i) for i in range(2)],
                "si": 0, "mi": 0, "fi": 0, "oi": 0, "ci": 0, "pair": 4, "ns": 3}

    def phase_nsa_attn(self, zT_d, qk_d, v_d, kcT_d, vc_d, mixT_d, C, side=None):
        self.phase()
        A, op, ps, pss = self.A, self.op, self.ps, self.pss
        cs = self.cslot
        q, q_s = A.alloc([8, T], BF16, "q")
        kk, kk_s = A.alloc([4, T], BF16, "kk")
        vv, vv_s = A.alloc([4, 16, 128], BF16, "vv")
        kcT, kcT_s = A.alloc([2, 128], BF16, "kcT")
        vc, vc_s = A.alloc([2, 128], BF16, "vc")
        sg, sg_s = A.alloc([T], BF16, "sg")
        validT, k1 = A.alloc([T], BF16, "validT")
        cm, k2 = A.alloc([4, 512], BF16, "cm")
        wm, k3 = A.alloc([6, 512], BF16, "wm")
        E, k4 = A.alloc([16, 128], BF16, "E")
        SelG, k5 = A.alloc([24, 128], BF16, "SelG")
        selT, selT_s = A.alloc([2, T], BF16, "selT")
        mark = A.off
        glf, glf_s = A.alloc([T], F32, "glf")
        ovl, k6 = A.alloc([32], F32, "ovl")
        topA, k7 = A.alloc([16, 32], F32, "topA")
        topB, k8 = A.alloc([16, 32], F32, "topB")
        ks_ = Slot("nsaconst")
        for dst, name in ((validT, "validT"), (cm.rearrange("p a b -> p (a b)"), "cm"), (wm.rearrange("p a b -> p (a b)"), "wm"),
                          (E.rearrange("p a b -> p (a b)"), "E"), (SelG.rearrange("p a b -> p (a b)"), "SelG"), (ovl, "ovl"),
                          (topA.rearrange("p a b -> p (a b)"), "topA"), (topB.rearrange("p a b -> p (a b)"), "topB")):
            op("sp", "dma_start", out=dst, in_=C[name][:, :], writes=[ks_], dma=DMA_MISC)
        for h in range(8):
            op("sp", "dma_start", out=q[:, h, :], in_=qk_d[h], reads=[self.dslot(("qk", h))], writes=[q_s], dma=DMA_LD)
        for i in range(4):
            op("sp", "dma_start", out=kk[:, i, :], in_=qk_d[8 + i], reads=[self.dslot(("qk", 8 + i))], writes=[kk_s], dma=DMA_LD)
            op("sp", "dma_start", out=vv[:, i, :, :].rearrange("p a b -> p (a b)"), in_=v_d[i], reads=[self.dslot(("v", i))], writes=[vv_s], dma=DMA_LD)
        for g in range(2):
            op("sp", "dma_start", out=kcT[:, g, :], in_=kcT_d[g, :, :], reads=[self.dslot(("kcT", g))], writes=[kcT_s], dma=DMA_LD)
            op("sp", "dma_start", out=vc[:, g, :], in_=vc_d[g, :, :], reads=[self.dslot(("vc", g))], writes=[vc_s], dma=DMA_LD)
        op("sp", "dma_start", out=glf[0:24, :], in_=zT_d[36 * 128:36 * 128 + 24, :], reads=[self.dslot(("zT", 36))], writes=[glf_s], dma=DMA_LD)
        op("act", "activation", sg[0:24, :], glf[0:24, :], AF.Sigmoid, reads=[glf_s], writes=[sg_s])
        impT, impT_s = A.alloc([2, T], F32, "impT")
        e32 = [A.alloc([512], F32, "e32%d" % i) for i in range(2)]
        rdn = [A.alloc([512], F32, "rdn%d" % i) for i in range(2)]
        items = [(g, tq, r) for g in range(2) for tq in range(4) for r in range(4)]

        def s1_a(i):
            g, tq, r = items[i]
            sl = slice(tq * 512, (tq + 1) * 512)
            h = g * 4 + r
            e, e_s = e32[i % 2]
            sb = i % 2
            db = 2 + i % 2
            op("pe", "matmul", ps[sb][:, :], kcT[:, g, :], q[:, h, sl], start=True, stop=False, reads=[kcT_s, q_s], writes=[pss[sb]])
            op("pe", "matmul", ps[sb][:, :], self.identb, validT[:, sl], start=False, stop=True, reads=[cs, ks_], writes=[pss[sb]])
            op("act", "activation", e, ps[sb][:, :], AF.Exp, reads=[pss[sb]], writes=[e_s])
            op("pe", "matmul", ps[db][:, :], self.onesf, e, start=True, stop=True, reads=[cs, e_s], writes=[pss[db]])

        def s1_b(i):
            g, tq, r = items[i]
            sl = slice(tq * 512, (tq + 1) * 512)
            e, e_s = e32[i % 2]
            rd, rd_s = rdn[i % 2]
            db = 2 + i % 2
            op("dve", "tensor_scalar", rd, ps[db][:, :], 1e-30, None, ALU.max, reads=[pss[db]], writes=[rd_s])
            op("dve", "reciprocal", rd, rd, reads=[rd_s], writes=[rd_s])
            op("pool", "tensor_tensor", e, e, rd, ALU.mult, reads=[e_s, rd_s], writes=[e_s])
            op("pe", "matmul", ps[4][0:32, :], ovl, e, start=(r == 0), stop=(r == 3), reads=[ks_, e_s], writes=[pss[4]])
            if r == 3:
                op("act", "copy", impT[0:32, g, sl], ps[4][0:32, :], reads=[pss[4]], writes=[impT_s])

        s1_a(0)
        for i in range(len(items)):
            if i + 1 < len(items):
                s1_a(i + 1)
            s1_b(i)
        sc = [A.alloc([32], F32, "sc%d" % i) for i in range(2)]
        m8 = [A.alloc([8], F32, "m8%d" % i) for i in range(2)]
        titems = [(g, tt) for g in range(2) for tt in range(16)]

        def tk_a(i):
            g, tt = titems[i]
            s_, s_s = sc[i % 2]
            m_, m_s = m8[i % 2]
            b5 = 5 + i % 2
            op("pe", "matmul", ps[b5][:, 0:32], impT[0:32, g, tt * 128:(tt + 1) * 128], self.identf[0:32, 0:32], start=True, stop=True,
               reads=[impT_s, cs], writes=[pss[b5]])
            op("dve", "tensor_tensor", s_, ps[b5][:, 0:32], topA[:, tt, :], ALU.mult, reads=[pss[b5], ks_], writes=[s_s])
            op("dve", "tensor_tensor", s_, s_, topB[:, tt, :], ALU.add, reads=[s_s, ks_], writes=[s_s])
            op("dve", "max", m_, s_, reads=[s_s], writes=[m_s])

        def tk_b(i):
            g, tt = titems[i]
            s_, s_s = sc[i % 2]
            m_, m_s = m8[i % 2]
            op("dve", "tensor_scalar", s_, s_, m_[:, 7:8], None, ALU.is_ge, reads=[s_s, m_s], writes=[s_s])
            op("pe", "matmul", ps[7][0:32, 0:128], s_, self.identf, start=True, stop=True, reads=[s_s, cs], writes=[pss[7]])
            op("act", "activation", selT[0:32, g, tt * 128:(tt + 1) * 128], ps[7][0:32, 0:128], AF.Identity, scale=30000.0,
               bias=self.epsc[0:32, 3:4], reads=[pss[7], cs], writes=[selT_s])

        tk_a(0)
        for i in range(len(titems)):
            if i + 1 < len(titems):
                tk_a(i + 1)
            tk_b(i)
        self.P.barrier()
        A.off = mark
        W = self.attn_work()
        W["q_s"] = q_s
        gen = None
        if side is not None:
            W["ns"] = 2
            gen = side()
        for h in range(8):
            g = h // 4
            for tq in range(4):
                if gen is not None:
                    try:
                        next(gen)
                    except StopIteration:
                        gen = None
                sl = slice(tq * 512, (tq + 1) * 512)
                qv = q[:, h, sl]
                acc, acc_s = W["acc"][W["oi"] % 2]
                ob, ob_s = W["ob"][W["oi"] % 2]
                W["oi"] += 1
                tiles = [dict(kT=kcT[:, g, :], kslots=[kcT_s], v=vc[:, g, :], vslots=[vc_s], masks=[(self.identb, validT[:, sl], [cs, ks_])])]
                self.attend(W, qv, tiles)
                self.attn_finish(W, acc, acc_s, True, gate=(SelG[0:24, h * 3 + 0, :], sg[0:24, sl], [ks_, sg_s]))
                tiles = []
                for kt in range(4 * tq + 4):
                    masks = [(E[0:32, kt, :], selT[0:32, g, sl], [ks_, selT_s])]
                    if kt >= 4 * tq:
                        masks.append((self.identb, cm[:, kt - 4 * tq, :], [cs, ks_]))
                    tiles.append(dict(kT=kk[:, g, kt * 128:(kt + 1) * 128], kslots=[kk_s], v=vv[:, g, kt, :], vslots=[vv_s], masks=masks))
                self.attend(W, qv, tiles)
                self.attn_finish(W, acc, acc_s, False, gate=(SelG[0:24, h * 3 + 1, :], sg[0:24, sl], [ks_, sg_s]))
                tiles = []
                for kt in range(max(0, 4 * tq - 2), 4 * tq + 4):
                    tiles.append(dict(kT=kk[:, 2 + g, kt * 128:(kt + 1) * 128], kslots=[kk_s], v=vv[:, 2 + g, kt, :], vslots=[vv_s],
                                      masks=[(self.identb, wm[:, kt - 4 * tq + 2, :], [cs, ks_])]))
                self.attend(W, qv, tiles)
                self.attn_finish(W, acc, acc_s, False, gate=(SelG[0:24, h * 3 + 2, :], sg[0:24, sl], [ks_, sg_s]))
                op("act", "copy", ob, acc, reads=[acc_s], writes=[ob_s])
                r0 = 1024 + h * 128
                op("sp", "dma_start", out=mixT_d[r0:r0 + 128, sl], in_=ob, reads=[ob_s], writes=[self.dslot(("mixT", 8 + h))], dma=DMA_ST)
        if gen is not None:
            for _ in gen:
                pass

    def phase_swa_attn(self, qk_d, v_d, sinks, mixT_d, C):
        self.phase()
        A, op, ps, pss = self.A, self.op, self.ps, self.pss
        q, q_s = A.alloc([8, T], BF16, "q")
        kk, kk_s = A.alloc([2, T], BF16, "kk")
        vv, vv_s = A.alloc([2, 16, 128], BF16, "vv")
        sm, ks_ = A.alloc([5, 512], BF16, "sm")
        es, es_s = A.alloc([8], F32, "es")
        op("sp", "dma_start", out=sm.rearrange("p a b -> p (a b)"), in_=C["sm"][:, :], writes=[ks_], dma=DMA_MISC)
        op("sp", "dma_start", out=es, in_=sinks.partition_broadcast(128), writes=[es_s], dma=DMA_MISC)
        op("act", "activation", es, es, AF.Exp, reads=[es_s], writes=[es_s])
        for h in range(8):
            op("sp", "dma_start", out=q[:, h, :], in_=qk_d[h], reads=[self.dslot(("qk", h))], writes=[q_s], dma=DMA_LD)
        for i in range(2):
            op("sp", "dma_start", out=kk[:, i, :], in_=qk_d[8 + i], reads=[self.dslot(("qk", 8 + i))], writes=[kk_s], dma=DMA_LD)
            op("sp", "dma_start", out=vv[:, i, :, :].rearrange("p a b -> p (a b)"), in_=v_d[i], reads=[self.dslot(("v", i))], writes=[vv_s], dma=DMA_LD)
        W = self.attn_work()
        W["q_s"] = q_s
        for h in range(8):
            g = h // 4
            for tq in range(4):
                sl = slice(tq * 512, (tq + 1) * 512)
                acc, acc_s = W["acc"][W["oi"] % 2]
                ob, ob_s = W["ob"][W["oi"] % 2]
                W["oi"] += 1
                tiles = []
                for kt in range(max(0, 4 * tq - 1), 4 * tq + 4):
                    tiles.append(dict(kT=kk[:, g, kt * 128:(kt + 1) * 128], kslots=[kk_s], v=vv[:, g, kt, :], vslots=[vv_s],
                                      masks=[(self.identb, sm[:, kt - 4 * tq + 1, :], [self.cslot, ks_])]))
                self.attend(W, q[:, h, sl], tiles)
                self.attn_finish(W, acc, acc_s, True, gate=None, den_add=(es[:, h:h + 1], es_s))
                op("act", "copy", ob, acc, reads=[acc_s], writes=[ob_s])
                op("sp", "dma_start", out=mixT_d[h * 128:(h + 1) * 128, sl], in_=ob, reads=[ob_s], writes=[self.dslot(("mixT", h))], dma=DMA_ST)

    def gelu_tanh(self, out, out_s, x, x_s, t, t_s):
        self.op("act", "activation", out, x, AF.Gelu_apprx_tanh, reads=[x_s], writes=[out_s])

    def phase_gmlp(self, zT_d, mixT_d, ln_g, ln_b, w_s, b_s, C):
        self.phase()
        A, op, ps, pss = self.A, self.op, self.ps, self.pss
        cs = self.cslot
        NC = 8
        cw, cw_s = A.alloc([NC, 2], F32, "gcw")
        mark = A.off
        self.rows_to_cols([ln_g, ln_b], 1024, cw, cw_s, 0)
        self.P.barrier()
        A.off = mark
        wT, wT_s = A.alloc([NC, 128], BF16, "wT")
        tril, tril_s = A.alloc([128], F32, "tril")
        wl = [A.alloc([128], F32, "wl%d" % i) for i in range(2)]
        bsb, bsb_s = A.alloc([NC, 512], F32, "bsb")
        op("sp", "dma_start", out=tril, in_=C["tril"][:, :], writes=[tril_s], dma=DMA_MISC)
        for r in range(4):
            op("sp", "dma_start", out=bsb[:, :, r * 128:(r + 1) * 128], in_=b_s.partition_broadcast(128), writes=[bsb_s], dma=DMA_MISC)
        for g in range(NC):
            w, w_s2 = wl[g % 2]
            op("sp", "dma_start", out=w, in_=w_s[g], writes=[w_s2], dma=DMA_MISC)
            op("dve", "tensor_tensor", w, w, tril, ALU.mult, reads=[w_s2, tril_s], writes=[w_s2])
            bk = g % 2
            op("pe", "matmul", ps[bk][:, 0:128], w, self.identf, start=True, stop=True, reads=[w_s2, cs], writes=[pss[bk]])
            op("act", "copy", wT[:, g, :], ps[bk][:, 0:128], reads=[pss[bk]], writes=[wT_s])
        self.P.barrier()
        vg = [A.alloc([T], F32, "vg%d" % c) for c in range(NC)]
        xl = [A.alloc([T], F32, "gx%d" % i) for i in range(2)]
        tb, tb_s = A.alloc([T], F32, "gt")
        sqb, sqb_s = A.alloc([T], F32, "gsq")
        for c in range(NC):
            x, x_s = xl[c % 2]
            y, y_s = vg[c]
            op("sp", "dma_start", out=x, in_=zT_d[(20 + c) * 128:(21 + c) * 128, :], reads=[self.dslot(("zT", 20 + c))], writes=[x_s], dma=DMA_LD)
            self.gelu_tanh(y, y_s, x, x_s, tb, tb_s)
            self.ln_stats_acc(y, y_s, sqb, sqb_s, c, NC)
        mean, mean_s = A.alloc([T], F32, "gmean")
        rstd, rstd_s = A.alloc([T], F32, "grstd")
        self.ln_finish(mean, mean_s, rstd, rstd_s, 1024)
        self.P.barrier()
        ugs = [(sqb, sqb_s), (tb, tb_s)]
        vnb = [A.alloc([T], BF16, "vnb%d" % i) for i in range(2)]
        vtok = [A.alloc([16, 128], BF16, "vtok%d" % i) for i in range(2)]
        tmp = [A.alloc([512], F32, "gtmp%d" % i) for i in range(2)]
        ob = [A.alloc([T], BF16, "gob%d" % i) for i in range(2)]

        def stage_a(c):
            y, y_s = vg[c]
            x, x_s = xl[c % 2]
            vn, vn_s = vnb[c % 2]
            vt, vt_s = vtok[c % 2]
            ug, ug_s = ugs[c % 2]
            op("sp", "dma_start", out=x, in_=zT_d[(12 + c) * 128:(13 + c) * 128, :], reads=[self.dslot(("zT", 12 + c))], writes=[x_s], dma=DMA_LD)
            op("act", "activation", ug, x, AF.Gelu_apprx_tanh, reads=[x_s], writes=[ug_s])
            op("dve", "tensor_tensor", y, y, mean, ALU.subtract, reads=[y_s, mean_s], writes=[y_s])
            op("pool", "tensor_tensor", y, y, rstd, ALU.mult, reads=[y_s, rstd_s], writes=[y_s])
            op("act", "activation", vn, y, AF.Identity, bias=cw[:, c, 1:2], scale=cw[:, c, 0:1], reads=[y_s, cw_s], writes=[vn_s])
            for half in range(2):
                bk = 6 + half
                pv = ps[bk][:].bitcast(BF16)
                for j in range(8):
                    n = half * 8 + j
                    op("pe", "transpose", pv[:, j * 128:(j + 1) * 128], vn[:, n * 128:(n + 1) * 128], self.identb,
                       reads=[vn_s, cs], writes=[pss[bk]])
                op("dve", "tensor_copy", vt[:, half * 8:(half + 1) * 8, :].rearrange("p a b -> p (a b)"), pv[:, 0:1024],
                   reads=[pss[bk]], writes=[vt_s])

        def stage_b(c):
            vt, vt_s = vtok[c % 2]
            o, o_s = ob[c % 2]
            ug, ug_s = ugs[c % 2]
            for tq in range(4):
                bk = tq
                t_, t_s = tmp[tq % 2]
                sl = slice(tq * 512, (tq + 1) * 512)
                for r in range(4):
                    n = tq * 4 + r
                    op("pe", "matmul", ps[bk][:, r * 128:(r + 1) * 128], vt[:, n, :], wT[:, c, :], start=True, stop=True,
                       reads=[vt_s, wT_s], writes=[pss[bk]])
                op("dve", "tensor_tensor", t_, ps[bk][:, :], bsb[:, c, :], ALU.add, reads=[pss[bk], bsb_s], writes=[t_s])
                op("pool", "tensor_tensor", o[:, sl], t_, ug[:, sl], ALU.mult, reads=[t_s, ug_s], writes=[o_s])
            r0 = 1024 + c * 128
            op("sp", "dma_start", out=mixT_d[r0:r0 + 128, :], in_=o, reads=[o_s], writes=[self.dslot(("mixT", 8 + c))], dma=DMA_ST)

        stage_a(0)
        for c in range(NC):
            if c + 1 < NC:
                stage_a(c + 1)
            stage_b(c)


def host_consts():
    c = {}
    c["identb"] = np.eye(128, dtype=np.float32).astype(ml_dtypes.bfloat16)
    c["identf"] = np.eye(128, dtype=np.float32)
    c["onesf"] = np.ones((128, 128), dtype=np.float32)
    c["onesb"] = np.ones((128, 128), dtype=np.float32).astype(ml_dtypes.bfloat16)
    e = np.zeros((128, 8), dtype=np.float32)
    e[:, 0] = EPS
    e[:, 1] = 1e-30
    e[:, 2] = -np.pi
    e[:, 3] = -30000.0
    c["epsc"] = e
    inv = (10000.0 ** (-np.arange(0, 128, 2, dtype=np.float32) / 128)).astype(np.float32)
    c["invc"] = np.concatenate([inv, inv])[:, None].astype(np.float32)
    c["invrow"] = np.broadcast_to(inv[None, :], (128, 64)).astype(np.float32).copy()
    rp = np.zeros((128, 128), np.float32)
    for k in range(128):
        if k >= 64:
            rp[k, k - 64] = -1.0
        else:
            rp[k, k + 64] = 1.0
    c["rotPT"] = rp
    bf = ml_dtypes.bfloat16
    tt = np.arange(T)
    ends = np.arange(127) * 16 + 31
    v = np.zeros((128, T), np.float32)
    v[:127] = (ends[:, None] <= tt[None, :])
    NEGM = 30000.0
    c["validT"] = ((v - 1.0) * NEGM).astype(bf)
    p = np.arange(128)[:, None]
    j = np.arange(512)[None, :]
    cmk = np.stack([((i * 128 + p) <= j) for i in range(4)], axis=1).astype(np.float32)
    c["cm"] = ((cmk - 1.0) * NEGM).reshape(128, 4 * 512).astype(bf)
    rels = [-256, -128, 0, 128, 256, 384]
    wmk = np.stack([((j - p - r) >= 0) & ((j - p - r) < 256) for r in rels], axis=1).astype(np.float32)
    c["wm"] = ((wmk - 1.0) * NEGM).reshape(128, 6 * 512).astype(bf)
    rels = [-128, 0, 128, 256, 384]
    smk = np.stack([((j - p - r) >= 0) & ((j - p - r) < 128) for r in rels], axis=1).astype(np.float32)
    c["sm"] = ((smk - 1.0) * NEGM).reshape(128, 5 * 512).astype(bf)
    E = np.zeros((128, 16, 128), np.float32)
    for kt in range(16):
        for pp in range(128):
            E[2 * kt + pp // 64, kt, pp] = 1.0
    c["E"] = E.reshape(128, 16 * 128).astype(bf)
    SG = np.zeros((128, 24, 128), np.float32)
    for i in range(24):
        SG[i, i, :] = 1.0
    c["SelG"] = SG.reshape(128, 24 * 128).astype(bf)
    starts = np.arange(127) * 16
    sel_start = np.arange(32) * 64
    ov = np.zeros((128, 32), np.float32)
    ov[:127] = ((starts[:, None] <= sel_start[None, :] + 63) & (ends[:, None] >= sel_start[None, :]))
    c["ovl"] = ov
    cur = tt // 64
    jb = np.arange(32)[None, :]
    forced = (jb == 0) | (jb == cur[:, None]) | (jb == cur[:, None] - 1)
    valid_s = sel_start[None, :] <= tt[:, None]
    Am = (valid_s & ~forced).astype(np.float32)
    Bm = np.where(forced, 1e6, np.where(valid_s, 0.0, -1.0)).astype(np.float32)
    c["topA"] = Am.reshape(16, 128, 32).transpose(1, 0, 2).reshape(128, 16 * 32).copy()
    c["topB"] = Bm.reshape(16, 128, 32).transpose(1, 0, 2).reshape(128, 16 * 32).copy()
    c["tril"] = np.tril(np.ones((128, 128), np.float32))
    return c


CONST_SPECS = {"identb": ([128, 128], BF16), "identf": ([128, 128], F32), "onesf": ([128, 128], F32),
               "onesb": ([128, 128], BF16), "epsc": ([128, 8], F32), "invc": ([128, 1], F32),
               "invrow": ([128, 64], F32), "rotPT": ([128, 128], F32),
               "validT": ([128, T], BF16), "cm": ([128, 2048], BF16), "wm": ([128, 3072], BF16), "sm": ([128, 2560], BF16),
               "E": ([128, 2048], BF16), "SelG": ([128, 3072], BF16), "ovl": ([128, 32], F32),
               "topA": ([128, 512], F32), "topB": ([128, 512], F32), "tril": ([128, 128], F32)}

IN_SPECS = [
    ("x", [T, D], F32), ("c", [1, D], F32), ("positions", [1, T], I32),
    ("ada_w", [2, D, 6 * D], F32), ("ada_b", [2, 6 * D], F32), ("norm_g", [2, 2, D], F32),
    ("ffn_w_up", [2, D, 2 * DFF], F32), ("ffn_conv_w", [2, 3, 2 * DFF], F32), ("ffn_conv_b", [2, 2 * DFF], F32),
    ("ffn_w_down", [2, DFF, D], F32),
    ("ev_w_in", [1, D, EVEN_IN], F32), ("ev_w_out", [1, D, D], F32), ("ev_conv_w", [1, 31, 1024], F32),
    ("ev_conv_b", [1, 1024], F32), ("ev_conv_ln_g", [1, 1024], F32), ("ev_conv_ln_b", [1, 1024], F32),
    ("ev_q_norm", [1, 128], F32), ("ev_k_norm", [1, 3, 128], F32),
    ("ev_cmp_k_pos", [1, 32, 128], F32), ("ev_cmp_k_w1", [1, 4096, 256], F32), ("ev_cmp_k_w2", [1, 256, 128], F32),
    ("ev_cmp_v_pos", [1, 32, 128], F32), ("ev_cmp_v_w1", [1, 4096, 256], F32), ("ev_cmp_v_w2", [1, 256, 128], F32),
    ("od_w_in", [1, D, ODD_IN], F32), ("od_w_out", [1, D, D], F32), ("od_q_norm", [1, 128], F32),
    ("od_k_norm", [1, 128], F32), ("od_sinks", [1, 8], F32), ("od_gmlp_ln_g", [1, 1024], F32),
    ("od_gmlp_ln_b", [1, 1024], F32), ("od_gmlp_w_s", [1, 8, 128, 128], F32), ("od_gmlp_b_s", [1, 8, 128], F32),
]


def build_nc(dbg_outs=(), stop=None):
    nc = bass.Bass("TRN2", target_bir_lowering=False)
    I = {}
    for name, shape, dt in IN_SPECS:
        I[name] = nc.dram_tensor(name, shape, dt, kind="ExternalInput").ap()
    C = {}
    for name, (shape, dt) in CONST_SPECS.items():
        C[name] = nc.dram_tensor("k_" + name, shape, dt, kind="ExternalInput").ap()
    out = nc.dram_tensor("out", [T, D], F32, kind="ExternalOutput").ap()

    def scratch(name, shape, dt):
        kind = "ExternalOutput" if name in dbg_outs else "Internal"
        return nc.dram_tensor(name, shape, dt, kind=kind).ap()

    gbc_d = scratch("gbc_d", [2, 2, 128, D], F32)
    zT_d = scratch("zT_d", [37 * 128, T], F32)
    mixT_d = scratch("mixT_d", [D, T], BF16)
    xa_d = scratch("xa_d", [T, D], F32)
    xb_d = scratch("xb_d", [T, D], F32)
    cs_d = scratch("cs_d", [2, 128, T], F32)
    qk_d = scratch("qk_d", [12, 128, T], BF16)
    v_d = scratch("v_d", [4, 128, T], BF16)
    kcT_d = scratch("kcT_d", [2, 128, 128], BF16)
    vc_d = scratch("vc_d", [2, 128, 128], BF16)

    with ExitStack() as st:
        B = Builder(nc, st)
        B.load_consts(C)
        B.phase_rope(I["positions"], cs_d, C)
        B.phase_inproj(0, I["x"], "x_in", I["ev_w_in"][0], EVEN_IN, zT_d,
                       side=lambda: B.mod_gen([0], I["c"], I["ada_w"], I["ada_b"], I["norm_g"], gbc_d, (4, 4, 4, 4)), side_first=8)
        B.phase_conformer(zT_d, mixT_d, I["ev_conv_w"][0], I["ev_conv_b"][0:1, :], I["ev_conv_ln_g"][0:1, :], I["ev_conv_ln_b"][0:1, :])
        jobs = []
        for h in range(8):
            jobs.append((16 + h, I["ev_q_norm"][0:1, :], 128 ** -0.5, qk_d[h], ("qk", h)))
        for g in range(2):
            jobs.append((28 + g, I["ev_k_norm"][0, 1:2, :], 1.0, qk_d[8 + g], ("qk", 8 + g)))
        for g in range(2):
            jobs.append((32 + g, I["ev_k_norm"][0, 2:3, :], 1.0, qk_d[10 + g], ("qk", 10 + g)))
        B.phase_qk(zT_d, cs_d, jobs, C)
        vj = []
        for g in range(2):
            vj.append((30 + g, v_d[g], ("v", g)))
        for g in range(2):
            vj.append((34 + g, v_d[2 + g], ("v", 2 + g)))
        B.phase_vprep(zT_d, vj)
        prm = {"k_pos": I["ev_cmp_k_pos"][0], "k_w1": I["ev_cmp_k_w1"][0], "k_w2": I["ev_cmp_k_w2"][0],
               "v_pos": I["ev_cmp_v_pos"][0], "v_w1": I["ev_cmp_v_w1"][0], "v_w2": I["ev_cmp_v_w2"][0],
               "k_gain": I["ev_k_norm"][0, 0:1, :]}
        B.phase_compress(zT_d, I["positions"], C, prm, kcT_d, vc_d)
        B.phase_nsa_attn(zT_d, qk_d, v_d, kcT_d, vc_d, mixT_d, C,
                         side=lambda: B.mod_gen([1], I["c"], I["ada_w"], I["ada_b"], I["norm_g"], gbc_d, (2, 2, 2, 2)))
        B.phase_outproj(mixT_d, I["ev_w_out"][0], I["x"], "x_in", gbc_d[0, 0], ("gbc", 0, 0), xa_d, "xa")
        if stop == "mix0":
            B.P.barrier()
            B.P.emit(st)
            return nc
        B.phase_ffn(0, xa_d, "xa", xb_d, "xb", I["ffn_w_up"][0], I["ffn_conv_w"][0], I["ffn_conv_b"][0:1, :], I["ffn_w_down"][0],
                    gbc_d[0, 1], ("gbc", 0, 1))
        B.phase_inproj(1, xb_d, "xb", I["od_w_in"][0], ODD_IN, zT_d)
        jobs = []
        for h in range(8):
            jobs.append((h, I["od_q_norm"][0:1, :], 128 ** -0.5, qk_d[h], ("qk", h)))
        for g in range(2):
            jobs.append((8 + g, I["od_k_norm"][0:1, :], 1.0, qk_d[8 + g], ("qk", 8 + g)))
        B.phase_qk(zT_d, cs_d, jobs, C)
        B.phase_vprep(zT_d, [(10 + g, v_d[g], ("v", g)) for g in range(2)])
        B.phase_swa_attn(qk_d, v_d, I["od_sinks"][0:1, :], mixT_d, C)
        B.phase_gmlp(zT_d, mixT_d, I["od_gmlp_ln_g"][0:1, :], I["od_gmlp_ln_b"][0:1, :], I["od_gmlp_w_s"][0],
                     I["od_gmlp_b_s"][0].rearrange("g t -> (g t)").rearrange("(o n) -> o n", o=1), C)
        B.phase_outproj(mixT_d, I["od_w_out"][0], xb_d, "xb", gbc_d[1, 0], ("gbc", 1, 0), xa_d, "xa")
        B.phase_ffn(1, xa_d, "xa", out, "out", I["ffn_w_up"][1], I["ffn_conv_w"][1], I["ffn_conv_b"][1:2, :], I["ffn_w_down"][1],
                    gbc_d[1, 1], ("gbc", 1, 1))
        B.P.barrier()
        B.P.emit(st)
    return nc


_NC_CACHE = {}


def kernel(**inputs):
    n = 8
    if "nc" not in _NC_CACHE:
        _NC_CACHE["nc"] = build_nc()
    nc = _NC_CACHE["nc"]
    consts = host_consts()
    shared = {}
    for name, shape, dt in IN_SPECS:
        if name in ("x", "c", "positions"):
            continue
        shared[name] = np.ascontiguousarray(inputs[name])
    for k, v in consts.items():
        shared["k_" + k] = v
    in_maps = []
    for b in range(n):
        m = dict(shared)
        m["x"] = np.ascontiguousarray(inputs["x"][b])
        m["c"] = np.ascontiguousarray(inputs["c"][b:b + 1])
        m["positions"] = np.ascontiguousarray(inputs["positions"][b:b + 1]).astype(np.int32)
        in_maps.append(m)
    res = run_bass_kernel_spmd(nc, in_maps, core_ids=list(range(n)))
    return np.stack([np.asarray(r["out"]) for r in res.results], axis=0).astype(np.float32)
```
